# Optimizing a Trainium2 kernel written in Bass

```python
import jax, jax.numpy as jnp
from jax import lax
import numpy as np

D_MODEL = 1024
BATCH = 2
SEQ = 16384
DEPTH = 1

PLE_DIM = 256
CONV_WIDTH = 1024
CONV_K = 3
N_HEADS = 4
QK_DIM = 1024
V_DIM = 2048
DK = QK_DIM // N_HEADS
DV = V_DIM // N_HEADS
CHUNK = 64
EPS = 1e-6
SPLIT_SIZES = (CONV_WIDTH, CONV_WIDTH, CONV_WIDTH, CONV_WIDTH,
               QK_DIM, QK_DIM, V_DIM, V_DIM, V_DIM, N_HEADS, N_HEADS,
               D_MODEL, D_MODEL)
N_IN = sum(SPLIT_SIZES)
SPLIT_POINTS = tuple(int(s) for s in np.cumsum(SPLIT_SIZES)[:-1])

kernel_name = "hybrid_shortconv_mlstm_gated_merge"


def rmsnorm(x, g):
    xf = x.astype(jnp.float32)
    y = xf * lax.rsqrt(jnp.mean(xf * xf, axis=-1, keepdims=True) + EPS)
    return (y * g.astype(jnp.float32)).astype(x.dtype)


def causal_short_conv(u, w, b):
    s = u.shape[1]
    up = jnp.pad(u, ((0, 0), (CONV_K - 1, 0), (0, 0)))
    y = up[:, 0:s] * w[0]
    for j in range(1, CONV_K):
        y = y + up[:, j:j + s] * w[j]
    return y + b


def mlstm_chunkwise(q, k, v, i_raw, f_raw):
    bsz, s = q.shape[0], q.shape[1]
    nc = s // CHUNK
    f32 = jnp.float32

    def to_chunks(t):
        return t.astype(f32).reshape(bsz, nc, CHUNK, N_HEADS, -1).transpose(1, 0, 3, 2, 4)

    def gate_chunks(t):
        return t.astype(f32).reshape(bsz, nc, CHUNK, N_HEADS).transpose(1, 0, 3, 2)

    qc = to_chunks(q) * (DK ** -0.5)
    kc = to_chunks(k)
    vc = to_chunks(v)
    lic = gate_chunks(i_raw)
    lfc = jax.nn.log_sigmoid(gate_chunks(f_raw))
    mask = jnp.tril(jnp.ones((CHUNK, CHUNK), dtype=bool))

    def step(carry, xs):
        c_st, n_st, m_st = carry
        qq, kk, vv, li, lf = xs
        bcum = jnp.cumsum(lf, axis=-1)
        dmat = bcum[..., :, None] - bcum[..., None, :] + li[..., None, :]
        dmat = jnp.where(mask, dmat, -jnp.inf)
        a = bcum + m_st[..., None]
        m_row = jnp.maximum(a, jnp.max(dmat, axis=-1))
        sc = jnp.einsum('bhld,bhsd->bhls', qq, kk) * jnp.exp(dmat - m_row[..., None])
        inter = jnp.exp(a - m_row)
        num = (jnp.einsum('bhls,bhsv->bhlv', sc, vv)
               + inter[..., None] * jnp.einsum('bhld,bhdv->bhlv', qq, c_st))
        den = jnp.sum(sc, axis=-1) + inter * jnp.einsum('bhld,bhd->bhl', qq, n_st)
        h = num / jnp.maximum(jnp.abs(den), jnp.exp(-m_row))[..., None]
        b_last = bcum[..., -1]
        g = b_last[..., None] - bcum + li
        m_new = jnp.maximum(b_last + m_st, jnp.max(g, axis=-1))
        w = jnp.exp(g - m_new[..., None])
        decay = jnp.exp(b_last + m_st - m_new)
        c_new = decay[..., None, None] * c_st + jnp.einsum('bhsd,bhsv->bhdv', kk * w[..., None], vv)
        n_new = decay[..., None] * n_st + jnp.einsum('bhs,bhsd->bhd', w, kk)
        return (c_new, n_new, m_new), h

    init = (jnp.zeros((bsz, N_HEADS, DK, DV), f32),
            jnp.zeros((bsz, N_HEADS, DK), f32),
            jnp.full((bsz, N_HEADS), -jnp.inf, f32))
    _, hs = lax.scan(step, init, (qc, kc, vc, lic, lfc))
    return hs.transpose(1, 0, 3, 2, 4).reshape(bsz, s, N_HEADS, DV)


def setup_inputs(seed: int = 0) -> dict:
    key = jax.random.key(seed)
    ks = jax.random.split(key, 16)
    nrm = jax.random.normal
    f32 = jnp.float32
    x = nrm(ks[0], (BATCH, SEQ, D_MODEL), f32)
    p = nrm(ks[1], (DEPTH, BATCH, SEQ, PLE_DIM), f32)
    g_mix = 1.0 + 0.02 * nrm(ks[2], (DEPTH, D_MODEL), f32)
    w_in = nrm(ks[3], (DEPTH, D_MODEL, N_IN), f32) * D_MODEL ** -0.5
    conv_w = nrm(ks[4], (DEPTH, CONV_K, CONV_WIDTH), f32) * CONV_K ** -0.5
    conv_b = 0.02 * nrm(ks[5], (DEPTH, CONV_WIDTH), f32)
    w_a_out = nrm(ks[6], (DEPTH, CONV_WIDTH, D_MODEL), f32) * CONV_WIDTH ** -0.5
    gb_noise = 0.1 * nrm(ks[7], (DEPTH, 2 * N_HEADS), f32)
    f_offset = jnp.concatenate([jnp.zeros((N_HEADS,), f32), jnp.linspace(3.0, 6.0, N_HEADS, dtype=f32)])
    b_gates = gb_noise + f_offset
    g_head = 1.0 + 0.02 * nrm(ks[8], (DEPTH, V_DIM), f32)
    w_b_out = nrm(ks[9], (DEPTH, V_DIM, D_MODEL), f32) * V_DIM ** -0.5
    w_o = nrm(ks[10], (DEPTH, D_MODEL, D_MODEL), f32) * D_MODEL ** -0.5
    g_ple = 1.0 + 0.02 * nrm(ks[11], (DEPTH, D_MODEL), f32)
    w_ple_gate = nrm(ks[12], (DEPTH, D_MODEL, D_MODEL), f32) * D_MODEL ** -0.5
    w_ple = nrm(ks[13], (DEPTH, PLE_DIM, D_MODEL), f32) * PLE_DIM ** -0.5
    g_final = 1.0 + 0.02 * nrm(ks[14], (D_MODEL,), f32)
    return {"x": x, "p": p, "g_mix": g_mix, "w_in": w_in, "conv_w": conv_w,
            "conv_b": conv_b, "w_a_out": w_a_out, "b_gates": b_gates,
            "g_head": g_head, "w_b_out": w_b_out, "w_o": w_o, "g_ple": g_ple,
            "w_ple_gate": w_ple_gate, "w_ple": w_ple, "g_final": g_final}


def reference(x, p, g_mix, w_in, conv_w, conv_b, w_a_out, b_gates, g_head,
              w_b_out, w_o, g_ple, w_ple_gate, w_ple, g_final):
    bsz, s = x.shape[0], x.shape[1]
    for l in range(DEPTH):
        hn = rmsnorm(x, g_mix[l])
        proj = jnp.einsum('bsd,de->bse', hn, w_in[l])
        (xa, ba, ca, za, q, k, v, o, zb, ig, fg, ga, gb) = jnp.split(proj, SPLIT_POINTS, axis=-1)

        ya = ba * causal_short_conv(ca * xa, conv_w[l], conv_b[l])
        ya = jnp.einsum('bsc,cd->bsd', ya * jax.nn.silu(za), w_a_out[l])

        ig = ig + b_gates[l, :N_HEADS]
        fg = fg + b_gates[l, N_HEADS:]
        hb = mlstm_chunkwise(q.reshape(bsz, s, N_HEADS, DK), k.reshape(bsz, s, N_HEADS, DK),
                             v.reshape(bsz, s, N_HEADS, DV), ig, fg)
        hb = rmsnorm(hb, g_head[l].reshape(N_HEADS, DV)).reshape(bsz, s, V_DIM).astype(x.dtype)
        yb = jax.nn.sigmoid(o) * hb * jax.nn.silu(zb)
        yb = jnp.einsum('bsv,vd->bsd', yb, w_b_out[l])

        merged = jax.nn.sigmoid(ga) * ya + jax.nn.sigmoid(gb) * yb
        x = x + jnp.einsum('bsd,de->bse', merged, w_o[l])

        gate = jax.nn.sigmoid(jnp.einsum('bsd,de->bse', rmsnorm(x, g_ple[l]), w_ple_gate[l]))
        x = x + gate * jnp.einsum('bsp,pd->bsd', p[l], w_ple[l])
    return rmsnorm(x, g_final)
```

```python
import os
import numpy as np
import concourse.bass as bass
import concourse.mybir as mybir
from concourse.bass_utils import run_bass_kernel_spmd

F32 = mybir.dt.float32
BF16 = mybir.dt.bfloat16
AF = mybir.ActivationFunctionType
ALU = mybir.AluOpType

NCORES = 8
D = 1024
SEQ = 16384
NT = 4096
T = 512
NMT = NT // T
L = 128
NJ = T // L
NCHK = NT // L
H = 4
DK = 256
DV = 512
EPS = 1e-6
GW = 4112

CH_CONV = 0
CH_Q = 8
CH_GATE = 10
CH_GA = 18
CH_WA = 20
CH_GB = 22
CH_WB = 24
CH_WO = 28
CH_WPG = 30
CH_WP = 32
NWCH = 33

C_ID = 0
C_U = 128
C_ONES = 256
C_MASK = 384
C_GMIX = 896
C_GPLE = 1920
C_GFIN = 2944
C_GHEAD = 3968
C_CW = 3984
C_CB = 4008
C_BG = 4016
C_BTW = 4024
C_SEL = 4040
NCST = 4048


class Ev:
    __slots__ = ("sem", "val")

    def __init__(self, sem, val):
        self.sem = sem
        self.val = val


class Buf:
    def __init__(self, name):
        self.name = name
        self.w = None
        self.r = {}
        self.dsem = None
        self.dcount = 0


class Sched:
    ENGS = ("pe", "act", "dve", "pool", "sp")

    def __init__(self, nc, sems, dma_sems):
        self.nc = nc
        self.sem = sems
        self.dma_sems = list(dma_sems)
        self.q = {e: [] for e in self.ENGS}
        self.cnt = {e: 0 for e in self.ENGS}
        self.waited = {e: {} for e in self.ENGS}
        self.dbufs = []
        self.extra_evs = []
        self.stopped = False

    def stop_if(self, tag):
        if os.environ.get("KSTOP", "") == tag:
            self.stopped = True

    def _collect(self, eng, reads, writes, extra):
        need = {}

        def add(ev):
            if ev is None:
                return
            k = id(ev.sem)
            if k not in need or need[k].val < ev.val:
                need[k] = ev

        for b in reads:
            add(b.w)
        for b in writes:
            add(b.w)
            for ev in b.r.values():
                add(ev)
        for ev in extra:
            add(ev)
        out = []
        wd = self.waited[eng]
        for k, ev in need.items():
            if wd.get(k, 0) >= ev.val:
                continue
            wd[k] = ev.val
            out.append(ev)
        return out

    def _commit(self, ev, reads, writes):
        for b in reads:
            k = id(ev.sem)
            if k not in b.r or b.r[k].val < ev.val:
                b.r[k] = ev
        for b in writes:
            b.w = ev
            b.r = {}

    def op(self, eng, fn, reads=(), writes=(), extra=()):
        if self.stopped:
            return Ev(None, 0)
        waits = self._collect(eng, reads, writes, extra)
        self.cnt[eng] += 1
        ev = Ev(self.sem[eng], self.cnt[eng])
        self.q[eng].append((waits, fn, (self.sem[eng], 1)))
        self._commit(ev, reads, writes)
        return ev

    def dma(self, eng, fn, owner, reads=(), writes=(), ndma=1, extra=()):
        if self.stopped:
            return Ev(None, 0)
        waits = self._collect(eng, reads, writes, extra)
        if owner.dsem is None:
            owner.dsem = self.dma_sems.pop()
            self.dbufs.append(owner)
        owner.dcount += 16 * ndma
        ev = Ev(owner.dsem, owner.dcount)
        self.q[eng].append((waits, fn, ("dma", owner.dsem)))
        self._commit(ev, reads, writes)
        return ev

    def barrier(self):
        if self.stopped:
            return
        evs = [Ev(self.sem[f], self.cnt[f]) for f in self.ENGS if self.cnt[f] > 0]
        evs += [Ev(b.dsem, b.dcount) for b in self.dbufs]
        evs += list(self.extra_evs)
        for e in self.ENGS:
            self.wait_only(e, evs)

    def wait_only(self, eng, evs):
        if self.stopped:
            return
        waits = self._collect(eng, (), (), evs)
        self.q[eng].append((waits, None, None))

    def emit(self, eng, handle):
        for waits, fn, inc in self.q[eng]:
            for ev in waits:
                handle.wait_ge(ev.sem, ev.val)
            if fn is None:
                continue
            if inc[0] == "dma":
                dsem = inc[1]
                fn(handle, lambda ins, dsem=dsem: ins.then_inc(dsem, 16))
            else:
                ins = fn(handle)
                if inc[1] is None:
                    ins.then_inc(inc[0])
                else:
                    ins.then_inc(inc[0], inc[1])


def build_nc():
    nc = bass.Bass("TRN2", target_bir_lowering=False)
    x_seg = nc.dram_tensor("x_seg", [NT, D], F32, kind="ExternalInput").ap()
    x_prev = nc.dram_tensor("x_prev", [L, D], F32, kind="ExternalInput").ap()
    p_seg = nc.dram_tensor("p_seg", [NT, 256], F32, kind="ExternalInput").ap()
    wq32 = nc.dram_tensor("wq32", [NWCH, 128, 4096], F32, kind="ExternalInput").ap()
    wkv32 = nc.dram_tensor("wkv32", [128, 8 * 3072], F32, kind="ExternalInput").ap()
    wif32 = nc.dram_tensor("wif32", [128, 64], F32, kind="ExternalInput").ap()
    cst_d = nc.dram_tensor("cst", [128, NCST], F32, kind="ExternalInput").ap()
    out_d = nc.dram_tensor("out", [NT, D], F32, kind="ExternalOutput").ap()
    wq16 = nc.dram_tensor("wq16", [NWCH, 128, 4096], BF16).ap()
    kws = nc.dram_tensor("kws", [NT, 1024], BF16).ap()
    hns = nc.dram_tensor("hns", [NMT, 128, 8 * T], BF16).ap()
    vs = nc.dram_tensor("vs", [NT, 2048], BF16).ap()
    GWS = (2048, 2048, 16)
    GOFF = (0, 2048, 4096)
    gin_ts = [nc.dram_tensor("gin%d" % i, [128, w], F32) for i, w in enumerate(GWS)]
    gout_ts = [nc.dram_tensor("gout%d" % i, [4 * 128, w], F32) for i, w in enumerate(GWS)]

    import contextlib
    es = contextlib.ExitStack()
    with es:
        def sb(name, shape, dt):
            return es.enter_context(nc.sbuf_tensor(name, shape, dt))

        def sem(name):
            return es.enter_context(nc.semaphore(name))

        DBG = os.environ.get("KDBG", "") != ""
        dbg_t = {}
        dbgB = Buf("dbg")

        def dbg(name, ap, shape, dt, reads):
            if not DBG or name in dbg_t:
                return
            t_ = nc.dram_tensor("dbg_" + name, list(shape), dt, kind="ExternalOutput")
            dbg_t[name] = t_
            S.dma("pool", lambda e, inc: inc(e.dma_start(out=t_.ap(), in_=ap)), dbgB, reads=reads)

        sems = {e: sem("s_" + e) for e in Sched.ENGS}
        dma_sems = [sem("d%d" % i) for i in range(60)]
        cc_sem = sem("cc")
        S = Sched(nc, sems, dma_sems)

        cst = sb("cst_sb", [128, NCST], F32)
        cstB = Buf("cst")
        ident = sb("ident", [128, 128], BF16)
        mask4 = sb("mask4", [128, 512], BF16)
        ones_bf = sb("ones_bf", [128, 1], BF16)
        constB = Buf("constbf")
        Cst = sb("Cst", [128, 4096], F32)
        CB = [[Buf("C%d_%d" % (h, cc)) for cc in range(2)] for h in range(H)]
        CBall = [b for hb in CB for b in hb]
        nst = sb("nst", [128, 8], F32)
        nB = Buf("n")
        EQ = sb("EQ", [128, NCHK, 4], F32)
        DEC = sb("DEC", [128, NCHK, 4], F32)
        RSTD1 = sb("RSTD1", [128, NCHK], F32)
        gateB = [Buf("gates%d" % m) for m in range(NMT)]
        rstdB = [Buf("rstd%d" % m) for m in range(NMT)]
        LD = sb("LD", [128, 4], F32)
        LDB = Buf("LD")
        junk = sb("junk", [128, 1024], BF16)
        junkB = {"act": Buf("junk_act")}
        xbuf = [sb("xbuf%d" % i, [128, NJ, 1024], F32) for i in range(2)]
        xB = [[Buf("x%d_%d" % (i, j)) for j in range(NJ)] for i in range(2)]
        hn_bf = sb("hn_bf", [128, 1024], BF16)
        hnB = Buf("hn")
        hnT = sb("hnT", [128, 8, T], BF16)
        hnTB = Buf("hnT")
        hnsB = [Buf("hns%d" % i) for i in range(NMT)]
        hT_h = sb("hT_h", [128, 8, 2], BF16)
        hThB = Buf("hT_h")

        banks = [es.enter_context(nc.psum_tensor("bank%d" % i, [128, 512], F32)) for i in range(8)]
        bankB = [Buf("bank%d" % i) for i in range(8)]
        bank_rr = [0]

        def next_bank():
            i = bank_rr[0] % 8
            bank_rr[0] += 1
            return banks[i], bankB[i]

        def v3(ap, a):
            return ap.rearrange("p (a b) -> p a b", a=a)

        S.dma("sp", lambda e, inc: inc(e.dma_start(out=cst[:], in_=cst_d[:, :])), cstB, writes=[cstB])
        S.op("dve", lambda e: e.tensor_copy(out=ident[:], in_=cst[:, C_ID:C_ID + 128]), reads=[cstB], writes=[constB])
        S.op("dve", lambda e: e.tensor_copy(out=mask4[:], in_=cst[:, C_MASK:C_MASK + 512]), reads=[cstB], writes=[constB])
        S.op("dve", lambda e: e.tensor_copy(out=ones_bf[:], in_=cst[:, C_ONES:C_ONES + 1]), reads=[cstB], writes=[constB])
        S.op("dve", lambda e: e.memset(Cst[:], 0.0), writes=CBall)
        S.op("dve", lambda e: e.memset(nst[:], 0.0), writes=[nB])
        S.op("dve", lambda e: e.memset(LD[:], 0.0), writes=[LDB])

        wq16B = [Buf("wq16_%d" % i) for i in range(NWCH)]

        def ld_x(m):
            xb_, xbB_ = xbuf[m % 2], xB[m % 2]
            S.dma("sp", lambda e, inc: inc(e.dma_start(
                out=xb_[:], in_=x_seg[m * T:(m + 1) * T, :].rearrange("(j p) d -> p j d", p=128))), xbB_[0], writes=xbB_)

        p1 = contextlib.ExitStack()
        with p1:
            def sb1(name, shape, dt):
                return p1.enter_context(nc.sbuf_tensor(name, shape, dt))

            wkv = sb1("wkv", [128, 8, 3072], BF16)
            wkvB = Buf("wkv")
            wif = sb1("wif", [128, 8, 8], BF16)
            wifB = Buf("wif")
            S.dma("pool", lambda e, inc: inc(e.dma_start(out=wif[:], in_=wif32.rearrange("p (k c) -> p k c", k=8))), wifB, writes=[wifB])
            ld_x(0)
            x1flat = xbuf[1][:].rearrange("p j d -> p (j d)")
            wkvflat = wkv[:].rearrange("p k c -> p (k c)")
            stgB = [Buf("stg_a"), Buf("stg_b")]
            for q in range(12):
                hb_ = q % 2
                S.dma("sp", lambda e, inc, q=q, hb_=hb_: inc(e.dma_start(out=x1flat[:, hb_ * 2048:(hb_ + 1) * 2048],
                                                                       in_=wkv32[:, q * 2048:(q + 1) * 2048])),
                      stgB[hb_], writes=[stgB[hb_]])
                if q % 2 == 0:
                    S.op("act", lambda e, q=q, hb_=hb_: e.activation(out=wkvflat[:, q * 2048:(q + 1) * 2048],
                                                                    in_=x1flat[:, hb_ * 2048:(hb_ + 1) * 2048], func=AF.Copy),
                         reads=[stgB[hb_]], writes=[wkvB])
                else:
                    S.op("dve", lambda e, q=q, hb_=hb_: e.tensor_copy(out=wkvflat[:, q * 2048:(q + 1) * 2048],
                                                                     in_=x1flat[:, hb_ * 2048:(hb_ + 1) * 2048]),
                         reads=[stgB[hb_]], writes=[wkvB])
            for b_ in xB[1]:
                b_.r = dict(stgB[0].r)
                b_.r.update(stgB[1].r)
            st32 = [sb1("st32_%d" % i, [128, 1024], F32) for i in range(2)]
            st32B = [Buf("st32_%d" % i) for i in range(2)]
            st16 = [sb1("st16_%d" % i, [128, 1024], BF16) for i in range(2)]
            st16B = [Buf("st16_%d" % i) for i in range(2)]
            NU = NWCH * 4
            cv_state = {"u": 0}

            def cv_load(u):
                S.dma("pool", lambda e, inc, u=u: inc(e.dma_start(out=st32[u % 2][:], in_=wq32[u // 4][:, (u % 4) * 1024:(u % 4 + 1) * 1024])),
                      st32B[u % 2], writes=[st32B[u % 2]])
            cv_load(0)
            cv_load(1)

            def cv_emit_until(n):
                while cv_state["u"] < min(n, NU):
                    u = cv_state["u"]
                    S.op("act", lambda e, u=u: e.activation(out=st16[u % 2][:], in_=st32[u % 2][:], func=AF.Copy),
                         reads=[st32B[u % 2]], writes=[st16B[u % 2]])
                    S.dma("pool", lambda e, inc, u=u: inc(e.dma_start(out=wq16[u // 4][:, (u % 4) * 1024:(u % 4 + 1) * 1024], in_=st16[u % 2][:])),
                          st16B[u % 2], reads=[st16B[u % 2]], writes=[wq16B[u // 4]])
                    if u + 2 < NU:
                        cv_load(u + 2)
                    cv_state["u"] += 1

            S.stop_if("setup")
            Gs = sb1("Gs", [128, NJ, 8], F32)
            GsB = Buf("Gs")
            sp_t = sb1("sp_t", [128, NJ, 4], F32)
            spB = Buf("sp")
            tot_s = sb1("tot_s", [128, NJ, 4], F32)
            totB = Buf("tot")
            arg1 = sb1("arg1", [128, NJ, 4], F32)
            arg1B = Buf("arg1")
            wsc = sb1("wsc", [128, NJ, 4], F32)
            wB = Buf("w")
            ss1 = sb1("ss1", [128, NJ], F32)
            ss1B = Buf("ss1")
            lnt = sb1("lnt", [128, NJ], F32)
            lntB = Buf("lnt")
            kw_m = [sb1("kw_m%d" % i, [128, NJ, 1024], BF16) for i in range(2)]
            kwmB = [[Buf("kwm%d_%d" % (i, j)) for j in range(NJ)] for i in range(2)]
            v_m = [sb1("v_m%d" % i, [128, NJ, 2048], BF16) for i in range(2)]
            vmB = [[Buf("vm%d_%d" % (i, j)) for j in range(NJ)] for i in range(2)]
            ntmp = sb1("ntmp", [128, 8], F32)
            ntmpB = Buf("ntmp")
            dec8 = sb1("dec8", [128, 8], F32)
            dec8B = Buf("dec8")

            hnT2 = sb1("hnT2", [128, 8, T], BF16)
            hnTs = [hnT, hnT2]
            hnTBs = [[Buf("hnT%d_%d" % (i, j)) for j in range(NJ)] for i in range(2)]

            def p1_stats(m):
                xb, xbB = xbuf[m % 2], xB[m % 2]
                gB = gateB[m]
                for j in range(NJ):
                    S.op("act", lambda e, xb=xb, j=j: e.activation(out=junk[:], in_=xb[:, j, :], func=AF.Square,
                                                                  accum_out=ss1[:, j:j + 1]),
                         reads=[xbB[j]], writes=[ss1B])
                S.op("act", lambda e: e.activation(out=lnt[:], in_=ss1[:], func=AF.Ln, bias=EPS, scale=1.0 / D),
                     reads=[ss1B], writes=[lntB])
                S.op("act", lambda e, m=m: e.activation(out=RSTD1[:, m * NJ:(m + 1) * NJ], in_=lnt[:], func=AF.Exp, scale=-0.5),
                     reads=[lntB], writes=[rstdB[m]])

            def p1_hn_tile(m, j, part=None):
                xb, xbB = xbuf[m % 2], xB[m % 2]
                hT, hTB = hnTs[m % 2], hnTBs[m % 2]
                if part in (None, 0):
                  S.op("dve", lambda e, xb=xb, j=j, m=m: e.scalar_tensor_tensor(
                    out=hn_bf[:], in0=xb[:, j, :], scalar=RSTD1[:, m * NJ + j:m * NJ + j + 1],
                    in1=cst[:, C_GMIX:C_GMIX + D], op0=ALU.mult, op1=ALU.mult),
                    reads=[xbB[j], rstdB[m], cstB], writes=[hnB])
                if part == 0:
                    return
                for half in range(2):
                    bk, bB = next_bank()

                    def tr(e, bk=bk, half=half):
                        ins = None
                        for i in range(4):
                            kc = half * 4 + i
                            ins = e.matmul(bk[:, i * 128:(i + 1) * 128], lhsT=hn_bf[:, kc * 128:(kc + 1) * 128],
                                           rhs=ident[:], start=True, stop=True)
                        return ins
                    S.op("pe", tr, reads=[hnB, constB], writes=[bB])
                    S.op("act", lambda e, bk=bk, half=half, j=j, hT=hT: e.activation(
                        out=hT[:, half * 4:half * 4 + 4, j * 128:(j + 1) * 128], in_=v3(bk[:, :], 4), func=AF.Copy),
                        reads=[bB], writes=[hTB[j]])

            def p1_store_hns(m):
                hT, hTB = hnTs[m % 2], hnTBs[m % 2]
                S.dma("sp", lambda e, inc, m=m, hT=hT: inc(e.dma_start(out=hns[m], in_=hT[:].rearrange("p k t -> p (k t)"))),
                      hTB[0], reads=hTB, writes=[hnsB[m]])

            def p1_gates(m):
                hT, hTB = hnTs[m % 2], hnTBs[m % 2]
                gB = gateB[m]
                for j in range(NJ):
                    bk, bB = next_bank()

                    def gj(e, bk=bk, j=j, hT=hT):
                        ins = None
                        for kc in range(8):
                            ins = e.matmul(bk[:, 0:8], lhsT=hT[:, kc, j * 128:(j + 1) * 128], rhs=wif[:, kc, :],
                                           start=(kc == 0), stop=(kc == 7))
                        return ins
                    S.op("pe", gj, reads=[hTB[j], wifB], writes=[bB])
                    S.op("dve", lambda e, bk=bk, j=j: e.tensor_tensor(out=Gs[:, j, :], in0=bk[:, 0:8],
                                                                      in1=cst[:, C_BG:C_BG + 8], op=ALU.add),
                         reads=[bB, cstB], writes=[GsB])
                S.op("act", lambda e: e.activation(out=sp_t[:], in_=Gs[:, :, 4:8], func=AF.Exp, scale=-1.0),
                     reads=[GsB], writes=[spB])
                S.op("act", lambda e: e.activation(out=sp_t[:], in_=sp_t[:], func=AF.Ln, bias=1.0, scale=1.0),
                     reads=[spB], writes=[spB])

            def p1_gates_b(m):
                gB = gateB[m]
                bk_cs, bB_cs = next_bank()

                def csj(e, bk=bk_cs):
                    ins = None
                    for j in range(NJ):
                        ins = e.matmul(bk[:, j * 4:(j + 1) * 4], lhsT=cst[:, C_U:C_U + 128], rhs=sp_t[:, j, :],
                                       start=True, stop=True)
                    for j in range(NJ):
                        ins = e.matmul(bk[:, 16 + j * 4:16 + (j + 1) * 4], lhsT=cst[:, C_ONES:C_ONES + 128],
                                       rhs=sp_t[:, j, :], start=True, stop=True)
                    return ins
                S.op("pe", csj, reads=[spB, cstB], writes=[bB_cs])
                S.op("act", lambda e, bk=bk_cs: e.activation(out=tot_s[:], in_=v3(bk[:, 16:32], NJ), func=AF.Copy),
                     reads=[bB_cs], writes=[totB])
                S.op("dve", lambda e, bk=bk_cs: e.tensor_tensor(out=arg1[:], in0=v3(bk[:, 0:16], NJ), in1=tot_s[:],
                                                                op=ALU.subtract),
                     reads=[bB_cs, totB], writes=[arg1B])
                S.op("act", lambda e, m=m: e.activation(out=EQ[:, m * NJ:(m + 1) * NJ, :], in_=arg1[:], func=AF.Exp),
                     reads=[arg1B], writes=[gB])
                S.op("dve", lambda e: e.tensor_tensor(out=arg1[:], in0=arg1[:], in1=Gs[:, :, 0:4], op=ALU.add),
                     reads=[arg1B, GsB], writes=[arg1B])
                S.op("act", lambda e: e.activation(out=wsc[:], in_=arg1[:], func=AF.Exp), reads=[arg1B], writes=[wB])
                S.op("act", lambda e, m=m: e.activation(out=DEC[:, m * NJ:(m + 1) * NJ, :], in_=tot_s[:], func=AF.Exp,
                                                       scale=-1.0), reads=[totB], writes=[gB])
                for j in range(NJ):
                    S.op("dve", lambda e, j=j: e.tensor_tensor(out=LD[:], in0=LD[:], in1=tot_s[:, j, :], op=ALU.add),
                         reads=[totB, LDB], writes=[LDB])

            def p1_proj_job(m, j, cb):
                hT, hTB = hnTs[m % 2], hnTBs[m % 2]
                kwt, kwtB = kw_m[m % 2], kwmB[m % 2]
                vt, vtB = v_m[m % 2], vmB[m % 2]
                bk, bB = next_bank()

                def pj(e, bk=bk, j=j, cb=cb, hT=hT):
                    ins = None
                    for kc in range(8):
                        ins = e.matmul(bk[:, :], lhsT=hT[:, kc, j * 128:(j + 1) * 128],
                                       rhs=wkv[:, kc, cb * 512:(cb + 1) * 512], start=(kc == 0), stop=(kc == 7))
                    return ins
                S.op("pe", pj, reads=[hTB[j], wkvB], writes=[bB])
                if cb < 2:
                    for hh in range(2):
                        h = cb * 2 + hh
                        S.op("act", lambda e, bk=bk, j=j, h=h, hh=hh, kwt=kwt: e.activation(
                            out=kwt[:, j, h * 256:(h + 1) * 256], in_=bk[:, hh * 256:(hh + 1) * 256], func=AF.Copy,
                            scale=wsc[:, j, h:h + 1]), reads=[bB, wB], writes=[kwtB[j]])
                else:
                    c0 = (cb - 2) * 512
                    S.op("dve", lambda e, bk=bk, j=j, c0=c0, vt=vt: e.tensor_copy(out=vt[:, j, c0:c0 + 512], in_=bk[:, :]),
                         reads=[bB], writes=[vtB[j]])

            def p1_state_items(m, j):
                kwt, kwtB = kw_m[m % 2], kwmB[m % 2]
                vt, vtB = v_m[m % 2], vmB[m % 2]
                gB = gateB[m]
                ch = m * NJ + j
                items = []

                def it_n():
                    bkn, bBn = next_bank()

                    def nj(e, bk=bkn):
                        ins = None
                        for hc in range(8):
                            ins = e.matmul(bk[:, hc:hc + 1], lhsT=kwt[:, j, hc * 128:(hc + 1) * 128], rhs=ones_bf[:],
                                           start=True, stop=True)
                        return ins
                    S.op("pe", nj, reads=[kwtB[j], constB], writes=[bBn])
                    for h in range(H):
                        S.op("dve", lambda e, h=h: e.tensor_scalar(
                            out=nst[:, h * 2:h * 2 + 2], in0=nst[:, h * 2:h * 2 + 2], scalar1=DEC[:, ch, h:h + 1],
                            scalar2=None, op0=ALU.mult), reads=[nB, gB], writes=[nB])
                    S.op("dve", lambda e, bk=bkn: e.tensor_tensor(out=nst[:], in0=nst[:], in1=bk[:, 0:8], op=ALU.add),
                         reads=[nB, bBn], writes=[nB])
                items.append(it_n)
                for h in range(H):
                    for cc in range(2):
                        def it_s(h=h, cc=cc):
                            bk, bB = next_bank()
                            S.op("pe", lambda e, bk=bk: e.matmul(
                                bk[:, :], lhsT=kwt[:, j, h * 256 + cc * 128:h * 256 + (cc + 1) * 128],
                                rhs=vt[:, j, h * 512:(h + 1) * 512], start=True, stop=True),
                                reads=[kwtB[j], vtB[j]], writes=[bB])
                            S.op("dve", lambda e, bk=bk: e.scalar_tensor_tensor(
                                out=Cst[:, h * 1024 + cc * 512:h * 1024 + (cc + 1) * 512], in0=Cst[:, h * 1024 + cc * 512:h * 1024 + (cc + 1) * 512],
                                scalar=DEC[:, ch, h:h + 1], in1=bk[:, :], op0=ALU.mult, op1=ALU.add),
                                reads=[bB, gB, CB[h][cc]], writes=[CB[h][cc]])
                        items.append(it_s)
                return items

            p1_stats(0)
            for j in range(NJ):
                p1_hn_tile(0, j)
            p1_store_hns(0)
            ld_x(1)
            p1_stats(1)
            slot = 0
            for m in range(NMT):
                if m == 1:
                    S.stop_if("p1m0")
                if m + 2 < NMT:
                    ld_x(m + 2)
                items = []
                for j in range(NJ):
                    its = []
                    if m >= 1:
                        its += p1_state_items(m - 1, j)
                    if m + 1 < NMT:
                        its.insert(0, lambda m=m, j=j: p1_hn_tile(m + 1, j, part=0))
                        its.append(lambda m=m, j=j: p1_hn_tile(m + 1, j, part=1))
                    items += its
                porder = ([(0, cb) for cb in (2, 3, 4, 5)] + ["GA"] + [(1, cb) for cb in (2, 3, 4, 5)]
                          + [(2, cb) for cb in (2, 3, 4, 5)] + ["GB"]
                          + [(0, 0), (0, 1), (1, 0), (1, 1), (2, 0), (2, 1)]
                          + [(3, cb) for cb in (2, 3, 4, 5, 0, 1)])
                k = 0
                idx = 0
                for po in porder:
                    if po == "GA":
                        p1_gates(m)
                        continue
                    if po == "GB":
                        p1_gates_b(m)
                        continue
                    p1_proj_job(m, po[0], po[1])
                    idx += 1
                    tgt = (len(items) * idx + 23) // 24
                    while k < tgt:
                        items[k]()
                        k += 1
                    slot += 1
                    cv_emit_until((NU * slot) // 192)
                if m + 2 < NMT:
                    p1_stats(m + 2)
                if m == NMT - 1:
                    xhb, xhB = xbuf[1], xB[1][0]
                    S.dma("sp", lambda e, inc: inc(e.dma_start(out=xhb[:, 0, :], in_=x_prev[:, :])), xhB, writes=[xhB])
                    S.op("act", lambda e: e.activation(out=junk[:], in_=xhb[:, 0, :], func=AF.Square, accum_out=ss1[:, 0:1]),
                         reads=[xhB], writes=[ss1B])
                    S.op("act", lambda e: e.activation(out=lnt[:, 0:1], in_=ss1[:, 0:1], func=AF.Ln, bias=EPS, scale=1.0 / D),
                         reads=[ss1B], writes=[lntB])
                    S.op("act", lambda e: e.activation(out=lnt[:, 0:1], in_=lnt[:, 0:1], func=AF.Exp, scale=-0.5),
                         reads=[lntB], writes=[lntB])
                    S.op("dve", lambda e: e.scalar_tensor_tensor(out=hn_bf[:], in0=xhb[:, 0, :], scalar=lnt[:, 0:1],
                                                                 in1=cst[:, C_GMIX:C_GMIX + D], op0=ALU.mult, op1=ALU.mult),
                         reads=[xhB, lntB, cstB], writes=[hnB])
                    for half in range(2):
                        bk, bB = next_bank()

                        def trh(e, bk=bk, half=half):
                            ins = None
                            for i in range(4):
                                kc = half * 4 + i
                                ins = e.matmul(bk[:, i * 128:(i + 1) * 128], lhsT=hn_bf[:, kc * 128:(kc + 1) * 128],
                                               rhs=ident[:], start=True, stop=True)
                            return ins
                        S.op("pe", trh, reads=[hnB, constB], writes=[bB])
                        S.op("act", lambda e, bk=bk, half=half: e.activation(
                            out=hnT[:, half * 4:half * 4 + 4, 0:128], in_=v3(bk[:, :], 4), func=AF.Copy),
                            reads=[bB], writes=[hnTBs[0][0]])
                    S.op("dve", lambda e: e.tensor_copy(out=hT_h[:], in_=hnT[:, :, 126:128]), reads=[hnTBs[0][0]], writes=[hThB])
                    ld_x(0)
                    S.dma("sp", lambda e, inc: inc(e.dma_start(out=hnT[:].rearrange("p k t -> p (k t)"), in_=hns[0])),
                          hnTB, reads=[hnsB[0]], writes=[hnTB] + hnTBs[0])
                kwt, kwtB = kw_m[m % 2], kwmB[m % 2]
                vt, vtB = v_m[m % 2], vmB[m % 2]
                if m == 0:
                    dbg("kw0", kwt[:], [128, NJ, 1024], BF16, kwtB)
                    dbg("v0", vt[:], [128, NJ, 2048], BF16, vtB)
                    dbg("EQ", EQ[:], [128, NCHK, 4], F32, [gateB[0]])
                    dbg("DEC", DEC[:], [128, NCHK, 4], F32, [gateB[0]])
                    dbg("Gs", Gs[:], [128, NJ, 8], F32, [GsB])
                S.dma("sp", lambda e, inc, kwt=kwt, m=m: inc(e.dma_start(
                    out=kws[m * T:(m + 1) * T, :].rearrange("(j p) d -> p j d", p=128), in_=kwt[:])), kwtB[0], reads=kwtB)
                S.dma("sp", lambda e, inc, vt=vt, m=m: inc(e.dma_start(
                    out=vs[m * T:(m + 1) * T, :].rearrange("(j p) d -> p j d", p=128), in_=vt[:])), vtB[0], reads=vtB)
                if m + 1 < NMT:
                    p1_store_hns(m + 1)
            for j in range(NJ):
                for it in p1_state_items(NMT - 1, j):
                    it()
            cv_emit_until(NU)

            S.stop_if("p1")
            ginB = Buf("gin")

            def st_gin(e, inc):
                inc(e.dma_start(out=gin_ts[0].ap()[:, :], in_=Cst[:, 0:2048]))
                inc(e.dma_start(out=gin_ts[1].ap()[:, :], in_=Cst[:, 2048:4096]))
                inc(e.dma_start(out=gin_ts[2].ap()[:, 0:8], in_=nst[:]))
                inc(e.dma_start(out=gin_ts[2].ap()[:, 8:12], in_=LD[:]))
                inc(e.dma_start(out=gin_ts[2].ap()[:, 12:16], in_=LD[:]))
            S.dma("pool", st_gin, ginB, reads=CBall + [nB, LDB], writes=[ginB], ndma=5)
            goutB = Buf("gout")
            waits = S._collect("pool", [ginB], [goutB], ())
            if not S.stopped:
                for i in range(3):
                    S.q["pool"].append((waits if i == 0 else [], lambda e, i=i: e.collective_compute(
                        "AllGather", ALU.bypass, replica_groups=[[0, 1, 2, 3], [4, 5, 6, 7]],
                        ins=[gin_ts[i].ap().opt()], outs=[gout_ts[i].ap().opt()]), (cc_sem, None)))
            cc_ev = Ev(cc_sem, 3)
            goutB.w = cc_ev
            S.barrier()

        S.stop_if("comb")
        p2 = contextlib.ExitStack()
        with p2:
            def sb2(name, shape, dt):
                return p2.enter_context(nc.sbuf_tensor(name, shape, dt))
            NWB = 3
            wring = [sb2("wring%d" % i, [128, 4096], BF16) for i in range(NWB)]
            wringB = [Buf("wring%d" % i) for i in range(NWB)]
            wr_state = {"n": 0}
            order = ([CH_Q, CH_Q + 1] + list(range(CH_GATE, CH_GATE + 8)) + list(range(CH_CONV, CH_CONV + 8))
                     + [CH_GA, CH_WA, CH_GA + 1, CH_WA + 1]
                     + [CH_GB, CH_WB, CH_WB + 1, CH_GB + 1, CH_WB + 2, CH_WB + 3]
                     + [CH_WO, CH_WO + 1, CH_WPG, CH_WPG + 1, CH_WP])
            stream = [c for _ in range(NMT) for c in order]
            pre = {"issued": 0}

            def issue_w():
                k = pre["issued"]
                if k >= len(stream):
                    return
                ci = stream[k]
                slot = k % NWB
                S.dma("sp", lambda e, inc, ci=ci, slot=slot: inc(e.dma_start(out=wring[slot][:], in_=wq16[ci])),
                      wringB[slot], reads=[wq16B[ci]], writes=[wringB[slot]])
                pre["issued"] += 1

            def get_w(expect):
                k = wr_state["n"]
                assert stream[k] == expect, (stream[k], expect)
                while pre["issued"] < k + 1:
                    issue_w()
                wr_state["n"] += 1
                return wring[k % NWB], wringB[k % NWB]

            def after_w(hold=0):
                k = wr_state["n"]
                while pre["issued"] < min(len(stream), k + NWB - 1 - hold):
                    issue_w()

            Cgp = [sb2("Cgp%d" % i, [128, 2048], F32) for i in range(1)]
            CgpB = [Buf("Cgp%d" % i) for i in range(1)]
            Cgs = sb2("Cgs", [128, 4, 16], F32)
            CgsB = Buf("Cgs")
            Eacc = sb2("Eacc", [128, 4, 4], F32)
            EB = Buf("E")
            coef = sb2("coef", [128, 4, 4], F32)
            coefB = Buf("coef")

            def emit_combine():
                def ld_s(e, inc):
                    for r in range(4):
                        inc(e.dma_start(out=Cgs[:, r, :], in_=gout_ts[2].ap()[r * 128:(r + 1) * 128, :]))
                S.dma("sp", ld_s, CgsB, reads=[goutB], writes=[CgsB], ndma=4)
                S.op("dve", lambda e: e.memset(Eacc[:], 0.0), writes=[EB])
                for r in range(4):
                    for i in range(4):
                        S.op("dve", lambda e, r=r, i=i: e.scalar_tensor_tensor(
                            out=Eacc[:, r, :], in0=Cgs[:, i, 8:12], scalar=cst[:, C_BTW + i * 4 + r:C_BTW + i * 4 + r + 1],
                            in1=Eacc[:, r, :], op0=ALU.mult, op1=ALU.add), reads=[CgsB, cstB, EB], writes=[EB])
                S.op("act", lambda e: e.activation(out=coef[:], in_=Eacc[:], func=AF.Exp, scale=-1.0), reads=[EB], writes=[coefB])
                for r in range(4):
                    S.op("dve", lambda e, r=r: e.tensor_scalar(out=coef[:, r, :], in0=coef[:, r, :],
                                                               scalar1=cst[:, C_SEL + r:C_SEL + r + 1], scalar2=None, op0=ALU.mult),
                         reads=[coefB, cstB], writes=[coefB])
                for h in range(H):
                    S.op("dve", lambda e, h=h: e.tensor_scalar(out=nst[:, h * 2:h * 2 + 2], in0=Cgs[:, 0, h * 2:h * 2 + 2],
                                                               scalar1=coef[:, 0, h:h + 1], scalar2=None, op0=ALU.mult),
                         reads=[CgsB, coefB], writes=[nB])
                    for r in range(1, 4):
                        S.op("dve", lambda e, h=h, r=r: e.scalar_tensor_tensor(
                            out=nst[:, h * 2:h * 2 + 2], in0=Cgs[:, r, h * 2:h * 2 + 2], scalar=coef[:, r, h:h + 1],
                            in1=nst[:, h * 2:h * 2 + 2], op0=ALU.mult, op1=ALU.add), reads=[CgsB, coefB, nB], writes=[nB])
                k = 0
                for half in range(2):
                    for r in range(4):
                        pb, pbB = Cgp[0], CgpB[0]
                        k += 1
                        S.dma("sp", lambda e, inc, pb=pb, half=half, r=r: inc(e.dma_start(
                            out=pb[:], in_=gout_ts[half].ap()[r * 128:(r + 1) * 128, :])), pbB, reads=[goutB], writes=[pbB])
                        for hh in range(2):
                            h = half * 2 + hh
                            if r == 0:
                                S.op("dve", lambda e, h=h, hh=hh, pb=pb: e.tensor_scalar(
                                    out=Cst[:, h * 1024:(h + 1) * 1024], in0=pb[:, hh * 1024:(hh + 1) * 1024],
                                    scalar1=coef[:, 0, h:h + 1], scalar2=None, op0=ALU.mult),
                                    reads=[pbB, coefB], writes=CB[h])
                            else:
                                S.op("dve", lambda e, h=h, hh=hh, pb=pb, r=r: e.scalar_tensor_tensor(
                                    out=Cst[:, h * 1024:(h + 1) * 1024], in0=pb[:, hh * 1024:(hh + 1) * 1024],
                                    scalar=coef[:, r, h:h + 1], in1=Cst[:, h * 1024:(h + 1) * 1024], op0=ALU.mult, op1=ALU.add),
                                    reads=[pbB, coefB] + CB[h], writes=CB[h])

            ya_inT = sb2("ya_inT", [128, 8, T], BF16)
            yainB = Buf("ya_in")
            mrg = sb2("mrg", [128, 8, T], BF16)
            mrgB = [Buf("mrg%d" % e) for e in range(8)]
            qT = sb2("qT", [128, 8, T], BF16)
            qTB = Buf("qT")
            gateT = sb2("gateT", [128, 16, T], BF16)
            gtB = [Buf("gateT%d" % d) for d in range(16)]
            hn2b = [hn_bf, sb2("hn_bf2", [128, 1024], BF16)]
            hn2bB = [hnB, Buf("hn_bf2")]
            hn2T = sb2("hn2T", [128, 8, T], BF16)
            hn2TB = Buf("hn2T")
            u_t = sb2("u_t", [128, T + 2], F32)
            uB = Buf("u")
            uh = sb2("uh", [128, 8, 2], F32)
            uhB = Buf("uh")
            t1 = [sb2("t1_%d" % i, [128, T], F32) for i in range(4)]
            t1B = [Buf("t1_%d" % i) for i in range(4)]
            kw_c = [sb2("kw_c%d" % i, [128, 1024], BF16) for i in range(2)]
            kwcB = [Buf("kwc%d" % i) for i in range(2)]
            v_c = [sb2("v_c%d" % i, [128, 2048], BF16) for i in range(2)]
            vcB = [Buf("vc%d" % i) for i in range(2)]
            kwT = sb2("kwT", [128, 8, 128], BF16)
            kwTB = Buf("kwT")
            scT = sb2("scT", [128, 512], BF16)
            scTB = Buf("scT")
            cdec = sb2("cdec", [128, 4096], BF16)
            cdecB = [Buf("cdec%d" % h) for h in range(H)]
            ndec = sb2("ndec", [128, 8], BF16)
            ndecB = Buf("ndec")
            hbh = sb2("hbh", [128, 2048], BF16)
            hbhB = Buf("hbh")
            sm = {k: sb2("sm_" + k, [128, 4], F32) for k in ("c", "rc", "ss", "t", "ln", "sc")}
            smB = {k: Buf("sm_" + k) for k in sm}
            ss2 = sb2("ss2", [128, NJ], F32)
            ss2B = Buf("ss2")
            rs2 = sb2("rs2", [128, NJ], F32)
            rs2B = Buf("rs2")
            p_t = sb2("p_t", [128, NJ, 256], F32)
            ptB = Buf("p_t")
            p_bf = sb2("p_bf", [128, NJ, 256], BF16)
            pbfB = Buf("p_bf")
            pT = sb2("pT", [128, 2, T], BF16)
            pTB = Buf("pT")
            out_evs = []


            def ld_p(m):
                S.dma("sp", lambda e, inc: inc(e.dma_start(
                    out=p_t[:], in_=p_seg[m * T:(m + 1) * T, :].rearrange("(j p) d -> p j d", p=128))), ptB, writes=[ptB])

            def ld_kv(g):
                kwc_, vc_ = kw_c[g % 2], v_c[g % 2]
                S.dma("sp", lambda e, inc: inc(e.dma_start(out=kwc_[:], in_=kws[g * L:(g + 1) * L, :])), kwcB[g % 2], writes=[kwcB[g % 2]])
                S.dma("sp", lambda e, inc: inc(e.dma_start(out=vc_[:], in_=vs[g * L:(g + 1) * L, :])), vcB[g % 2], writes=[vcB[g % 2]])

            def ld_hnT(m):
                S.dma("sp", lambda e, inc: inc(e.dma_start(out=hnT[:].rearrange("p k t -> p (k t)"), in_=hns[m])),
                      hnTB, reads=[hnsB[m]], writes=[hnTB])

            for m in range(NMT):
                xb, xbB = xbuf[m % 2], xB[m % 2]
                gB = gateB[m]
                if m == 0:
                    ld_p(0)
                if m == 0:
                    pass
                hn_jobs = []
                for j in range(NJ):
                    hn_jobs.append(j)
                if m == 0:
                    pass
                dbg("hnT", hnT[:], [128, 8, T], BF16, [hnTB])
                S.stop_if("p2B")
                def conv_chunk(c, m=m):
                    wt, wtB = get_w(CH_CONV + c)
                    w3 = v3(wt[:, :], 8)
                    if m == 0:
                        bk, bB = next_bank()

                        def hal(e, bk=bk, w3=w3):
                            ins = None
                            for g in range(2):
                                for kc in range(8):
                                    ins = e.matmul(bk[:, g * 2:g * 2 + 2], lhsT=w3[:, kc, g * 128:(g + 1) * 128],
                                                   rhs=hT_h[:, kc, :], start=(kc == 0), stop=(kc == 7))
                            return ins
                        S.op("pe", hal, reads=[wtB, hThB], writes=[bB])
                        S.op("act", lambda e, bk=bk: e.activation(out=t1[0][:, 0:2], in_=bk[:, 0:2], func=AF.Copy),
                             reads=[bB], writes=[t1B[0]])
                        S.op("dve", lambda e, bk=bk: e.tensor_tensor(out=u_t[:, 0:2], in0=t1[0][:, 0:2], in1=bk[:, 2:4],
                                                                     op=ALU.mult), reads=[bB, t1B[0]], writes=[uB])
                    else:
                        S.op("pool", lambda e, c=c: e.tensor_copy(out=u_t[:, 0:2], in_=uh[:, c, :]), reads=[uhB], writes=[uB])
                    bks = []
                    for g in range(4):
                        bk, bB = next_bank()

                        def cj(e, bk=bk, g=g, w3=w3):
                            ins = None
                            for kc in range(8):
                                ins = e.matmul(bk[:, :], lhsT=w3[:, kc, g * 128:(g + 1) * 128], rhs=hnT[:, kc, :],
                                               start=(kc == 0), stop=(kc == 7))
                            return ins
                        S.op("pe", cj, reads=[wtB, hnTB], writes=[bB])
                        bks.append((bk, bB))
                    after_w()
                    (bxa, Bxa), (bca, Bca), (bba, Bba), (bza, Bza) = bks
                    S.op("act", lambda e, bk=bxa: e.activation(out=t1[0][:], in_=bk[:, :], func=AF.Copy),
                         reads=[Bxa], writes=[t1B[0]])
                    S.op("dve", lambda e, bk=bca: e.tensor_tensor(out=u_t[:, 2:T + 2], in0=t1[0][:], in1=bk[:, :], op=ALU.mult),
                         reads=[Bca, t1B[0]], writes=[uB])
                    S.op("pool", lambda e, c=c: e.tensor_copy(out=uh[:, c, :], in_=u_t[:, T:T + 2]), reads=[uB], writes=[uhB])
                    S.op("act", lambda e, c=c: e.activation(out=t1[1][:], in_=u_t[:, 0:T], func=AF.Identity,
                                                            bias=cst[:, C_CB + c:C_CB + c + 1],
                                                            scale=cst[:, C_CW + c * 3:C_CW + c * 3 + 1]),
                         reads=[uB, cstB], writes=[t1B[1]])
                    S.op("dve", lambda e, c=c: e.scalar_tensor_tensor(out=t1[1][:], in0=u_t[:, 1:T + 1],
                                                                      scalar=cst[:, C_CW + c * 3 + 1:C_CW + c * 3 + 2],
                                                                      in1=t1[1][:], op0=ALU.mult, op1=ALU.add),
                         reads=[uB, cstB, t1B[1]], writes=[t1B[1]])
                    S.op("dve", lambda e, c=c: e.scalar_tensor_tensor(out=t1[1][:], in0=u_t[:, 2:T + 2],
                                                                      scalar=cst[:, C_CW + c * 3 + 2:C_CW + c * 3 + 3],
                                                                      in1=t1[1][:], op0=ALU.mult, op1=ALU.add),
                         reads=[uB, cstB, t1B[1]], writes=[t1B[1]])
                    S.op("dve", lambda e, bk=bba: e.tensor_tensor(out=t1[2][:], in0=t1[1][:], in1=bk[:, :], op=ALU.mult),
                         reads=[Bba, t1B[1]], writes=[t1B[2]])
                    S.op("act", lambda e, bk=bza: e.activation(out=t1[3][:], in_=bk[:, :], func=AF.Silu),
                         reads=[Bza], writes=[t1B[3]])
                    S.op("pool", lambda e, c=c: e.tensor_tensor(out=ya_inT[:, c, :], in0=t1[2][:], in1=t1[3][:], op=ALU.mult),
                         reads=[t1B[2], t1B[3]], writes=[yainB])

                for i in range(2):
                    wt, wtB = get_w(CH_Q + i)
                    w3 = v3(wt[:, :], 8)
                    for s4 in range(4):
                        hc = i * 4 + s4
                        bk, bB = next_bank()

                        def qj(e, bk=bk, w3=w3, s4=s4):
                            ins = None
                            for kc in range(8):
                                ins = e.matmul(bk[:, :], lhsT=w3[:, kc, s4 * 128:(s4 + 1) * 128], rhs=hnT[:, kc, :],
                                               start=(kc == 0), stop=(kc == 7))
                            return ins
                        S.op("pe", qj, reads=[wtB, hnTB], writes=[bB])
                        S.op("act", lambda e, bk=bk, hc=hc: e.activation(out=qT[:, hc, :], in_=bk[:, :], func=AF.Copy,
                                                                         scale=DK ** -0.5), reads=[bB], writes=[qTB])
                    after_w()

                S.op("pool", lambda e: e.tensor_copy(out=p_bf[:], in_=p_t[:]), reads=[ptB], writes=[pbfB])
                if m + 1 < NMT:
                    ld_p(m + 1)
                for kc2 in range(2):
                    bk, bB = next_bank()

                    def trp(e, bk=bk, kc2=kc2):
                        ins = None
                        for j in range(NJ):
                            ins = e.matmul(bk[:, j * 128:(j + 1) * 128], lhsT=p_bf[:, j, kc2 * 128:(kc2 + 1) * 128],
                                           rhs=ident[:], start=True, stop=True)
                        return ins
                    S.op("pe", trp, reads=[pbfB, constB], writes=[bB])
                    S.op("act", lambda e, bk=bk, kc2=kc2: e.activation(out=pT[:, kc2, :], in_=bk[:, :], func=AF.Copy),
                         reads=[bB], writes=[pTB])

                for i in range(8):
                    wt, wtB = get_w(CH_GATE + i)
                    w3 = v3(wt[:, :], 8)
                    for dd in range(2):
                        d = i * 2 + dd
                        bko, bBo = next_bank()

                        def oj(e, bk=bko, w3=w3, dd=dd):
                            ins = None
                            for kc in range(8):
                                ins = e.matmul(bk[:, :], lhsT=w3[:, kc, dd * 256:dd * 256 + 128], rhs=hnT[:, kc, :],
                                               start=(kc == 0), stop=(kc == 7))
                            return ins
                        S.op("pe", oj, reads=[wtB, hnTB], writes=[bBo])
                        bkz, bBz = next_bank()

                        def zj(e, bk=bkz, w3=w3, dd=dd):
                            ins = None
                            for kc in range(8):
                                ins = e.matmul(bk[:, :], lhsT=w3[:, kc, dd * 256 + 128:dd * 256 + 256], rhs=hnT[:, kc, :],
                                               start=(kc == 0), stop=(kc == 7))
                            return ins
                        S.op("pe", zj, reads=[wtB, hnTB], writes=[bBz])
                        ta, tb = (2, 3) if d % 2 == 0 else (0, 1)
                        S.op("act", lambda e, bk=bko, ta=ta: e.activation(out=t1[ta][:], in_=bk[:, :], func=AF.Tanh, scale=0.5),
                             reads=[bBo], writes=[t1B[ta]])
                        S.op("act", lambda e, bk=bkz, tb=tb: e.activation(out=t1[tb][:], in_=bk[:, :], func=AF.Silu),
                             reads=[bBz], writes=[t1B[tb]])
                        S.op("dve", lambda e, ta=ta, tb=tb: e.scalar_tensor_tensor(out=t1[ta][:], in0=t1[ta][:], scalar=1.0, in1=t1[tb][:],
                                                                                   op0=ALU.add, op1=ALU.mult),
                             reads=[t1B[ta], t1B[tb]], writes=[t1B[ta]])
                        S.op("pool", lambda e, d=d, ta=ta: e.tensor_scalar(out=gateT[:, d, :], in0=t1[ta][:],
                                                                           scalar1=cst[:, C_GHEAD + d:C_GHEAD + d + 1], scalar2=0.5,
                                                                           op0=ALU.mult, op1=ALU.mult),
                             reads=[t1B[ta], cstB], writes=[gtB[d]])
                    after_w()

                dbg("qT", qT[:], [128, 8, T], BF16, [qTB])
                dbg("gateF", gateT[:], [128, 16, T], BF16, gtB)
                S.stop_if("p2F")
                if m + 1 < NMT:
                    ld_x(m + 1)
                if m == 0:
                    ld_kv(0)
                    emit_combine()
                for j in range(NJ):
                    ch = m * NJ + j
                    tok0 = m * T + j * L
                    kwc, kwB_ = kw_c[ch % 2], kwcB[ch % 2]
                    vc, vB_ = v_c[ch % 2], vcB[ch % 2]
                    if ch + 1 < NCHK:
                        ld_kv(ch + 1)
                    for half in range(2):
                        bk, bB = next_bank()

                        def trk(e, bk=bk, half=half, kwc=kwc):
                            ins = None
                            for i in range(4):
                                hc = half * 4 + i
                                ins = e.matmul(bk[:, i * 128:(i + 1) * 128], lhsT=kwc[:, hc * 128:(hc + 1) * 128],
                                               rhs=ident[:], start=True, stop=True)
                            return ins
                        S.op("pe", trk, reads=[kwB_, constB], writes=[bB])
                        S.op("act", lambda e, bk=bk, half=half: e.activation(
                            out=kwT[:, half * 4:half * 4 + 4, :], in_=v3(bk[:, :], 4), func=AF.Copy),
                            reads=[bB], writes=[kwTB])
                    for h in range(H):
                        S.op("act", lambda e, h=h, ch=ch: e.activation(out=cdec[:, h * 1024:(h + 1) * 1024], in_=Cst[:, h * 1024:(h + 1) * 1024],
                                                                       func=AF.Copy, scale=DEC[:, ch, h:h + 1]),
                             reads=[CB[h][0], CB[h][1], gB], writes=[cdecB[h]])
                        S.op("dve", lambda e, h=h, ch=ch: e.tensor_scalar(out=ndec[:, h * 2:h * 2 + 2], in0=nst[:, h * 2:h * 2 + 2],
                                                                          scalar1=DEC[:, ch, h:h + 1], scalar2=None,
                                                                          op0=ALU.mult), reads=[nB, gB], writes=[ndecB])
                    bks_, bBs = next_bank()

                    def sj(e, bk=bks_, j=j):
                        ins = None
                        for h in range(H):
                            for cc in range(2):
                                ins = e.matmul(bk[:, h * 128:(h + 1) * 128], lhsT=kwT[:, h * 2 + cc, :],
                                               rhs=qT[:, h * 2 + cc, j * 128:(j + 1) * 128], start=(cc == 0), stop=(cc == 1))
                        return ins
                    S.op("pe", sj, reads=[kwTB, qTB], writes=[bBs])
                    conv_chunk(2 * j)
                    S.op("dve", lambda e, bk=bks_: e.tensor_tensor(out=scT[:], in0=bk[:, :], in1=mask4[:], op=ALU.mult),
                         reads=[bBs, constB], writes=[scTB])
                    bkd, bBd = next_bank()

                    def dj(e, bk=bkd, j=j):
                        ins = None
                        for h in range(H):
                            ins = e.matmul(bk[:, h:h + 1], lhsT=scT[:, h * 128:(h + 1) * 128], rhs=ones_bf[:], start=True, stop=False)
                            for cc in range(2):
                                ins = e.matmul(bk[:, h:h + 1], lhsT=qT[:, h * 2 + cc, j * 128:(j + 1) * 128],
                                               rhs=ndec[:, h * 2 + cc:h * 2 + cc + 1], start=False, stop=(cc == 1))
                        return ins
                    S.op("pe", dj, reads=[scTB, qTB, ndecB, constB], writes=[bBd])
                    S.op("dve", lambda e, bk=bkd: e.tensor_scalar(out=sm["t"][:], in0=bk[:, 0:4], scalar1=-1.0, scalar2=None,
                                                                  op0=ALU.mult), reads=[bBd], writes=[smB["t"]])
                    S.op("dve", lambda e, bk=bkd: e.tensor_tensor(out=sm["c"][:], in0=bk[:, 0:4], in1=sm["t"][:], op=ALU.max),
                         reads=[bBd, smB["t"]], writes=[smB["c"]])
                    S.op("dve", lambda e, ch=ch: e.tensor_tensor(out=sm["c"][:], in0=sm["c"][:], in1=EQ[:, ch, :], op=ALU.max),
                         reads=[smB["c"], gB], writes=[smB["c"]])
                    S.op("dve", lambda e: e.reciprocal(out=sm["rc"][:], in_=sm["c"][:]), reads=[smB["c"]], writes=[smB["rc"]])
                    nbk = []
                    for h in range(H):
                        bk, bB = next_bank()

                        def numj(e, bk=bk, h=h, j=j, vc=vc):
                            e.matmul(bk[:, :], lhsT=scT[:, h * 128:(h + 1) * 128], rhs=vc[:, h * 512:(h + 1) * 512],
                                     start=True, stop=False)
                            ins = None
                            for cc in range(2):
                                ins = e.matmul(bk[:, :], lhsT=qT[:, h * 2 + cc, j * 128:(j + 1) * 128],
                                               rhs=cdec[:, h * 1024 + cc * 512:h * 1024 + (cc + 1) * 512], start=False, stop=(cc == 1))
                            return ins
                        S.op("pe", numj, reads=[scTB, vB_, qTB, cdecB[h]], writes=[bB])
                        S.op("act", lambda e, bk=bk, h=h: e.activation(out=junk[:, 0:512], in_=bk[:, :], func=AF.Square,
                                                                       accum_out=sm["ss"][:, h:h + 1]),
                             reads=[bB], writes=[smB["ss"]])
                        nbk.append((bk, bB))
                    S.op("dve", lambda e: e.tensor_tensor(out=sm["t"][:], in0=sm["rc"][:], in1=sm["rc"][:], op=ALU.mult),
                         reads=[smB["rc"]], writes=[smB["t"]])
                    S.op("dve", lambda e: e.tensor_tensor(out=sm["t"][:], in0=sm["t"][:], in1=sm["ss"][:], op=ALU.mult),
                         reads=[smB["t"], smB["ss"]], writes=[smB["t"]])
                    S.op("act", lambda e: e.activation(out=sm["ln"][:], in_=sm["t"][:], func=AF.Ln, bias=EPS, scale=1.0 / DV),
                         reads=[smB["t"]], writes=[smB["ln"]])
                    S.op("act", lambda e: e.activation(out=sm["ln"][:], in_=sm["ln"][:], func=AF.Exp, scale=-0.5),
                         reads=[smB["ln"]], writes=[smB["ln"]])
                    S.op("dve", lambda e: e.tensor_tensor(out=sm["sc"][:], in0=sm["ln"][:], in1=sm["rc"][:], op=ALU.mult),
                         reads=[smB["ln"], smB["rc"]], writes=[smB["sc"]])
                    for h in range(H):
                        bk, bB = nbk[h]
                        S.op("act", lambda e, bk=bk, h=h: e.activation(out=hbh[:, h * 512:(h + 1) * 512], in_=bk[:, :], func=AF.Copy,
                                                                       scale=sm["sc"][:, h:h + 1]),
                             reads=[bB, smB["sc"]], writes=[hbhB])
                    conv_chunk(2 * j + 1)
                    for q4 in range(4):
                        bk, bB = next_bank()

                        def trh2(e, bk=bk, q4=q4):
                            ins = None
                            for i in range(4):
                                d = q4 * 4 + i
                                ins = e.matmul(bk[:, i * 128:(i + 1) * 128], lhsT=hbh[:, d * 128:(d + 1) * 128],
                                               rhs=ident[:], start=True, stop=True)
                            return ins
                        S.op("pe", trh2, reads=[hbhB, constB], writes=[bB])
                        S.op("dve", lambda e, bk=bk, q4=q4, j=j: e.tensor_tensor(
                            out=gateT[:, q4 * 4:q4 * 4 + 4, j * 128:(j + 1) * 128], in0=v3(bk[:, :], 4),
                            in1=gateT[:, q4 * 4:q4 * 4 + 4, j * 128:(j + 1) * 128], op=ALU.mult),
                            reads=[bB] + gtB[q4 * 4:q4 * 4 + 4], writes=gtB[q4 * 4:q4 * 4 + 4])
                    bkn, bBn = next_bank()

                    def nj2(e, bk=bkn, kwc=kwc):
                        ins = None
                        for hc in range(8):
                            ins = e.matmul(bk[:, hc:hc + 1], lhsT=kwc[:, hc * 128:(hc + 1) * 128], rhs=ones_bf[:],
                                           start=True, stop=True)
                        return ins
                    S.op("pe", nj2, reads=[kwB_, constB], writes=[bBn])
                    for h in range(H):
                        S.op("dve", lambda e, h=h, ch=ch: e.tensor_scalar(
                            out=nst[:, h * 2:h * 2 + 2], in0=nst[:, h * 2:h * 2 + 2], scalar1=DEC[:, ch, h:h + 1],
                            scalar2=None, op0=ALU.mult), reads=[nB, gB], writes=[nB])
                    S.op("dve", lambda e, bk=bkn: e.tensor_tensor(out=nst[:], in0=nst[:], in1=bk[:, 0:8], op=ALU.add),
                         reads=[nB, bBn], writes=[nB])
                    for h in range(H):
                        for cc in range(2):
                            bk, bB = next_bank()
                            S.op("pe", lambda e, bk=bk, h=h, cc=cc, kwc=kwc, vc=vc: e.matmul(
                                bk[:, :], lhsT=kwc[:, h * 256 + cc * 128:h * 256 + (cc + 1) * 128],
                                rhs=vc[:, h * 512:(h + 1) * 512], start=True, stop=True),
                                reads=[kwB_, vB_], writes=[bB])
                            S.op("dve", lambda e, bk=bk, h=h, cc=cc, ch=ch: e.scalar_tensor_tensor(
                                out=Cst[:, h * 1024 + cc * 512:h * 1024 + (cc + 1) * 512], in0=Cst[:, h * 1024 + cc * 512:h * 1024 + (cc + 1) * 512],
                                scalar=DEC[:, ch, h:h + 1], in1=bk[:, :], op0=ALU.mult, op1=ALU.add),
                                reads=[bB, gB, CB[h][cc]], writes=[CB[h][cc]])

                dbg("ya_inT", ya_inT[:], [128, 8, T], BF16, [yainB])
                S.stop_if("p2C")
                for e8 in range(8):
                    if e8 % 4 == 0:
                        if e8:
                            after_w()
                        wg, wgB = get_w(CH_GA + e8 // 4)
                        wa, waB = get_w(CH_WA + e8 // 4)
                    wg3, wa3 = v3(wg[:, :], 8), v3(wa[:, :], 8)
                    co = (e8 % 4) * 128
                    bkg, bBg = next_bank()

                    def gaj(e, bk=bkg, wg3=wg3, co=co):
                        ins = None
                        for kc in range(8):
                            ins = e.matmul(bk[:, :], lhsT=wg3[:, kc, co:co + 128], rhs=hnT[:, kc, :], start=(kc == 0), stop=(kc == 7))
                        return ins
                    S.op("pe", gaj, reads=[wgB, hnTB], writes=[bBg])
                    bka, bBa = next_bank()

                    def yaj(e, bk=bka, wa3=wa3, co=co):
                        ins = None
                        for kc in range(8):
                            ins = e.matmul(bk[:, :], lhsT=wa3[:, kc, co:co + 128], rhs=ya_inT[:, kc, :], start=(kc == 0), stop=(kc == 7))
                        return ins
                    S.op("pe", yaj, reads=[waB, yainB], writes=[bBa])
                    ta = (e8 % 4)
                    S.op("act", lambda e, bk=bkg, ta=ta: e.activation(out=t1[ta][:], in_=bk[:, :], func=AF.Tanh, scale=0.5),
                         reads=[bBg], writes=[t1B[ta]])
                    S.op("dve", lambda e, bk=bka, e8=e8, ta=ta: e.scalar_tensor_tensor(out=mrg[:, e8, :], in0=t1[ta][:], scalar=1.0,
                                                                                       in1=bk[:, :], op0=ALU.add, op1=ALU.mult),
                         reads=[bBa, t1B[ta]], writes=[mrgB[e8]])
                after_w()

                dbg("mrgD", mrg[:], [128, 8, T], BF16, mrgB)
                dbg("gateG", gateT[:], [128, 16, T], BF16, gtB)
                dbg("Cst", Cst[:], [128, 4096], F32, CBall)
                S.stop_if("p2G")
                for i in range(4):
                    if i % 2 == 0:
                        wg, wgB = get_w(CH_GB + i // 2)
                    wt, wtB = get_w(CH_WB + i)
                    wb3 = v3(wt[:, :], 16)
                    for ee in range(2):
                        e8 = i * 2 + ee
                        wg3 = v3(wg[:, :], 8)
                        co = (e8 % 4) * 128
                        bkg, bBg = next_bank()

                        def gbj(e, bk=bkg, wg3=wg3, co=co):
                            ins = None
                            for kc in range(8):
                                ins = e.matmul(bk[:, :], lhsT=wg3[:, kc, co:co + 128], rhs=hnT[:, kc, :], start=(kc == 0), stop=(kc == 7))
                            return ins
                        S.op("pe", gbj, reads=[wgB, hnTB], writes=[bBg])
                        bky, bBy = next_bank()

                        def ybj(e, bk=bky, wb3=wb3, ee=ee):
                            ins = None
                            for kc in range(16):
                                ins = e.matmul(bk[:, :], lhsT=wb3[:, kc, ee * 128:(ee + 1) * 128], rhs=gateT[:, kc, :],
                                               start=(kc == 0), stop=(kc == 15))
                            return ins
                        S.op("pe", ybj, reads=[wtB] + gtB, writes=[bBy])
                        ta, tb = (0, 1) if e8 % 2 == 0 else (2, 3)
                        S.op("act", lambda e, bk=bkg, ta=ta: e.activation(out=t1[ta][:], in_=bk[:, :], func=AF.Tanh, scale=0.5),
                             reads=[bBg], writes=[t1B[ta]])
                        S.op("dve", lambda e, bk=bky, ta=ta, tb=tb: e.scalar_tensor_tensor(out=t1[tb][:], in0=t1[ta][:], scalar=1.0, in1=bk[:, :],
                                                                                           op0=ALU.add, op1=ALU.mult),
                             reads=[bBy, t1B[ta]], writes=[t1B[tb]])
                        S.op("pool", lambda e, e8=e8, tb=tb: e.tensor_tensor(out=mrg[:, e8, :], in0=mrg[:, e8, :], in1=t1[tb][:], op=ALU.add),
                             reads=[t1B[tb], mrgB[e8]], writes=[mrgB[e8]])
                    after_w(hold=1 if i % 2 == 0 else 0)

                dbg("mrgH", mrg[:], [128, 8, T], BF16, mrgB)
                S.stop_if("p2H")
                if m + 1 < NMT:
                    ld_hnT(m + 1)
                wow = [get_w(CH_WO), get_w(CH_WO + 1)]
                ss2j = [Buf("ss2_%d" % j) for j in range(NJ)]
                rs2j = [Buf("rs2_%d" % j) for j in range(NJ)]
                for jj in range(NJ + 1):
                    if jj < NJ:
                        j = jj
                        for cb in range(2):
                            wo, woB = wow[cb]
                            wo3 = v3(wo[:, :], 8)
                            bk, bB = next_bank()

                            def x2j(e, bk=bk, wo3=wo3, j=j):
                                ins = None
                                for kc in range(8):
                                    ins = e.matmul(bk[:, :], lhsT=mrg[:, kc, j * 128:(j + 1) * 128], rhs=wo3[:, kc, :],
                                                   start=(kc == 0), stop=(kc == 7))
                                return ins
                            S.op("pe", x2j, reads=[woB] + mrgB, writes=[bB])
                            S.op("dve", lambda e, bk=bk, xb=xb, j=j, cb=cb: e.scalar_tensor_tensor(
                                out=xb[:, j, cb * 512:(cb + 1) * 512], in0=bk[:, :], scalar=0.5, in1=xb[:, j, cb * 512:(cb + 1) * 512],
                                op0=ALU.mult, op1=ALU.add), reads=[bB, xbB[j]], writes=[xbB[j]])
                        if jj == NJ - 1:
                            after_w()
                        S.op("act", lambda e, xb=xb, j=j: e.activation(out=junk[:], in_=xb[:, j, :], func=AF.Square,
                                                                      accum_out=ss2[:, j:j + 1]),
                             reads=[xbB[j]], writes=[ss2j[j]])
                        S.op("act", lambda e, j=j: e.activation(out=rs2[:, j:j + 1], in_=ss2[:, j:j + 1], func=AF.Ln, bias=EPS, scale=1.0 / D),
                             reads=[ss2j[j]], writes=[rs2j[j]])
                        S.op("act", lambda e, j=j: e.activation(out=rs2[:, j:j + 1], in_=rs2[:, j:j + 1], func=AF.Exp, scale=-0.5),
                             reads=[rs2j[j]], writes=[rs2j[j]])
                        S.op("dve", lambda e, xb=xb, j=j: e.scalar_tensor_tensor(
                            out=hn2b[j % 2][:], in0=xb[:, j, :], scalar=rs2[:, j:j + 1], in1=cst[:, C_GPLE:C_GPLE + D],
                            op0=ALU.mult, op1=ALU.mult), reads=[xbB[j], rs2j[j], cstB], writes=[hn2bB[j % 2]])
                    if jj >= 1:
                        j = jj - 1
                        for half in range(2):
                            bk, bB = next_bank()

                            def tr2(e, bk=bk, half=half, j=j):
                                ins = None
                                for i in range(4):
                                    kc = half * 4 + i
                                    ins = e.matmul(bk[:, i * 128:(i + 1) * 128], lhsT=hn2b[j % 2][:, kc * 128:(kc + 1) * 128],
                                                   rhs=ident[:], start=True, stop=True)
                                return ins
                            S.op("pe", tr2, reads=[hn2bB[j % 2], constB], writes=[bB])
                            S.op("act", lambda e, bk=bk, half=half, j=j: e.activation(
                                out=hn2T[:, half * 4:half * 4 + 4, j * 128:(j + 1) * 128], in_=v3(bk[:, :], 4), func=AF.Copy),
                                reads=[bB], writes=[hn2TB])

                dbg("x2", xb[:], [128, NJ, 1024], F32, xbB)
                wpgw = [get_w(CH_WPG), get_w(CH_WPG + 1)]
                wp, wpB = get_w(CH_WP)
                wp3 = wp[:, 0:2048].rearrange("p (a b) -> p a b", a=2)
                for j in range(NJ):
                    for cb in range(2):
                        wg, wgB = wpgw[cb]
                        wg3 = v3(wg[:, :], 8)
                        bkg, bBg = next_bank()

                        def pgj(e, bk=bkg, wg3=wg3, j=j):
                            ins = None
                            for kc in range(8):
                                ins = e.matmul(bk[:, :], lhsT=hn2T[:, kc, j * 128:(j + 1) * 128], rhs=wg3[:, kc, :],
                                               start=(kc == 0), stop=(kc == 7))
                            return ins
                        S.op("pe", pgj, reads=[wgB, hn2TB], writes=[bBg])
                        bkp, bBp = next_bank()

                        def pwj(e, bk=bkp, j=j, cb=cb):
                            ins = None
                            for kc in range(2):
                                ins = e.matmul(bk[:, :], lhsT=pT[:, kc, j * 128:(j + 1) * 128], rhs=wp3[:, kc, cb * 512:(cb + 1) * 512],
                                               start=(kc == 0), stop=(kc == 1))
                            return ins
                        S.op("pe", pwj, reads=[wpB, pTB], writes=[bBp])
                        ta, tb = (0, 1) if cb == 0 else (2, 3)
                        S.op("act", lambda e, bk=bkg, ta=ta: e.activation(out=t1[ta][:], in_=bk[:, :], func=AF.Tanh, scale=0.5),
                             reads=[bBg], writes=[t1B[ta]])
                        S.op("dve", lambda e, bk=bkp, ta=ta, tb=tb: e.scalar_tensor_tensor(out=t1[tb][:], in0=t1[ta][:], scalar=1.0, in1=bk[:, :],
                                                                                           op0=ALU.add, op1=ALU.mult),
                             reads=[bBp, t1B[ta]], writes=[t1B[tb]])
                        S.op("dve", lambda e, xb=xb, j=j, cb=cb, tb=tb: e.scalar_tensor_tensor(
                            out=xb[:, j, cb * 512:(cb + 1) * 512], in0=t1[tb][:], scalar=0.5, in1=xb[:, j, cb * 512:(cb + 1) * 512],
                            op0=ALU.mult, op1=ALU.add), reads=[t1B[tb], xbB[j]], writes=[xbB[j]])
                after_w()

                dbg("x3", xb[:], [128, NJ, 1024], F32, xbB)
                S.stop_if("p2J")
                for j in range(NJ):
                    S.op("act", lambda e, xb=xb, j=j: e.activation(out=junk[:], in_=xb[:, j, :], func=AF.Square,
                                                                  accum_out=ss2[:, j:j + 1]),
                         reads=[xbB[j]], writes=[ss2B])
                S.op("act", lambda e: e.activation(out=rs2[:], in_=ss2[:], func=AF.Ln, bias=EPS, scale=1.0 / D),
                     reads=[ss2B], writes=[rs2B])
                S.op("act", lambda e: e.activation(out=rs2[:], in_=rs2[:], func=AF.Exp, scale=-0.5), reads=[rs2B], writes=[rs2B])
                for j in range(NJ):
                    S.op("dve", lambda e, xb=xb, j=j: e.scalar_tensor_tensor(
                        out=xb[:, j, :], in0=xb[:, j, :], scalar=rs2[:, j:j + 1], in1=cst[:, C_GFIN:C_GFIN + D],
                        op0=ALU.mult, op1=ALU.mult), reads=[xbB[j], rs2B, cstB], writes=[xbB[j]])
                ev = S.dma("sp", lambda e, inc, xb=xb, m=m: inc(e.dma_start(
                    out=out_d[m * T:(m + 1) * T, :].rearrange("(j p) d -> p j d", p=128), in_=xb[:])), xbB[0], reads=xbB)
                out_evs.append(ev)

            S.stop_if("m0") if False else None
            S.wait_only("sp", out_evs)

            with nc.Block() as block:
                @block.tensor
                def _(pe):
                    S.emit("pe", pe)

                @block.scalar
                def _(act):
                    S.emit("act", act)

                @block.vector
                def _(dve):
                    S.emit("dve", dve)

                @block.gpsimd
                def _(pool):
                    S.emit("pool", pool)

                @block.sync
                def _(sp):
                    S.emit("sp", sp)
    return nc


def _chunk(W, cols, kc):
    a = W[:, cols]
    n = a.shape[1]
    a = a.reshape(kc, 128, n).transpose(1, 0, 2).reshape(128, kc * n)
    out = np.zeros((128, 4096), np.float32)
    out[:, :kc * n] = a
    return out


_NC_CACHE = {}


def kernel(x, p, g_mix, w_in, conv_w, conv_b, w_a_out, b_gates, g_head, w_b_out, w_o, g_ple,
           w_ple_gate, w_ple, g_final):
    x = np.asarray(x, np.float32)
    p = np.asarray(p, np.float32)
    w_in0 = np.asarray(w_in, np.float32)[0]
    r = np.arange
    OXA, OBA, OCA, OZA, OQ, OK_, OV, OO, OZB, OIG, OGA, OGB = 0, 1024, 2048, 3072, 4096, 5120, 6144, 8192, 10240, 12288, 12296, 13320
    chunks = []
    for c in range(8):
        cols = np.concatenate([OXA + c * 128 + r(128), OCA + c * 128 + r(128), OBA + c * 128 + r(128), OZA + c * 128 + r(128)])
        chunks.append(_chunk(w_in0, cols, 8))
    for i in range(2):
        chunks.append(_chunk(w_in0, OQ + i * 512 + r(512), 8))
    for i in range(8):
        d0, d1 = 2 * i, 2 * i + 1
        cols = np.concatenate([OO + d0 * 128 + r(128), OZB + d0 * 128 + r(128), OO + d1 * 128 + r(128), OZB + d1 * 128 + r(128)])
        chunks.append(_chunk(w_in0, cols, 8))
    for i in range(2):
        chunks.append(_chunk(w_in0, OGA + i * 512 + r(512), 8))
    wa = np.asarray(w_a_out, np.float32)[0]
    for i in range(2):
        chunks.append(_chunk(wa, i * 512 + r(512), 8))
    for i in range(2):
        chunks.append(_chunk(w_in0, OGB + i * 512 + r(512), 8))
    wb = np.asarray(w_b_out, np.float32)[0]
    for i in range(4):
        chunks.append(_chunk(wb, i * 256 + r(256), 16))
    wo = np.asarray(w_o, np.float32)[0]
    for i in range(2):
        chunks.append(_chunk(wo, i * 512 + r(512), 8))
    wpg = np.asarray(w_ple_gate, np.float32)[0]
    for i in range(2):
        chunks.append(_chunk(wpg, i * 512 + r(512), 8))
    wp = np.asarray(w_ple, np.float32)[0]
    chunks.append(_chunk(wp, r(1024), 2))
    wq32 = np.ascontiguousarray(np.stack(chunks, 0))
    assert wq32.shape == (NWCH, 128, 4096)
    wkv32 = np.ascontiguousarray(w_in0[:, OK_:OK_ + 3072].reshape(8, 128, 3072).transpose(1, 0, 2).reshape(128, 8 * 3072))
    wif32 = np.ascontiguousarray(w_in0[:, OIG:OIG + 8].reshape(8, 128, 8).transpose(1, 0, 2).reshape(128, 64))

    cst = np.zeros((NCORES, 128, NCST), np.float32)
    cst[:, :, C_ID:C_ID + 128] = np.eye(128, dtype=np.float32)
    tri = (np.arange(128)[:, None] <= np.arange(128)[None, :]).astype(np.float32)
    cst[:, :, C_U:C_U + 128] = tri
    cst[:, :, C_ONES:C_ONES + 128] = 1.0
    cst[:, :, C_MASK:C_MASK + 512] = np.tile(tri, (1, 4))
    cst[:, :, C_GMIX:C_GMIX + D] = np.asarray(g_mix, np.float32)[0][None, :]
    cst[:, :, C_GPLE:C_GPLE + D] = np.asarray(g_ple, np.float32)[0][None, :]
    cst[:, :, C_GFIN:C_GFIN + D] = np.asarray(g_final, np.float32)[None, :]
    cst[:, :, C_GHEAD:C_GHEAD + 16] = np.asarray(g_head, np.float32)[0].reshape(16, 128).T
    cw = np.asarray(conv_w, np.float32)[0]
    cst[:, :, C_CW:C_CW + 24] = cw.reshape(3, 8, 128).transpose(2, 1, 0).reshape(128, 24)
    cst[:, :, C_CB:C_CB + 8] = np.asarray(conv_b, np.float32)[0].reshape(8, 128).T
    cst[:, :, C_BG:C_BG + 8] = np.asarray(b_gates, np.float32)[0][None, :]
    for c in range(NCORES):
        s = c % 4
        btw = np.zeros((4, 4), np.float32)
        sel = np.zeros((4,), np.float32)
        for rr in range(4):
            if rr < s:
                sel[rr] = 1.0
            for i in range(4):
                if rr < i < s:
                    btw[i, rr] = 1.0
        cst[c, :, C_BTW:C_BTW + 16] = btw.reshape(16)[None, :]
        cst[c, :, C_SEL:C_SEL + 4] = sel[None, :]

    in_maps = []
    for c in range(NCORES):
        b, s = c // 4, c % 4
        xs = np.ascontiguousarray(x[b, s * NT:(s + 1) * NT])
        if s == 0:
            xp = np.zeros((L, D), np.float32)
        else:
            xp = np.ascontiguousarray(x[b, s * NT - L:s * NT])
        ps = np.ascontiguousarray(p[0, b, s * NT:(s + 1) * NT])
        in_maps.append({"x_seg": xs, "x_prev": xp, "p_seg": ps, "wq32": wq32, "wkv32": wkv32, "wif32": wif32,
                        "cst": np.ascontiguousarray(cst[c])})
    if "nc" not in _NC_CACHE:
        _NC_CACHE["nc"] = build_nc()
    nc = _NC_CACHE["nc"]
    res = run_bass_kernel_spmd(nc, in_maps, core_ids=list(range(NCORES)))
    _NC_CACHE["res"] = res
    out = np.empty((2, SEQ, D), np.float32)
    for c in range(NCORES):
        b, s = c // 4, c % 4
        out[b, s * NT:(s + 1) * NT] = res.results[c]["out"]
    return out
```

```python
import os
import numpy as np
import concourse.bass as bass
import concourse.mybir as mybir
from concourse.bass_utils import run_bass_kernel_spmd

F32 = mybir.dt.float32
BF16 = mybir.dt.bfloat16
AF = mybir.ActivationFunctionType
ALU = mybir.AluOpType

NCORES = 8
D = 1024
SEQ = 16384
NT = 4096
T = 512
NMT = NT // T
L = 128
NJ = T // L
NCHK = NT // L
H = 4
DK = 256
DV = 512
EPS = 1e-6
GW = 4112

CH_CONV = 0
CH_Q = 8
CH_GATE = 10
CH_GA = 18
CH_WA = 20
CH_GB = 22
CH_WB = 24
CH_WO = 28
CH_WPG = 30
CH_WP = 32
NWCH = 33

C_ID = 0
C_U = 128
C_ONES = 256
C_MASK = 384
C_GMIX = 896
C_GPLE = 1920
C_GFIN = 2944
C_GHEAD = 3968
C_CW = 3984
C_CB = 4008
C_BG = 4016
C_BTW = 4024
C_SEL = 4040
NCST = 4048


class Ev:
    __slots__ = ("sem", "val")

    def __init__(self, sem, val):
        self.sem = sem
        self.val = val


class Buf:
    def __init__(self, name):
        self.name = name
        self.w = None
        self.r = {}
        self.dsem = None
        self.dcount = 0


class Sched:
    ENGS = ("pe", "act", "dve", "pool", "sp")

    def __init__(self, nc, sems, dma_sems):
        self.nc = nc
        self.sem = sems
        self.dma_sems = list(dma_sems)
        self.q = {e: [] for e in self.ENGS}
        self.cnt = {e: 0 for e in self.ENGS}
        self.waited = {e: {} for e in self.ENGS}
        self.dbufs = []
        self.extra_evs = []
        self.stopped = False

    def stop_if(self, tag):
        if os.environ.get("KSTOP", "") == tag:
            self.stopped = True

    def _collect(self, eng, reads, writes, extra):
        need = {}

        def add(ev):
            if ev is None:
                return
            k = id(ev.sem)
            if k not in need or need[k].val < ev.val:
                need[k] = ev

        for b in reads:
            add(b.w)
        for b in writes:
            add(b.w)
            for ev in b.r.values():
                add(ev)
        for ev in extra:
            add(ev)
        out = []
        wd = self.waited[eng]
        for k, ev in need.items():
            if wd.get(k, 0) >= ev.val:
                continue
            wd[k] = ev.val
            out.append(ev)
        return out

    def _commit(self, ev, reads, writes):
        for b in reads:
            k = id(ev.sem)
            if k not in b.r or b.r[k].val < ev.val:
                b.r[k] = ev
        for b in writes:
            b.w = ev
            b.r = {}

    def op(self, eng, fn, reads=(), writes=(), extra=()):
        if self.stopped:
            return Ev(None, 0)
        waits = self._collect(eng, reads, writes, extra)
        self.cnt[eng] += 1
        ev = Ev(self.sem[eng], self.cnt[eng])
        self.q[eng].append((waits, fn, (self.sem[eng], 1)))
        self._commit(ev, reads, writes)
        return ev

    def dma(self, eng, fn, owner, reads=(), writes=(), ndma=1, extra=()):
        if self.stopped:
            return Ev(None, 0)
        waits = self._collect(eng, reads, writes, extra)
        if owner.dsem is None:
            owner.dsem = self.dma_sems.pop()
            self.dbufs.append(owner)
        owner.dcount += 16 * ndma
        ev = Ev(owner.dsem, owner.dcount)
        self.q[eng].append((waits, fn, ("dma", owner.dsem)))
        self._commit(ev, reads, writes)
        return ev

    def barrier(self):
        if self.stopped:
            return
        evs = [Ev(self.sem[f], self.cnt[f]) for f in self.ENGS if self.cnt[f] > 0]
        evs += [Ev(b.dsem, b.dcount) for b in self.dbufs]
        evs += list(self.extra_evs)
        for e in self.ENGS:
            self.wait_only(e, evs)

    def wait_only(self, eng, evs):
        if self.stopped:
            return
        waits = self._collect(eng, (), (), evs)
        self.q[eng].append((waits, None, None))

    def emit(self, eng, handle):
        for waits, fn, inc in self.q[eng]:
            for ev in waits:
                handle.wait_ge(ev.sem, ev.val)
            if fn is None:
                continue
            if inc[0] == "dma":
                dsem = inc[1]
                fn(handle, lambda ins, dsem=dsem: ins.then_inc(dsem, 16))
            else:
                ins = fn(handle)
                if inc[1] is None:
                    ins.then_inc(inc[0])
                else:
                    ins.then_inc(inc[0], inc[1])


def build_nc():
    nc = bass.Bass("TRN2", target_bir_lowering=False)
    x_seg = nc.dram_tensor("x_seg", [NT, D], F32, kind="ExternalInput").ap()
    x_prev = nc.dram_tensor("x_prev", [L, D], F32, kind="ExternalInput").ap()
    p_seg = nc.dram_tensor("p_seg", [NT, 256], F32, kind="ExternalInput").ap()
    wq32 = nc.dram_tensor("wq32", [NWCH, 128, 4096], F32, kind="ExternalInput").ap()
    wkv32 = nc.dram_tensor("wkv32", [128, 8 * 3072], F32, kind="ExternalInput").ap()
    wif32 = nc.dram_tensor("wif32", [128, 64], F32, kind="ExternalInput").ap()
    cst_d = nc.dram_tensor("cst", [128, NCST], F32, kind="ExternalInput").ap()
    out_d = nc.dram_tensor("out", [NT, D], F32, kind="ExternalOutput").ap()
    wq16 = nc.dram_tensor("wq16", [NWCH, 128, 4096], BF16).ap()
    kws = nc.dram_tensor("kws", [NT, 1024], BF16).ap()
    hns = nc.dram_tensor("hns", [NMT, 128, 8 * T], BF16).ap()
    vs = nc.dram_tensor("vs", [NT, 2048], BF16).ap()
    GWS = (2048, 2048, 16)
    GOFF = (0, 2048, 4096)
    gin_ts = [nc.dram_tensor("gin%d" % i, [128, w], F32) for i, w in enumerate(GWS)]
    gout_ts = [nc.dram_tensor("gout%d" % i, [4 * 128, w], F32) for i, w in enumerate(GWS)]

    import contextlib
    es = contextlib.ExitStack()
    with es:
        def sb(name, shape, dt):
            return es.enter_context(nc.sbuf_tensor(name, shape, dt))

        def sem(name):
            return es.enter_context(nc.semaphore(name))

        DBG = os.environ.get("KDBG", "") != ""
        dbg_t = {}
        dbgB = Buf("dbg")

        def dbg(name, ap, shape, dt, reads):
            if not DBG or name in dbg_t:
                return
            t_ = nc.dram_tensor("dbg_" + name, list(shape), dt, kind="ExternalOutput")
            dbg_t[name] = t_
            S.dma("pool", lambda e, inc: inc(e.dma_start(out=t_.ap(), in_=ap)), dbgB, reads=reads)

        sems = {e: sem("s_" + e) for e in Sched.ENGS}
        dma_sems = [sem("d%d" % i) for i in range(60)]
        cc_sem = sem("cc")
        S = Sched(nc, sems, dma_sems)

        cst = sb("cst_sb", [128, NCST], F32)
        cstB = Buf("cst")
        ident = sb("ident", [128, 128], BF16)
        mask4 = sb("mask4", [128, 512], BF16)
        ones_bf = sb("ones_bf", [128, 1], BF16)
        constB = Buf("constbf")
        Cst = sb("Cst", [128, 4096], F32)
        CB = [[Buf("C%d_%d" % (h, cc)) for cc in range(2)] for h in range(H)]
        CBall = [b for hb in CB for b in hb]
        nst = sb("nst", [128, 8], F32)
        nB = Buf("n")
        EQ = sb("EQ", [128, NCHK, 4], F32)
        DEC = sb("DEC", [128, NCHK, 4], F32)
        EQ2 = sb("EQ2", [128, NCHK, 4], F32)
        RSTD1 = sb("RSTD1", [128, NCHK], F32)
        gateB = [Buf("gates%d" % m) for m in range(NMT)]
        rstdB = [Buf("rstd%d" % m) for m in range(NMT)]
        LD = sb("LD", [128, 4], F32)
        LDB = Buf("LD")
        junk = sb("junk", [128, 1024], BF16)
        junkB = {"act": Buf("junk_act")}
        xbuf = [sb("xbuf%d" % i, [128, NJ, 1024], F32) for i in range(2)]
        xB = [[Buf("x%d_%d" % (i, j)) for j in range(NJ)] for i in range(2)]
        hn_bf = sb("hn_bf", [128, 1024], BF16)
        hnB = Buf("hn")
        hnT = sb("hnT", [128, 8, T], BF16)
        hnTB = Buf("hnT")
        hnsB = [Buf("hns%d" % i) for i in range(NMT)]
        hT_h = sb("hT_h", [128, 8, 2], BF16)
        hThB = Buf("hT_h")

        banks = [es.enter_context(nc.psum_tensor("bank%d" % i, [128, 512], F32)) for i in range(8)]
        bankB = [Buf("bank%d" % i) for i in range(8)]
        bank_rr = [0]

        def next_bank():
            i = bank_rr[0] % 8
            bank_rr[0] += 1
            return banks[i], bankB[i]

        def v3(ap, a):
            return ap.rearrange("p (a b) -> p a b", a=a)

        S.dma("sp", lambda e, inc: inc(e.dma_start(out=cst[:], in_=cst_d[:, :])), cstB, writes=[cstB])
        S.op("dve", lambda e: e.tensor_copy(out=ident[:], in_=cst[:, C_ID:C_ID + 128]), reads=[cstB], writes=[constB])
        S.op("dve", lambda e: e.tensor_copy(out=mask4[:], in_=cst[:, C_MASK:C_MASK + 512]), reads=[cstB], writes=[constB])
        S.op("dve", lambda e: e.tensor_copy(out=ones_bf[:], in_=cst[:, C_ONES:C_ONES + 1]), reads=[cstB], writes=[constB])
        S.op("dve", lambda e: e.memset(Cst[:], 0.0), writes=CBall)
        S.op("dve", lambda e: e.memset(nst[:], 0.0), writes=[nB])
        S.op("dve", lambda e: e.memset(LD[:], 0.0), writes=[LDB])

        wq16B = [Buf("wq16_%d" % i) for i in range(NWCH)]

        def ld_x(m):
            xb_, xbB_ = xbuf[m % 2], xB[m % 2]
            S.dma("sp", lambda e, inc: inc(e.dma_start(
                out=xb_[:], in_=x_seg[m * T:(m + 1) * T, :].rearrange("(j p) d -> p j d", p=128))), xbB_[0], writes=xbB_)

        p1 = contextlib.ExitStack()
        with p1:
            def sb1(name, shape, dt):
                return p1.enter_context(nc.sbuf_tensor(name, shape, dt))

            wkv = sb1("wkv", [128, 8, 3072], BF16)
            wkvB = Buf("wkv")
            wif = sb1("wif", [128, 8, 8], BF16)
            wifB = Buf("wif")
            S.dma("pool", lambda e, inc: inc(e.dma_start(out=wif[:], in_=wif32.rearrange("p (k c) -> p k c", k=8))), wifB, writes=[wifB])
            ld_x(0)
            x1flat = xbuf[1][:].rearrange("p j d -> p (j d)")
            wkvflat = wkv[:].rearrange("p k c -> p (k c)")
            stgB = [Buf("stg_a"), Buf("stg_b")]
            for q in range(12):
                hb_ = q % 2
                S.dma("sp", lambda e, inc, q=q, hb_=hb_: inc(e.dma_start(out=x1flat[:, hb_ * 2048:(hb_ + 1) * 2048],
                                                                       in_=wkv32[:, q * 2048:(q + 1) * 2048])),
                      stgB[hb_], writes=[stgB[hb_]])
                if q % 2 == 0:
                    S.op("act", lambda e, q=q, hb_=hb_: e.activation(out=wkvflat[:, q * 2048:(q + 1) * 2048],
                                                                    in_=x1flat[:, hb_ * 2048:(hb_ + 1) * 2048], func=AF.Copy),
                         reads=[stgB[hb_]], writes=[wkvB])
                else:
                    S.op("dve", lambda e, q=q, hb_=hb_: e.tensor_copy(out=wkvflat[:, q * 2048:(q + 1) * 2048],
                                                                     in_=x1flat[:, hb_ * 2048:(hb_ + 1) * 2048]),
                         reads=[stgB[hb_]], writes=[wkvB])
            for b_ in xB[1]:
                b_.r = dict(stgB[0].r)
                b_.r.update(stgB[1].r)
            st32 = [sb1("st32_%d" % i, [128, 1024], F32) for i in range(2)]
            st32B = [Buf("st32_%d" % i) for i in range(2)]
            st16 = [sb1("st16_%d" % i, [128, 1024], BF16) for i in range(2)]
            st16B = [Buf("st16_%d" % i) for i in range(2)]
            NU = NWCH * 4
            cv_state = {"u": 0}

            def cv_load(u):
                S.dma("pool", lambda e, inc, u=u: inc(e.dma_start(out=st32[u % 2][:], in_=wq32[u // 4][:, (u % 4) * 1024:(u % 4 + 1) * 1024])),
                      st32B[u % 2], writes=[st32B[u % 2]])
            cv_load(0)
            cv_load(1)

            def cv_emit_until(n):
                while cv_state["u"] < min(n, NU):
                    u = cv_state["u"]
                    S.op("act", lambda e, u=u: e.activation(out=st16[u % 2][:], in_=st32[u % 2][:], func=AF.Copy),
                         reads=[st32B[u % 2]], writes=[st16B[u % 2]])
                    S.dma("pool", lambda e, inc, u=u: inc(e.dma_start(out=wq16[u // 4][:, (u % 4) * 1024:(u % 4 + 1) * 1024], in_=st16[u % 2][:])),
                          st16B[u % 2], reads=[st16B[u % 2]], writes=[wq16B[u // 4]])
                    if u + 2 < NU:
                        cv_load(u + 2)
                    cv_state["u"] += 1

            S.stop_if("setup")
            Gs = sb1("Gs", [128, NJ, 8], F32)
            GsB = Buf("Gs")
            sp_t = sb1("sp_t", [128, NJ, 4], F32)
            spB = Buf("sp")
            tot_s = sb1("tot_s", [128, NJ, 4], F32)
            totB = Buf("tot")
            arg1 = sb1("arg1", [128, NJ, 4], F32)
            arg1B = Buf("arg1")
            wsc = sb1("wsc", [128, NJ, 4], F32)
            wB = Buf("w")
            ss1 = sb1("ss1", [128, NJ], F32)
            ss1B = Buf("ss1")
            lnt = sb1("lnt", [128, NJ], F32)
            lntB = Buf("lnt")
            kw_m = [sb1("kw_m%d" % i, [128, NJ, 1024], BF16) for i in range(2)]
            kwmB = [[Buf("kwm%d_%d" % (i, j)) for j in range(NJ)] for i in range(2)]
            v_m = [sb1("v_m%d" % i, [128, NJ, 2048], BF16) for i in range(2)]
            vmB = [[Buf("vm%d_%d" % (i, j)) for j in range(NJ)] for i in range(2)]
            ntmp = sb1("ntmp", [128, 8], F32)
            ntmpB = Buf("ntmp")
            dec8 = sb1("dec8", [128, 8], F32)
            dec8B = Buf("dec8")

            hnT2 = sb1("hnT2", [128, 8, T], BF16)
            hnTs = [hnT, hnT2]
            hnTBs = [hnTB, Buf("hnT2")]

            def p1_stats(m):
                xb, xbB = xbuf[m % 2], xB[m % 2]
                gB = gateB[m]
                for j in range(NJ):
                    S.op("act", lambda e, xb=xb, j=j: e.activation(out=junk[:], in_=xb[:, j, :], func=AF.Square,
                                                                  accum_out=ss1[:, j:j + 1]),
                         reads=[xbB[j]], writes=[ss1B])
                S.op("act", lambda e: e.activation(out=lnt[:], in_=ss1[:], func=AF.Ln, bias=EPS, scale=1.0 / D),
                     reads=[ss1B], writes=[lntB])
                S.op("act", lambda e, m=m: e.activation(out=RSTD1[:, m * NJ:(m + 1) * NJ], in_=lnt[:], func=AF.Exp, scale=-0.5),
                     reads=[lntB], writes=[rstdB[m]])

            def p1_hn_tile(m, j, part=None):
                xb, xbB = xbuf[m % 2], xB[m % 2]
                hT, hTB = hnTs[m % 2], hnTBs[m % 2]
                if part in (None, 0):
                  S.op("dve", lambda e, xb=xb, j=j, m=m: e.scalar_tensor_tensor(
                    out=hn_bf[:], in0=xb[:, j, :], scalar=RSTD1[:, m * NJ + j:m * NJ + j + 1],
                    in1=cst[:, C_GMIX:C_GMIX + D], op0=ALU.mult, op1=ALU.mult),
                    reads=[xbB[j], rstdB[m], cstB], writes=[hnB])
                if part == 0:
                    return
                for half in range(2):
                    bk, bB = next_bank()

                    def tr(e, bk=bk, half=half):
                        ins = None
                        for i in range(4):
                            kc = half * 4 + i
                            ins = e.matmul(bk[:, i * 128:(i + 1) * 128], lhsT=hn_bf[:, kc * 128:(kc + 1) * 128],
                                           rhs=ident[:], start=True, stop=True)
                        return ins
                    S.op("pe", tr, reads=[hnB, constB], writes=[bB])
                    S.op("act", lambda e, bk=bk, half=half, j=j, hT=hT: e.activation(
                        out=hT[:, half * 4:half * 4 + 4, j * 128:(j + 1) * 128], in_=v3(bk[:, :], 4), func=AF.Copy),
                        reads=[bB], writes=[hTB])

            def p1_store_hns(m):
                hT, hTB = hnTs[m % 2], hnTBs[m % 2]
                S.dma("sp", lambda e, inc, m=m, hT=hT: inc(e.dma_start(out=hns[m], in_=hT[:].rearrange("p k t -> p (k t)"))),
                      hTB, reads=[hTB], writes=[hnsB[m]])

            def p1_gates(m):
                hT, hTB = hnTs[m % 2], hnTBs[m % 2]
                gB = gateB[m]
                for j in range(NJ):
                    bk, bB = next_bank()

                    def gj(e, bk=bk, j=j, hT=hT):
                        ins = None
                        for kc in range(8):
                            ins = e.matmul(bk[:, 0:8], lhsT=hT[:, kc, j * 128:(j + 1) * 128], rhs=wif[:, kc, :],
                                           start=(kc == 0), stop=(kc == 7))
                        return ins
                    S.op("pe", gj, reads=[hTB, wifB], writes=[bB])
                    S.op("dve", lambda e, bk=bk, j=j: e.tensor_tensor(out=Gs[:, j, :], in0=bk[:, 0:8],
                                                                      in1=cst[:, C_BG:C_BG + 8], op=ALU.add),
                         reads=[bB, cstB], writes=[GsB])
                S.op("act", lambda e: e.activation(out=sp_t[:], in_=Gs[:, :, 4:8], func=AF.Exp, scale=-1.0),
                     reads=[GsB], writes=[spB])
                S.op("act", lambda e: e.activation(out=sp_t[:], in_=sp_t[:], func=AF.Ln, bias=1.0, scale=1.0),
                     reads=[spB], writes=[spB])

            def p1_gates_b(m):
                gB = gateB[m]
                bk_cs, bB_cs = next_bank()

                def csj(e, bk=bk_cs):
                    ins = None
                    for j in range(NJ):
                        ins = e.matmul(bk[:, j * 4:(j + 1) * 4], lhsT=cst[:, C_U:C_U + 128], rhs=sp_t[:, j, :],
                                       start=True, stop=True)
                    for j in range(NJ):
                        ins = e.matmul(bk[:, 16 + j * 4:16 + (j + 1) * 4], lhsT=cst[:, C_ONES:C_ONES + 128],
                                       rhs=sp_t[:, j, :], start=True, stop=True)
                    return ins
                S.op("pe", csj, reads=[spB, cstB], writes=[bB_cs])
                S.op("act", lambda e, bk=bk_cs: e.activation(out=tot_s[:], in_=v3(bk[:, 16:32], NJ), func=AF.Copy),
                     reads=[bB_cs], writes=[totB])
                S.op("dve", lambda e, bk=bk_cs: e.tensor_tensor(out=arg1[:], in0=v3(bk[:, 0:16], NJ), in1=tot_s[:],
                                                                op=ALU.subtract),
                     reads=[bB_cs, totB], writes=[arg1B])
                S.op("act", lambda e, m=m: e.activation(out=EQ[:, m * NJ:(m + 1) * NJ, :], in_=arg1[:], func=AF.Exp),
                     reads=[arg1B], writes=[gB])
                S.op("act", lambda e, m=m: e.activation(out=EQ2[:, m * NJ:(m + 1) * NJ, :], in_=arg1[:], func=AF.Exp, scale=2.0),
                     reads=[arg1B], writes=[gB])
                S.op("dve", lambda e: e.tensor_tensor(out=arg1[:], in0=arg1[:], in1=Gs[:, :, 0:4], op=ALU.add),
                     reads=[arg1B, GsB], writes=[arg1B])
                S.op("act", lambda e: e.activation(out=wsc[:], in_=arg1[:], func=AF.Exp), reads=[arg1B], writes=[wB])
                S.op("act", lambda e, m=m: e.activation(out=DEC[:, m * NJ:(m + 1) * NJ, :], in_=tot_s[:], func=AF.Exp,
                                                       scale=-1.0), reads=[totB], writes=[gB])
                for j in range(NJ):
                    S.op("dve", lambda e, j=j: e.tensor_tensor(out=LD[:], in0=LD[:], in1=tot_s[:, j, :], op=ALU.add),
                         reads=[totB, LDB], writes=[LDB])

            def p1_proj_job(m, j, cb):
                hT, hTB = hnTs[m % 2], hnTBs[m % 2]
                kwt, kwtB = kw_m[m % 2], kwmB[m % 2]
                vt, vtB = v_m[m % 2], vmB[m % 2]
                bk, bB = next_bank()

                def pj(e, bk=bk, j=j, cb=cb, hT=hT):
                    ins = None
                    for kc in range(8):
                        ins = e.matmul(bk[:, :], lhsT=hT[:, kc, j * 128:(j + 1) * 128],
                                       rhs=wkv[:, kc, cb * 512:(cb + 1) * 512], start=(kc == 0), stop=(kc == 7))
                    return ins
                S.op("pe", pj, reads=[hTB, wkvB], writes=[bB])
                if cb < 2:
                    for hh in range(2):
                        h = cb * 2 + hh
                        S.op("act", lambda e, bk=bk, j=j, h=h, hh=hh, kwt=kwt: e.activation(
                            out=kwt[:, j, h * 256:(h + 1) * 256], in_=bk[:, hh * 256:(hh + 1) * 256], func=AF.Copy,
                            scale=wsc[:, j, h:h + 1]), reads=[bB, wB], writes=[kwtB[j]])
                else:
                    c0 = (cb - 2) * 512
                    S.op("dve", lambda e, bk=bk, j=j, c0=c0, vt=vt: e.tensor_copy(out=vt[:, j, c0:c0 + 512], in_=bk[:, :]),
                         reads=[bB], writes=[vtB[j]])

            def p1_state_items(m, j):
                kwt, kwtB = kw_m[m % 2], kwmB[m % 2]
                vt, vtB = v_m[m % 2], vmB[m % 2]
                gB = gateB[m]
                ch = m * NJ + j
                items = []

                def it_n():
                    bkn, bBn = next_bank()

                    def nj(e, bk=bkn):
                        ins = None
                        for hc in range(8):
                            ins = e.matmul(bk[:, hc:hc + 1], lhsT=kwt[:, j, hc * 128:(hc + 1) * 128], rhs=ones_bf[:],
                                           start=True, stop=True)
                        return ins
                    S.op("pe", nj, reads=[kwtB[j], constB], writes=[bBn])
                    for h in range(H):
                        S.op("dve", lambda e, h=h: e.tensor_scalar(
                            out=nst[:, h * 2:h * 2 + 2], in0=nst[:, h * 2:h * 2 + 2], scalar1=DEC[:, ch, h:h + 1],
                            scalar2=None, op0=ALU.mult), reads=[nB, gB], writes=[nB])
                    S.op("dve", lambda e, bk=bkn: e.tensor_tensor(out=nst[:], in0=nst[:], in1=bk[:, 0:8], op=ALU.add),
                         reads=[nB, bBn], writes=[nB])
                items.append(it_n)
                for h in range(H):
                    for cc in range(2):
                        def it_s(h=h, cc=cc):
                            bk, bB = next_bank()
                            S.op("pe", lambda e, bk=bk: e.matmul(
                                bk[:, :], lhsT=kwt[:, j, h * 256 + cc * 128:h * 256 + (cc + 1) * 128],
                                rhs=vt[:, j, h * 512:(h + 1) * 512], start=True, stop=True),
                                reads=[kwtB[j], vtB[j]], writes=[bB])
                            S.op("dve", lambda e, bk=bk: e.scalar_tensor_tensor(
                                out=Cst[:, h * 1024 + cc * 512:h * 1024 + (cc + 1) * 512], in0=Cst[:, h * 1024 + cc * 512:h * 1024 + (cc + 1) * 512],
                                scalar=DEC[:, ch, h:h + 1], in1=bk[:, :], op0=ALU.mult, op1=ALU.add),
                                reads=[bB, gB, CB[h][cc]], writes=[CB[h][cc]])
                        items.append(it_s)
                return items

            p1_stats(0)
            for j in range(NJ):
                p1_hn_tile(0, j)
            p1_store_hns(0)
            ld_x(1)
            p1_stats(1)
            slot = 0
            for m in range(NMT):
                if m == 1:
                    S.stop_if("p1m0")
                if m + 2 < NMT:
                    ld_x(m + 2)
                p1_gates(m)
                items = []
                for j in range(NJ):
                    its = []
                    if m >= 1:
                        its += p1_state_items(m - 1, j)
                    if m + 1 < NMT:
                        its.insert(0, lambda m=m, j=j: p1_hn_tile(m + 1, j, part=0))
                        its.append(lambda m=m, j=j: p1_hn_tile(m + 1, j, part=1))
                    items += its
                porder = ([(0, cb) for cb in (2, 3, 4, 5)] + [(1, cb) for cb in (2, 3, 4, 5)] + ["GB"]
                          + [(0, 0), (0, 1), (1, 0), (1, 1)]
                          + [(2, cb) for cb in (2, 3, 4, 5, 0, 1)] + [(3, cb) for cb in (2, 3, 4, 5, 0, 1)])
                k = 0
                idx = 0
                for po in porder:
                    if po == "GB":
                        p1_gates_b(m)
                        continue
                    p1_proj_job(m, po[0], po[1])
                    idx += 1
                    tgt = (len(items) * idx + 23) // 24
                    while k < tgt:
                        items[k]()
                        k += 1
                    slot += 1
                    cv_emit_until((NU * slot) // 192)
                if m + 2 < NMT:
                    p1_stats(m + 2)
                if m == NMT - 1:
                    xhb, xhB = xbuf[1], xB[1][0]
                    S.dma("sp", lambda e, inc: inc(e.dma_start(out=xhb[:, 0, :], in_=x_prev[:, :])), xhB, writes=[xhB])
                    S.op("act", lambda e: e.activation(out=junk[:], in_=xhb[:, 0, :], func=AF.Square, accum_out=ss1[:, 0:1]),
                         reads=[xhB], writes=[ss1B])
                    S.op("act", lambda e: e.activation(out=lnt[:, 0:1], in_=ss1[:, 0:1], func=AF.Ln, bias=EPS, scale=1.0 / D),
                         reads=[ss1B], writes=[lntB])
                    S.op("act", lambda e: e.activation(out=lnt[:, 0:1], in_=lnt[:, 0:1], func=AF.Exp, scale=-0.5),
                         reads=[lntB], writes=[lntB])
                    S.op("dve", lambda e: e.scalar_tensor_tensor(out=hn_bf[:], in0=xhb[:, 0, :], scalar=lnt[:, 0:1],
                                                                 in1=cst[:, C_GMIX:C_GMIX + D], op0=ALU.mult, op1=ALU.mult),
                         reads=[xhB, lntB, cstB], writes=[hnB])
                    for half in range(2):
                        bk, bB = next_bank()

                        def trh(e, bk=bk, half=half):
                            ins = None
                            for i in range(4):
                                kc = half * 4 + i
                                ins = e.matmul(bk[:, i * 128:(i + 1) * 128], lhsT=hn_bf[:, kc * 128:(kc + 1) * 128],
                                               rhs=ident[:], start=True, stop=True)
                            return ins
                        S.op("pe", trh, reads=[hnB, constB], writes=[bB])
                        S.op("act", lambda e, bk=bk, half=half: e.activation(
                            out=hnT[:, half * 4:half * 4 + 4, 0:128], in_=v3(bk[:, :], 4), func=AF.Copy),
                            reads=[bB], writes=[hnTB])
                    S.op("dve", lambda e: e.tensor_copy(out=hT_h[:], in_=hnT[:, :, 126:128]), reads=[hnTB], writes=[hThB])
                    ld_x(0)
                    S.dma("sp", lambda e, inc: inc(e.dma_start(out=hnT[:].rearrange("p k t -> p (k t)"), in_=hns[0])),
                          hnTB, reads=[hnsB[0]], writes=[hnTB])
                kwt, kwtB = kw_m[m % 2], kwmB[m % 2]
                vt, vtB = v_m[m % 2], vmB[m % 2]
                if m == 0:
                    dbg("kw0", kwt[:], [128, NJ, 1024], BF16, kwtB)
                    dbg("v0", vt[:], [128, NJ, 2048], BF16, vtB)
                    dbg("EQ", EQ[:], [128, NCHK, 4], F32, [gateB[0]])
                    dbg("DEC", DEC[:], [128, NCHK, 4], F32, [gateB[0]])
                    dbg("Gs", Gs[:], [128, NJ, 8], F32, [GsB])
                S.dma("sp", lambda e, inc, kwt=kwt, m=m: inc(e.dma_start(
                    out=kws[m * T:(m + 1) * T, :].rearrange("(j p) d -> p j d", p=128), in_=kwt[:])), kwtB[0], reads=kwtB)
                S.dma("sp", lambda e, inc, vt=vt, m=m: inc(e.dma_start(
                    out=vs[m * T:(m + 1) * T, :].rearrange("(j p) d -> p j d", p=128), in_=vt[:])), vtB[0], reads=vtB)
                if m + 1 < NMT:
                    p1_store_hns(m + 1)
            for j in range(NJ):
                for it in p1_state_items(NMT - 1, j):
                    it()
            cv_emit_until(NU)

            S.stop_if("p1")
            ginB = Buf("gin")

            def st_gin(e, inc):
                inc(e.dma_start(out=gin_ts[0].ap()[:, :], in_=Cst[:, 0:2048]))
                inc(e.dma_start(out=gin_ts[1].ap()[:, :], in_=Cst[:, 2048:4096]))
                inc(e.dma_start(out=gin_ts[2].ap()[:, 0:8], in_=nst[:]))
                inc(e.dma_start(out=gin_ts[2].ap()[:, 8:12], in_=LD[:]))
                inc(e.dma_start(out=gin_ts[2].ap()[:, 12:16], in_=LD[:]))
            S.dma("pool", st_gin, ginB, reads=CBall + [nB, LDB], writes=[ginB], ndma=5)
            goutB = Buf("gout")
            waits = S._collect("pool", [ginB], [goutB], ())
            if not S.stopped:
                for i in range(3):
                    S.q["pool"].append((waits if i == 0 else [], lambda e, i=i: e.collective_compute(
                        "AllGather", ALU.bypass, replica_groups=[[0, 1, 2, 3], [4, 5, 6, 7]],
                        ins=[gin_ts[i].ap().opt()], outs=[gout_ts[i].ap().opt()]), (cc_sem, None)))
            cc_ev = Ev(cc_sem, 3)
            goutB.w = cc_ev
            S.barrier()

        S.stop_if("comb")
        p2 = contextlib.ExitStack()
        with p2:
            def sb2(name, shape, dt):
                return p2.enter_context(nc.sbuf_tensor(name, shape, dt))
            NWB = 3
            wring = [sb2("wring%d" % i, [128, 4096], BF16) for i in range(NWB)]
            wringB = [Buf("wring%d" % i) for i in range(NWB)]
            wr_state = {"n": 0}
            order = ([CH_Q, CH_Q + 1] + list(range(CH_GATE, CH_GATE + 8)) + list(range(CH_CONV, CH_CONV + 8))
                     + [CH_GA, CH_WA, CH_GA + 1, CH_WA + 1]
                     + [CH_GB, CH_WB, CH_WB + 1, CH_GB + 1, CH_WB + 2, CH_WB + 3]
                     + [CH_WO, CH_WO + 1, CH_WPG, CH_WPG + 1, CH_WP])
            stream = [c for _ in range(NMT) for c in order]
            pre = {"issued": 0}

            def issue_w():
                k = pre["issued"]
                if k >= len(stream):
                    return
                ci = stream[k]
                slot = k % NWB
                S.dma("sp", lambda e, inc, ci=ci, slot=slot: inc(e.dma_start(out=wring[slot][:], in_=wq16[ci])),
                      wringB[slot], reads=[wq16B[ci]], writes=[wringB[slot]])
                pre["issued"] += 1

            def get_w(expect):
                k = wr_state["n"]
                assert stream[k] == expect, (stream[k], expect)
                while pre["issued"] < k + 1:
                    issue_w()
                wr_state["n"] += 1
                return wring[k % NWB], wringB[k % NWB]

            def after_w(hold=0):
                k = wr_state["n"]
                while pre["issued"] < min(len(stream), k + NWB - 1 - hold):
                    issue_w()

            Cgp = [sb2("Cgp%d" % i, [128, 2048], F32) for i in range(1)]
            CgpB = [Buf("Cgp%d" % i) for i in range(1)]
            Cgs = sb2("Cgs", [128, 4, 16], F32)
            CgsB = Buf("Cgs")
            Eacc = sb2("Eacc", [128, 4, 4], F32)
            EB = Buf("E")
            coef = sb2("coef", [128, 4, 4], F32)
            coefB = Buf("coef")

            def emit_combine():
                def ld_s(e, inc):
                    for r in range(4):
                        inc(e.dma_start(out=Cgs[:, r, :], in_=gout_ts[2].ap()[r * 128:(r + 1) * 128, :]))
                S.dma("sp", ld_s, CgsB, reads=[goutB], writes=[CgsB], ndma=4)
                S.op("dve", lambda e: e.memset(Eacc[:], 0.0), writes=[EB])
                for r in range(4):
                    for i in range(4):
                        S.op("dve", lambda e, r=r, i=i: e.scalar_tensor_tensor(
                            out=Eacc[:, r, :], in0=Cgs[:, i, 8:12], scalar=cst[:, C_BTW + i * 4 + r:C_BTW + i * 4 + r + 1],
                            in1=Eacc[:, r, :], op0=ALU.mult, op1=ALU.add), reads=[CgsB, cstB, EB], writes=[EB])
                S.op("act", lambda e: e.activation(out=coef[:], in_=Eacc[:], func=AF.Exp, scale=-1.0), reads=[EB], writes=[coefB])
                for r in range(4):
                    S.op("dve", lambda e, r=r: e.tensor_scalar(out=coef[:, r, :], in0=coef[:, r, :],
                                                               scalar1=cst[:, C_SEL + r:C_SEL + r + 1], scalar2=None, op0=ALU.mult),
                         reads=[coefB, cstB], writes=[coefB])
                for h in range(H):
                    S.op("dve", lambda e, h=h: e.tensor_scalar(out=nst[:, h * 2:h * 2 + 2], in0=Cgs[:, 0, h * 2:h * 2 + 2],
                                                               scalar1=coef[:, 0, h:h + 1], scalar2=None, op0=ALU.mult),
                         reads=[CgsB, coefB], writes=[nB])
                    for r in range(1, 4):
                        S.op("dve", lambda e, h=h, r=r: e.scalar_tensor_tensor(
                            out=nst[:, h * 2:h * 2 + 2], in0=Cgs[:, r, h * 2:h * 2 + 2], scalar=coef[:, r, h:h + 1],
                            in1=nst[:, h * 2:h * 2 + 2], op0=ALU.mult, op1=ALU.add), reads=[CgsB, coefB, nB], writes=[nB])
                k = 0
                for half in range(2):
                    for r in range(4):
                        pb, pbB = Cgp[0], CgpB[0]
                        k += 1
                        S.dma("sp", lambda e, inc, pb=pb, half=half, r=r: inc(e.dma_start(
                            out=pb[:], in_=gout_ts[half].ap()[r * 128:(r + 1) * 128, :])), pbB, reads=[goutB], writes=[pbB])
                        for hh in range(2):
                            h = half * 2 + hh
                            if r == 0:
                                S.op("dve", lambda e, h=h, hh=hh, pb=pb: e.tensor_scalar(
                                    out=Cst[:, h * 1024:(h + 1) * 1024], in0=pb[:, hh * 1024:(hh + 1) * 1024],
                                    scalar1=coef[:, 0, h:h + 1], scalar2=None, op0=ALU.mult),
                                    reads=[pbB, coefB], writes=CB[h])
                            else:
                                S.op("dve", lambda e, h=h, hh=hh, pb=pb, r=r: e.scalar_tensor_tensor(
                                    out=Cst[:, h * 1024:(h + 1) * 1024], in0=pb[:, hh * 1024:(hh + 1) * 1024],
                                    scalar=coef[:, r, h:h + 1], in1=Cst[:, h * 1024:(h + 1) * 1024], op0=ALU.mult, op1=ALU.add),
                                    reads=[pbB, coefB] + CB[h], writes=CB[h])

            ya_inT = sb2("ya_inT", [128, 8, T], BF16)
            yainB = Buf("ya_in")
            mrg = sb2("mrg", [128, 8, T], BF16)
            mrgB = [Buf("mrg%d" % e) for e in range(8)]
            qT = sb2("qT", [128, 8, T], BF16)
            qTB = Buf("qT")
            gateT = sb2("gateT", [128, 16, T], BF16)
            gtB = [Buf("gateT%d" % d) for d in range(16)]
            hn2b = [hn_bf, sb2("hn_bf2", [128, 1024], BF16)]
            hn2bB = [hnB, Buf("hn_bf2")]
            hn2T = sb2("hn2T", [128, 8, T], BF16)
            hn2TB = Buf("hn2T")
            u_t = sb2("u_t", [128, T + 2], F32)
            uB = Buf("u")
            uh = sb2("uh", [128, 8, 2], F32)
            uhB = Buf("uh")
            t1 = [sb2("t1_%d" % i, [128, T], F32) for i in range(4)]
            t1B = [Buf("t1_%d" % i) for i in range(4)]
            kw_c = [sb2("kw_c%d" % i, [128, 1024], BF16) for i in range(2)]
            kwcB = [Buf("kwc%d" % i) for i in range(2)]
            v_c = [sb2("v_c%d" % i, [128, 2048], BF16) for i in range(2)]
            vcB = [Buf("vc%d" % i) for i in range(2)]
            kwT = sb2("kwT", [128, 8, 128], BF16)
            kwTB = Buf("kwT")
            scT = sb2("scT", [128, 512], BF16)
            scTB = Buf("scT")
            cdec = sb2("cdec", [128, 4096], BF16)
            cdecB = [Buf("cdec%d" % h) for h in range(H)]
            ndec = sb2("ndec", [128, 8], BF16)
            ndecB = Buf("ndec")
            hbh = sb2("hbh", [128, 2048], BF16)
            hbhB = Buf("hbh")
            sm = {k: sb2("sm_" + k, [128, 4], F32) for k in ("c", "rc", "ss", "t", "ln", "sc")}
            smB = {k: Buf("sm_" + k) for k in sm}
            ss2 = sb2("ss2", [128, NJ], F32)
            ss2B = Buf("ss2")
            rs2 = sb2("rs2", [128, NJ], F32)
            rs2B = Buf("rs2")
            p_t = sb2("p_t", [128, NJ, 256], F32)
            ptB = Buf("p_t")
            p_bf = sb2("p_bf", [128, NJ, 256], BF16)
            pbfB = Buf("p_bf")
            pT = sb2("pT", [128, 2, T], BF16)
            pTB = Buf("pT")
            out_evs = []


            def ld_p(m):
                S.dma("sp", lambda e, inc: inc(e.dma_start(
                    out=p_t[:], in_=p_seg[m * T:(m + 1) * T, :].rearrange("(j p) d -> p j d", p=128))), ptB, writes=[ptB])

            def ld_kv(g):
                kwc_, vc_ = kw_c[g % 2], v_c[g % 2]
                S.dma("sp", lambda e, inc: inc(e.dma_start(out=kwc_[:], in_=kws[g * L:(g + 1) * L, :])), kwcB[g % 2], writes=[kwcB[g % 2]])
                S.dma("sp", lambda e, inc: inc(e.dma_start(out=vc_[:], in_=vs[g * L:(g + 1) * L, :])), vcB[g % 2], writes=[vcB[g % 2]])

            def ld_hnT(m):
                S.dma("sp", lambda e, inc: inc(e.dma_start(out=hnT[:].rearrange("p k t -> p (k t)"), in_=hns[m])),
                      hnTB, reads=[hnsB[m]], writes=[hnTB])

            for m in range(NMT):
                xb, xbB = xbuf[m % 2], xB[m % 2]
                gB = gateB[m]
                if m == 0:
                    ld_p(0)
                if m == 0:
                    pass
                hn_jobs = []
                for j in range(NJ):
                    hn_jobs.append(j)
                if m == 0:
                    pass
                dbg("hnT", hnT[:], [128, 8, T], BF16, [hnTB])
                S.stop_if("p2B")
                def conv_chunk(c, m=m):
                    wt, wtB = get_w(CH_CONV + c)
                    w3 = v3(wt[:, :], 8)
                    if m == 0:
                        bk, bB = next_bank()

                        def hal(e, bk=bk, w3=w3):
                            ins = None
                            for g in range(2):
                                for kc in range(8):
                                    ins = e.matmul(bk[:, g * 2:g * 2 + 2], lhsT=w3[:, kc, g * 128:(g + 1) * 128],
                                                   rhs=hT_h[:, kc, :], start=(kc == 0), stop=(kc == 7))
                            return ins
                        S.op("pe", hal, reads=[wtB, hThB], writes=[bB])
                        S.op("act", lambda e, bk=bk: e.activation(out=t1[0][:, 0:2], in_=bk[:, 0:2], func=AF.Copy),
                             reads=[bB], writes=[t1B[0]])
                        S.op("dve", lambda e, bk=bk: e.tensor_tensor(out=u_t[:, 0:2], in0=t1[0][:, 0:2], in1=bk[:, 2:4],
                                                                     op=ALU.mult), reads=[bB, t1B[0]], writes=[uB])
                    else:
                        S.op("pool", lambda e, c=c: e.tensor_copy(out=u_t[:, 0:2], in_=uh[:, c, :]), reads=[uhB], writes=[uB])
                    bks = []
                    for g in range(4):
                        bk, bB = next_bank()

                        def cj(e, bk=bk, g=g, w3=w3):
                            ins = None
                            for kc in range(8):
                                ins = e.matmul(bk[:, :], lhsT=w3[:, kc, g * 128:(g + 1) * 128], rhs=hnT[:, kc, :],
                                               start=(kc == 0), stop=(kc == 7))
                            return ins
                        S.op("pe", cj, reads=[wtB, hnTB], writes=[bB])
                        bks.append((bk, bB))
                    after_w()
                    (bxa, Bxa), (bca, Bca), (bba, Bba), (bza, Bza) = bks
                    S.op("act", lambda e, bk=bxa: e.activation(out=t1[0][:], in_=bk[:, :], func=AF.Copy),
                         reads=[Bxa], writes=[t1B[0]])
                    S.op("dve", lambda e, bk=bca: e.tensor_tensor(out=u_t[:, 2:T + 2], in0=t1[0][:], in1=bk[:, :], op=ALU.mult),
                         reads=[Bca, t1B[0]], writes=[uB])
                    S.op("pool", lambda e, c=c: e.tensor_copy(out=uh[:, c, :], in_=u_t[:, T:T + 2]), reads=[uB], writes=[uhB])
                    S.op("act", lambda e, c=c: e.activation(out=t1[1][:], in_=u_t[:, 0:T], func=AF.Identity,
                                                            bias=cst[:, C_CB + c:C_CB + c + 1],
                                                            scale=cst[:, C_CW + c * 3:C_CW + c * 3 + 1]),
                         reads=[uB, cstB], writes=[t1B[1]])
                    S.op("dve", lambda e, c=c: e.scalar_tensor_tensor(out=t1[1][:], in0=u_t[:, 1:T + 1],
                                                                      scalar=cst[:, C_CW + c * 3 + 1:C_CW + c * 3 + 2],
                                                                      in1=t1[1][:], op0=ALU.mult, op1=ALU.add),
                         reads=[uB, cstB, t1B[1]], writes=[t1B[1]])
                    S.op("dve", lambda e, c=c: e.scalar_tensor_tensor(out=t1[1][:], in0=u_t[:, 2:T + 2],
                                                                      scalar=cst[:, C_CW + c * 3 + 2:C_CW + c * 3 + 3],
                                                                      in1=t1[1][:], op0=ALU.mult, op1=ALU.add),
                         reads=[uB, cstB, t1B[1]], writes=[t1B[1]])
                    S.op("dve", lambda e, bk=bba: e.tensor_tensor(out=t1[2][:], in0=t1[1][:], in1=bk[:, :], op=ALU.mult),
                         reads=[Bba, t1B[1]], writes=[t1B[2]])
                    S.op("act", lambda e, bk=bza: e.activation(out=t1[3][:], in_=bk[:, :], func=AF.Silu),
                         reads=[Bza], writes=[t1B[3]])
                    S.op("pool", lambda e, c=c: e.tensor_tensor(out=ya_inT[:, c, :], in0=t1[2][:], in1=t1[3][:], op=ALU.mult),
                         reads=[t1B[2], t1B[3]], writes=[yainB])

                for i in range(2):
                    wt, wtB = get_w(CH_Q + i)
                    w3 = v3(wt[:, :], 8)
                    for s4 in range(4):
                        hc = i * 4 + s4
                        bk, bB = next_bank()

                        def qj(e, bk=bk, w3=w3, s4=s4):
                            ins = None
                            for kc in range(8):
                                ins = e.matmul(bk[:, :], lhsT=w3[:, kc, s4 * 128:(s4 + 1) * 128], rhs=hnT[:, kc, :],
                                               start=(kc == 0), stop=(kc == 7))
                            return ins
                        S.op("pe", qj, reads=[wtB, hnTB], writes=[bB])
                        S.op("act", lambda e, bk=bk, hc=hc: e.activation(out=qT[:, hc, :], in_=bk[:, :], func=AF.Copy,
                                                                         scale=DK ** -0.5), reads=[bB], writes=[qTB])
                    after_w()

                S.op("pool", lambda e: e.tensor_copy(out=p_bf[:], in_=p_t[:]), reads=[ptB], writes=[pbfB])
                if m + 1 < NMT:
                    ld_p(m + 1)
                for kc2 in range(2):
                    bk, bB = next_bank()

                    def trp(e, bk=bk, kc2=kc2):
                        ins = None
                        for j in range(NJ):
                            ins = e.matmul(bk[:, j * 128:(j + 1) * 128], lhsT=p_bf[:, j, kc2 * 128:(kc2 + 1) * 128],
                                           rhs=ident[:], start=True, stop=True)
                        return ins
                    S.op("pe", trp, reads=[pbfB, constB], writes=[bB])
                    S.op("act", lambda e, bk=bk, kc2=kc2: e.activation(out=pT[:, kc2, :], in_=bk[:, :], func=AF.Copy),
                         reads=[bB], writes=[pTB])

                for i in range(8):
                    wt, wtB = get_w(CH_GATE + i)
                    w3 = v3(wt[:, :], 8)
                    for dd in range(2):
                        d = i * 2 + dd
                        bko, bBo = next_bank()

                        def oj(e, bk=bko, w3=w3, dd=dd):
                            ins = None
                            for kc in range(8):
                                ins = e.matmul(bk[:, :], lhsT=w3[:, kc, dd * 256:dd * 256 + 128], rhs=hnT[:, kc, :],
                                               start=(kc == 0), stop=(kc == 7))
                            return ins
                        S.op("pe", oj, reads=[wtB, hnTB], writes=[bBo])
                        bkz, bBz = next_bank()

                        def zj(e, bk=bkz, w3=w3, dd=dd):
                            ins = None
                            for kc in range(8):
                                ins = e.matmul(bk[:, :], lhsT=w3[:, kc, dd * 256 + 128:dd * 256 + 256], rhs=hnT[:, kc, :],
                                               start=(kc == 0), stop=(kc == 7))
                            return ins
                        S.op("pe", zj, reads=[wtB, hnTB], writes=[bBz])
                        ta, tb = (2, 3) if d % 2 == 0 else (0, 1)
                        S.op("act", lambda e, bk=bko, ta=ta: e.activation(out=t1[ta][:], in_=bk[:, :], func=AF.Tanh, scale=0.5),
                             reads=[bBo], writes=[t1B[ta]])
                        S.op("act", lambda e, bk=bkz, tb=tb: e.activation(out=t1[tb][:], in_=bk[:, :], func=AF.Silu),
                             reads=[bBz], writes=[t1B[tb]])
                        S.op("dve", lambda e, ta=ta, tb=tb: e.scalar_tensor_tensor(out=t1[ta][:], in0=t1[ta][:], scalar=1.0, in1=t1[tb][:],
                                                                                   op0=ALU.add, op1=ALU.mult),
                             reads=[t1B[ta], t1B[tb]], writes=[t1B[ta]])
                        S.op("pool", lambda e, d=d, ta=ta: e.tensor_scalar(out=gateT[:, d, :], in0=t1[ta][:],
                                                                           scalar1=cst[:, C_GHEAD + d:C_GHEAD + d + 1], scalar2=0.5,
                                                                           op0=ALU.mult, op1=ALU.mult),
                             reads=[t1B[ta], cstB], writes=[gtB[d]])
                    after_w()

                dbg("qT", qT[:], [128, 8, T], BF16, [qTB])
                dbg("gateF", gateT[:], [128, 16, T], BF16, gtB)
                S.stop_if("p2F")
                if m + 1 < NMT:
                    ld_x(m + 1)
                if m == 0:
                    ld_kv(0)
                    emit_combine()
                for j in range(NJ):
                    ch = m * NJ + j
                    tok0 = m * T + j * L
                    kwc, kwB_ = kw_c[ch % 2], kwcB[ch % 2]
                    vc, vB_ = v_c[ch % 2], vcB[ch % 2]
                    if ch + 1 < NCHK:
                        ld_kv(ch + 1)
                    for half in range(2):
                        bk, bB = next_bank()

                        def trk(e, bk=bk, half=half, kwc=kwc):
                            ins = None
                            for i in range(4):
                                hc = half * 4 + i
                                ins = e.matmul(bk[:, i * 128:(i + 1) * 128], lhsT=kwc[:, hc * 128:(hc + 1) * 128],
                                               rhs=ident[:], start=True, stop=True)
                            return ins
                        S.op("pe", trk, reads=[kwB_, constB], writes=[bB])
                        S.op("act", lambda e, bk=bk, half=half: e.activation(
                            out=kwT[:, half * 4:half * 4 + 4, :], in_=v3(bk[:, :], 4), func=AF.Copy),
                            reads=[bB], writes=[kwTB])
                    for h in range(H):
                        S.op("act", lambda e, h=h, ch=ch: e.activation(out=cdec[:, h * 1024:(h + 1) * 1024], in_=Cst[:, h * 1024:(h + 1) * 1024],
                                                                       func=AF.Copy, scale=DEC[:, ch, h:h + 1]),
                             reads=[CB[h][0], CB[h][1], gB], writes=[cdecB[h]])
                        S.op("dve", lambda e, h=h, ch=ch: e.tensor_scalar(out=ndec[:, h * 2:h * 2 + 2], in0=nst[:, h * 2:h * 2 + 2],
                                                                          scalar1=DEC[:, ch, h:h + 1], scalar2=None,
                                                                          op0=ALU.mult), reads=[nB, gB], writes=[ndecB])
                    bks_, bBs = next_bank()

                    def sj(e, bk=bks_, j=j):
                        ins = None
                        for h in range(H):
                            for cc in range(2):
                                ins = e.matmul(bk[:, h * 128:(h + 1) * 128], lhsT=kwT[:, h * 2 + cc, :],
                                               rhs=qT[:, h * 2 + cc, j * 128:(j + 1) * 128], start=(cc == 0), stop=(cc == 1))
                        return ins
                    S.op("pe", sj, reads=[kwTB, qTB], writes=[bBs])
                    conv_chunk(2 * j)
                    S.op("dve", lambda e, bk=bks_: e.tensor_tensor(out=scT[:], in0=bk[:, :], in1=mask4[:], op=ALU.mult),
                         reads=[bBs, constB], writes=[scTB])
                    bkd, bBd = next_bank()

                    def dj(e, bk=bkd, j=j):
                        ins = None
                        for h in range(H):
                            ins = e.matmul(bk[:, h:h + 1], lhsT=scT[:, h * 128:(h + 1) * 128], rhs=ones_bf[:], start=True, stop=False)
                            for cc in range(2):
                                ins = e.matmul(bk[:, h:h + 1], lhsT=qT[:, h * 2 + cc, j * 128:(j + 1) * 128],
                                               rhs=ndec[:, h * 2 + cc:h * 2 + cc + 1], start=False, stop=(cc == 1))
                        return ins
                    S.op("pe", dj, reads=[scTB, qTB, ndecB, constB], writes=[bBd])
                    S.op("act", lambda e, bk=bkd: e.activation(out=sm["t"][:], in_=bk[:, 0:4], func=AF.Square),
                         reads=[bBd], writes=[smB["t"]])
                    S.op("dve", lambda e, ch=ch: e.tensor_tensor(out=sm["c"][:], in0=sm["t"][:], in1=EQ2[:, ch, :], op=ALU.max),
                         reads=[smB["t"], gB], writes=[smB["c"]])
                    nbk = []
                    for h in range(H):
                        bk, bB = next_bank()

                        def numj(e, bk=bk, h=h, j=j, vc=vc):
                            e.matmul(bk[:, :], lhsT=scT[:, h * 128:(h + 1) * 128], rhs=vc[:, h * 512:(h + 1) * 512],
                                     start=True, stop=False)
                            ins = None
                            for cc in range(2):
                                ins = e.matmul(bk[:, :], lhsT=qT[:, h * 2 + cc, j * 128:(j + 1) * 128],
                                               rhs=cdec[:, h * 1024 + cc * 512:h * 1024 + (cc + 1) * 512], start=False, stop=(cc == 1))
                            return ins
                        S.op("pe", numj, reads=[scTB, vB_, qTB, cdecB[h]], writes=[bB])
                        S.op("act", lambda e, bk=bk, h=h: e.activation(out=junk[:, 0:512], in_=bk[:, :], func=AF.Square,
                                                                       accum_out=sm["ss"][:, h:h + 1]),
                             reads=[bB], writes=[smB["ss"]])
                        nbk.append((bk, bB))
                    S.op("dve", lambda e: e.scalar_tensor_tensor(out=sm["rc"][:], in0=sm["c"][:], scalar=float(EPS * DV), in1=sm["ss"][:],
                                                                 op0=ALU.mult, op1=ALU.add),
                         reads=[smB["c"], smB["ss"]], writes=[smB["rc"]])
                    S.op("act", lambda e: e.activation(out=sm["ln"][:], in_=sm["rc"][:], func=AF.Ln, scale=1.0 / DV),
                         reads=[smB["rc"]], writes=[smB["ln"]])
                    S.op("act", lambda e: e.activation(out=sm["sc"][:], in_=sm["ln"][:], func=AF.Exp, scale=-0.5),
                         reads=[smB["ln"]], writes=[smB["sc"]])
                    for h in range(H):
                        bk, bB = nbk[h]
                        S.op("act", lambda e, bk=bk, h=h: e.activation(out=hbh[:, h * 512:(h + 1) * 512], in_=bk[:, :], func=AF.Copy,
                                                                       scale=sm["sc"][:, h:h + 1]),
                             reads=[bB, smB["sc"]], writes=[hbhB])
                    conv_chunk(2 * j + 1)
                    for q4 in range(4):
                        bk, bB = next_bank()

                        def trh2(e, bk=bk, q4=q4):
                            ins = None
                            for i in range(4):
                                d = q4 * 4 + i
                                ins = e.matmul(bk[:, i * 128:(i + 1) * 128], lhsT=hbh[:, d * 128:(d + 1) * 128],
                                               rhs=ident[:], start=True, stop=True)
                            return ins
                        S.op("pe", trh2, reads=[hbhB, constB], writes=[bB])
                        S.op("dve", lambda e, bk=bk, q4=q4, j=j: e.tensor_tensor(
                            out=gateT[:, q4 * 4:q4 * 4 + 4, j * 128:(j + 1) * 128], in0=v3(bk[:, :], 4),
                            in1=gateT[:, q4 * 4:q4 * 4 + 4, j * 128:(j + 1) * 128], op=ALU.mult),
                            reads=[bB] + gtB[q4 * 4:q4 * 4 + 4], writes=gtB[q4 * 4:q4 * 4 + 4])
                    bkn, bBn = next_bank()

                    def nj2(e, bk=bkn, kwc=kwc):
                        ins = None
                        for hc in range(8):
                            ins = e.matmul(bk[:, hc:hc + 1], lhsT=kwc[:, hc * 128:(hc + 1) * 128], rhs=ones_bf[:],
                                           start=True, stop=True)
                        return ins
                    S.op("pe", nj2, reads=[kwB_, constB], writes=[bBn])
                    for h in range(H):
                        S.op("dve", lambda e, h=h, ch=ch: e.tensor_scalar(
                            out=nst[:, h * 2:h * 2 + 2], in0=nst[:, h * 2:h * 2 + 2], scalar1=DEC[:, ch, h:h + 1],
                            scalar2=None, op0=ALU.mult), reads=[nB, gB], writes=[nB])
                    S.op("dve", lambda e, bk=bkn: e.tensor_tensor(out=nst[:], in0=nst[:], in1=bk[:, 0:8], op=ALU.add),
                         reads=[nB, bBn], writes=[nB])
                    for h in range(H):
                        for cc in range(2):
                            bk, bB = next_bank()
                            S.op("pe", lambda e, bk=bk, h=h, cc=cc, kwc=kwc, vc=vc: e.matmul(
                                bk[:, :], lhsT=kwc[:, h * 256 + cc * 128:h * 256 + (cc + 1) * 128],
                                rhs=vc[:, h * 512:(h + 1) * 512], start=True, stop=True),
                                reads=[kwB_, vB_], writes=[bB])
                            S.op("dve", lambda e, bk=bk, h=h, cc=cc, ch=ch: e.scalar_tensor_tensor(
                                out=Cst[:, h * 1024 + cc * 512:h * 1024 + (cc + 1) * 512], in0=Cst[:, h * 1024 + cc * 512:h * 1024 + (cc + 1) * 512],
                                scalar=DEC[:, ch, h:h + 1], in1=bk[:, :], op0=ALU.mult, op1=ALU.add),
                                reads=[bB, gB, CB[h][cc]], writes=[CB[h][cc]])

                dbg("ya_inT", ya_inT[:], [128, 8, T], BF16, [yainB])
                S.stop_if("p2C")
                for e8 in range(8):
                    if e8 % 4 == 0:
                        if e8:
                            after_w()
                        wg, wgB = get_w(CH_GA + e8 // 4)
                        wa, waB = get_w(CH_WA + e8 // 4)
                    wg3, wa3 = v3(wg[:, :], 8), v3(wa[:, :], 8)
                    co = (e8 % 4) * 128
                    bkg, bBg = next_bank()

                    def gaj(e, bk=bkg, wg3=wg3, co=co):
                        ins = None
                        for kc in range(8):
                            ins = e.matmul(bk[:, :], lhsT=wg3[:, kc, co:co + 128], rhs=hnT[:, kc, :], start=(kc == 0), stop=(kc == 7))
                        return ins
                    S.op("pe", gaj, reads=[wgB, hnTB], writes=[bBg])
                    bka, bBa = next_bank()

                    def yaj(e, bk=bka, wa3=wa3, co=co):
                        ins = None
                        for kc in range(8):
                            ins = e.matmul(bk[:, :], lhsT=wa3[:, kc, co:co + 128], rhs=ya_inT[:, kc, :], start=(kc == 0), stop=(kc == 7))
                        return ins
                    S.op("pe", yaj, reads=[waB, yainB], writes=[bBa])
                    ta = (e8 % 4)
                    S.op("act", lambda e, bk=bkg, ta=ta: e.activation(out=t1[ta][:], in_=bk[:, :], func=AF.Tanh, scale=0.5),
                         reads=[bBg], writes=[t1B[ta]])
                    S.op("dve", lambda e, bk=bka, e8=e8, ta=ta: e.scalar_tensor_tensor(out=mrg[:, e8, :], in0=t1[ta][:], scalar=1.0,
                                                                                       in1=bk[:, :], op0=ALU.add, op1=ALU.mult),
                         reads=[bBa, t1B[ta]], writes=[mrgB[e8]])
                after_w()

                dbg("mrgD", mrg[:], [128, 8, T], BF16, mrgB)
                dbg("gateG", gateT[:], [128, 16, T], BF16, gtB)
                dbg("Cst", Cst[:], [128, 4096], F32, CBall)
                S.stop_if("p2G")
                for i in range(4):
                    if i % 2 == 0:
                        wg, wgB = get_w(CH_GB + i // 2)
                    wt, wtB = get_w(CH_WB + i)
                    wb3 = v3(wt[:, :], 16)
                    for ee in range(2):
                        e8 = i * 2 + ee
                        wg3 = v3(wg[:, :], 8)
                        co = (e8 % 4) * 128
                        bkg, bBg = next_bank()

                        def gbj(e, bk=bkg, wg3=wg3, co=co):
                            ins = None
                            for kc in range(8):
                                ins = e.matmul(bk[:, :], lhsT=wg3[:, kc, co:co + 128], rhs=hnT[:, kc, :], start=(kc == 0), stop=(kc == 7))
                            return ins
                        S.op("pe", gbj, reads=[wgB, hnTB], writes=[bBg])
                        bky, bBy = next_bank()

                        def ybj(e, bk=bky, wb3=wb3, ee=ee):
                            ins = None
                            for kc in range(16):
                                ins = e.matmul(bk[:, :], lhsT=wb3[:, kc, ee * 128:(ee + 1) * 128], rhs=gateT[:, kc, :],
                                               start=(kc == 0), stop=(kc == 15))
                            return ins
                        S.op("pe", ybj, reads=[wtB] + gtB, writes=[bBy])
                        ta, tb = (0, 1) if e8 % 2 == 0 else (2, 3)
                        S.op("act", lambda e, bk=bkg, ta=ta: e.activation(out=t1[ta][:], in_=bk[:, :], func=AF.Tanh, scale=0.5),
                             reads=[bBg], writes=[t1B[ta]])
                        S.op("dve", lambda e, bk=bky, ta=ta, tb=tb: e.scalar_tensor_tensor(out=t1[tb][:], in0=t1[ta][:], scalar=1.0, in1=bk[:, :],
                                                                                           op0=ALU.add, op1=ALU.mult),
                             reads=[bBy, t1B[ta]], writes=[t1B[tb]])
                        S.op("pool", lambda e, e8=e8, tb=tb: e.tensor_tensor(out=mrg[:, e8, :], in0=mrg[:, e8, :], in1=t1[tb][:], op=ALU.add),
                             reads=[t1B[tb], mrgB[e8]], writes=[mrgB[e8]])
                    after_w(hold=1 if i % 2 == 0 else 0)

                dbg("mrgH", mrg[:], [128, 8, T], BF16, mrgB)
                S.stop_if("p2H")
                if m + 1 < NMT:
                    ld_hnT(m + 1)
                wow = [get_w(CH_WO), get_w(CH_WO + 1)]
                ss2j = [Buf("ss2_%d" % j) for j in range(NJ)]
                rs2j = [Buf("rs2_%d" % j) for j in range(NJ)]
                for jj in range(NJ + 1):
                    if jj < NJ:
                        j = jj
                        for cb in range(2):
                            wo, woB = wow[cb]
                            wo3 = v3(wo[:, :], 8)
                            bk, bB = next_bank()

                            def x2j(e, bk=bk, wo3=wo3, j=j):
                                ins = None
                                for kc in range(8):
                                    ins = e.matmul(bk[:, :], lhsT=mrg[:, kc, j * 128:(j + 1) * 128], rhs=wo3[:, kc, :],
                                                   start=(kc == 0), stop=(kc == 7))
                                return ins
                            S.op("pe", x2j, reads=[woB] + mrgB, writes=[bB])
                            S.op("dve", lambda e, bk=bk, xb=xb, j=j, cb=cb: e.scalar_tensor_tensor(
                                out=xb[:, j, cb * 512:(cb + 1) * 512], in0=bk[:, :], scalar=0.5, in1=xb[:, j, cb * 512:(cb + 1) * 512],
                                op0=ALU.mult, op1=ALU.add), reads=[bB, xbB[j]], writes=[xbB[j]])
                        if jj == NJ - 1:
                            after_w()
                        S.op("act", lambda e, xb=xb, j=j: e.activation(out=junk[:], in_=xb[:, j, :], func=AF.Square,
                                                                      accum_out=ss2[:, j:j + 1]),
                             reads=[xbB[j]], writes=[ss2j[j]])
                        S.op("act", lambda e, j=j: e.activation(out=rs2[:, j:j + 1], in_=ss2[:, j:j + 1], func=AF.Ln, bias=EPS, scale=1.0 / D),
                             reads=[ss2j[j]], writes=[rs2j[j]])
                        S.op("act", lambda e, j=j: e.activation(out=rs2[:, j:j + 1], in_=rs2[:, j:j + 1], func=AF.Exp, scale=-0.5),
                             reads=[rs2j[j]], writes=[rs2j[j]])
                        S.op("dve", lambda e, xb=xb, j=j: e.scalar_tensor_tensor(
                            out=hn2b[j % 2][:], in0=xb[:, j, :], scalar=rs2[:, j:j + 1], in1=cst[:, C_GPLE:C_GPLE + D],
                            op0=ALU.mult, op1=ALU.mult), reads=[xbB[j], rs2j[j], cstB], writes=[hn2bB[j % 2]])
                    if jj >= 1:
                        j = jj - 1
                        for half in range(2):
                            bk, bB = next_bank()

                            def tr2(e, bk=bk, half=half, j=j):
                                ins = None
                                for i in range(4):
                                    kc = half * 4 + i
                                    ins = e.matmul(bk[:, i * 128:(i + 1) * 128], lhsT=hn2b[j % 2][:, kc * 128:(kc + 1) * 128],
                                                   rhs=ident[:], start=True, stop=True)
                                return ins
                            S.op("pe", tr2, reads=[hn2bB[j % 2], constB], writes=[bB])
                            S.op("act", lambda e, bk=bk, half=half, j=j: e.activation(
                                out=hn2T[:, half * 4:half * 4 + 4, j * 128:(j + 1) * 128], in_=v3(bk[:, :], 4), func=AF.Copy),
                                reads=[bB], writes=[hn2TB])

                dbg("x2", xb[:], [128, NJ, 1024], F32, xbB)
                wpgw = [get_w(CH_WPG), get_w(CH_WPG + 1)]
                wp, wpB = get_w(CH_WP)
                wp3 = wp[:, 0:2048].rearrange("p (a b) -> p a b", a=2)
                for j in range(NJ):
                    for cb in range(2):
                        wg, wgB = wpgw[cb]
                        wg3 = v3(wg[:, :], 8)
                        bkg, bBg = next_bank()

                        def pgj(e, bk=bkg, wg3=wg3, j=j):
                            ins = None
                            for kc in range(8):
                                ins = e.matmul(bk[:, :], lhsT=hn2T[:, kc, j * 128:(j + 1) * 128], rhs=wg3[:, kc, :],
                                               start=(kc == 0), stop=(kc == 7))
                            return ins
                        S.op("pe", pgj, reads=[wgB, hn2TB], writes=[bBg])
                        bkp, bBp = next_bank()

                        def pwj(e, bk=bkp, j=j, cb=cb):
                            ins = None
                            for kc in range(2):
                                ins = e.matmul(bk[:, :], lhsT=pT[:, kc, j * 128:(j + 1) * 128], rhs=wp3[:, kc, cb * 512:(cb + 1) * 512],
                                               start=(kc == 0), stop=(kc == 1))
                            return ins
                        S.op("pe", pwj, reads=[wpB, pTB], writes=[bBp])
                        ta, tb = (0, 1) if cb == 0 else (2, 3)
                        S.op("act", lambda e, bk=bkg, ta=ta: e.activation(out=t1[ta][:], in_=bk[:, :], func=AF.Tanh, scale=0.5),
                             reads=[bBg], writes=[t1B[ta]])
                        S.op("dve", lambda e, bk=bkp, ta=ta, tb=tb: e.scalar_tensor_tensor(out=t1[tb][:], in0=t1[ta][:], scalar=1.0, in1=bk[:, :],
                                                                                           op0=ALU.add, op1=ALU.mult),
                             reads=[bBp, t1B[ta]], writes=[t1B[tb]])
                        S.op("dve", lambda e, xb=xb, j=j, cb=cb, tb=tb: e.scalar_tensor_tensor(
                            out=xb[:, j, cb * 512:(cb + 1) * 512], in0=t1[tb][:], scalar=0.5, in1=xb[:, j, cb * 512:(cb + 1) * 512],
                            op0=ALU.mult, op1=ALU.add), reads=[t1B[tb], xbB[j]], writes=[xbB[j]])
                after_w()

                dbg("x3", xb[:], [128, NJ, 1024], F32, xbB)
                S.stop_if("p2J")
                for j in range(NJ):
                    S.op("act", lambda e, xb=xb, j=j: e.activation(out=junk[:], in_=xb[:, j, :], func=AF.Square,
                                                                  accum_out=ss2[:, j:j + 1]),
                         reads=[xbB[j]], writes=[ss2B])
                S.op("act", lambda e: e.activation(out=rs2[:], in_=ss2[:], func=AF.Ln, bias=EPS, scale=1.0 / D),
                     reads=[ss2B], writes=[rs2B])
                S.op("act", lambda e: e.activation(out=rs2[:], in_=rs2[:], func=AF.Exp, scale=-0.5), reads=[rs2B], writes=[rs2B])
                for j in range(NJ):
                    S.op("dve", lambda e, xb=xb, j=j: e.scalar_tensor_tensor(
                        out=xb[:, j, :], in0=xb[:, j, :], scalar=rs2[:, j:j + 1], in1=cst[:, C_GFIN:C_GFIN + D],
                        op0=ALU.mult, op1=ALU.mult), reads=[xbB[j], rs2B, cstB], writes=[xbB[j]])
                ev = S.dma("sp", lambda e, inc, xb=xb, m=m: inc(e.dma_start(
                    out=out_d[m * T:(m + 1) * T, :].rearrange("(j p) d -> p j d", p=128), in_=xb[:])), xbB[0], reads=xbB)
                out_evs.append(ev)

            S.stop_if("m0") if False else None
            S.wait_only("sp", out_evs)

            with nc.Block() as block:
                @block.tensor
                def _(pe):
                    S.emit("pe", pe)

                @block.scalar
                def _(act):
                    S.emit("act", act)

                @block.vector
                def _(dve):
                    S.emit("dve", dve)

                @block.gpsimd
                def _(pool):
                    S.emit("pool", pool)

                @block.sync
                def _(sp):
                    S.emit("sp", sp)
    return nc


def _chunk(W, cols, kc):
    a = W[:, cols]
    n = a.shape[1]
    a = a.reshape(kc, 128, n).transpose(1, 0, 2).reshape(128, kc * n)
    out = np.zeros((128, 4096), np.float32)
    out[:, :kc * n] = a
    return out


_NC_CACHE = {}


def kernel(x, p, g_mix, w_in, conv_w, conv_b, w_a_out, b_gates, g_head, w_b_out, w_o, g_ple,
           w_ple_gate, w_ple, g_final):
    x = np.asarray(x, np.float32)
    p = np.asarray(p, np.float32)
    w_in0 = np.asarray(w_in, np.float32)[0]
    r = np.arange
    OXA, OBA, OCA, OZA, OQ, OK_, OV, OO, OZB, OIG, OGA, OGB = 0, 1024, 2048, 3072, 4096, 5120, 6144, 8192, 10240, 12288, 12296, 13320
    chunks = []
    for c in range(8):
        cols = np.concatenate([OXA + c * 128 + r(128), OCA + c * 128 + r(128), OBA + c * 128 + r(128), OZA + c * 128 + r(128)])
        chunks.append(_chunk(w_in0, cols, 8))
    for i in range(2):
        chunks.append(_chunk(w_in0, OQ + i * 512 + r(512), 8))
    for i in range(8):
        d0, d1 = 2 * i, 2 * i + 1
        cols = np.concatenate([OO + d0 * 128 + r(128), OZB + d0 * 128 + r(128), OO + d1 * 128 + r(128), OZB + d1 * 128 + r(128)])
        chunks.append(_chunk(w_in0, cols, 8))
    for i in range(2):
        chunks.append(_chunk(w_in0, OGA + i * 512 + r(512), 8))
    wa = np.asarray(w_a_out, np.float32)[0]
    for i in range(2):
        chunks.append(_chunk(wa, i * 512 + r(512), 8))
    for i in range(2):
        chunks.append(_chunk(w_in0, OGB + i * 512 + r(512), 8))
    wb = np.asarray(w_b_out, np.float32)[0]
    for i in range(4):
        chunks.append(_chunk(wb, i * 256 + r(256), 16))
    wo = np.asarray(w_o, np.float32)[0]
    for i in range(2):
        chunks.append(_chunk(wo, i * 512 + r(512), 8))
    wpg = np.asarray(w_ple_gate, np.float32)[0]
    for i in range(2):
        chunks.append(_chunk(wpg, i * 512 + r(512), 8))
    wp = np.asarray(w_ple, np.float32)[0]
    chunks.append(_chunk(wp, r(1024), 2))
    wq32 = np.ascontiguousarray(np.stack(chunks, 0))
    assert wq32.shape == (NWCH, 128, 4096)
    wkv32 = np.ascontiguousarray(w_in0[:, OK_:OK_ + 3072].reshape(8, 128, 3072).transpose(1, 0, 2).reshape(128, 8 * 3072))
    wif32 = np.ascontiguousarray(w_in0[:, OIG:OIG + 8].reshape(8, 128, 8).transpose(1, 0, 2).reshape(128, 64))

    cst = np.zeros((NCORES, 128, NCST), np.float32)
    cst[:, :, C_ID:C_ID + 128] = np.eye(128, dtype=np.float32)
    tri = (np.arange(128)[:, None] <= np.arange(128)[None, :]).astype(np.float32)
    cst[:, :, C_U:C_U + 128] = tri
    cst[:, :, C_ONES:C_ONES + 128] = 1.0
    cst[:, :, C_MASK:C_MASK + 512] = np.tile(tri, (1, 4))
    cst[:, :, C_GMIX:C_GMIX + D] = np.asarray(g_mix, np.float32)[0][None, :]
    cst[:, :, C_GPLE:C_GPLE + D] = np.asarray(g_ple, np.float32)[0][None, :]
    cst[:, :, C_GFIN:C_GFIN + D] = np.asarray(g_final, np.float32)[None, :]
    cst[:, :, C_GHEAD:C_GHEAD + 16] = np.asarray(g_head, np.float32)[0].reshape(16, 128).T
    cw = np.asarray(conv_w, np.float32)[0]
    cst[:, :, C_CW:C_CW + 24] = cw.reshape(3, 8, 128).transpose(2, 1, 0).reshape(128, 24)
    cst[:, :, C_CB:C_CB + 8] = np.asarray(conv_b, np.float32)[0].reshape(8, 128).T
    cst[:, :, C_BG:C_BG + 8] = np.asarray(b_gates, np.float32)[0][None, :]
    for c in range(NCORES):
        s = c % 4
        btw = np.zeros((4, 4), np.float32)
        sel = np.zeros((4,), np.float32)
        for rr in range(4):
            if rr < s:
                sel[rr] = 1.0
            for i in range(4):
                if rr < i < s:
                    btw[i, rr] = 1.0
        cst[c, :, C_BTW:C_BTW + 16] = btw.reshape(16)[None, :]
        cst[c, :, C_SEL:C_SEL + 4] = sel[None, :]

    in_maps = []
    for c in range(NCORES):
        b, s = c // 4, c % 4
        xs = np.ascontiguousarray(x[b, s * NT:(s + 1) * NT])
        if s == 0:
            xp = np.zeros((L, D), np.float32)
        else:
            xp = np.ascontiguousarray(x[b, s * NT - L:s * NT])
        ps = np.ascontiguousarray(p[0, b, s * NT:(s + 1) * NT])
        in_maps.append({"x_seg": xs, "x_prev": xp, "p_seg": ps, "wq32": wq32, "wkv32": wkv32, "wif32": wif32,
                        "cst": np.ascontiguousarray(cst[c])})
    if "nc" not in _NC_CACHE:
        _NC_CACHE["nc"] = build_nc()
    nc = _NC_CACHE["nc"]
    res = run_bass_kernel_spmd(nc, in_maps, core_ids=list(range(NCORES)))
    _NC_CACHE["res"] = res
    out = np.empty((2, SEQ, D), np.float32)
    for c in range(NCORES):
        b, s = c // 4, c % 4
        out[b, s * NT:(s + 1) * NT] = res.results[c]["out"]
    return out
```

```python
import os
import numpy as np
import concourse.bass as bass
import concourse.mybir as mybir
from concourse.bass_utils import run_bass_kernel_spmd

F32 = mybir.dt.float32
BF16 = mybir.dt.bfloat16
AF = mybir.ActivationFunctionType
ALU = mybir.AluOpType

NCORES = 8
D = 1024
SEQ = 16384
NT = 4096
T = 512
NMT = NT // T
L = 128
NJ = T // L
NCHK = NT // L
H = 4
DK = 256
DV = 512
EPS = 1e-6
GW = 4112

CH_CONV = 0
CH_Q = 8
CH_GATE = 10
CH_GA = 18
CH_WA = 20
CH_GB = 22
CH_WB = 24
CH_WO = 28
CH_WPG = 30
CH_WP = 32
NWCH = 33

C_ID = 0
C_U = 128
C_ONES = 256
C_MASK = 384
C_GMIX = 896
C_GPLE = 1920
C_GFIN = 2944
C_GHEAD = 3968
C_CW = 3984
C_CB = 4008
C_BG = 4016
C_BTW = 4024
C_SEL = 4040
NCST = 4048


class Ev:
    __slots__ = ("sem", "val")

    def __init__(self, sem, val):
        self.sem = sem
        self.val = val


class Buf:
    def __init__(self, name):
        self.name = name
        self.w = None
        self.r = {}
        self.dsem = None
        self.dcount = 0


class Sched:
    ENGS = ("pe", "act", "dve", "pool", "sp")

    def __init__(self, nc, sems, dma_sems):
        self.nc = nc
        self.sem = sems
        self.dma_sems = list(dma_sems)
        self.q = {e: [] for e in self.ENGS}
        self.cnt = {e: 0 for e in self.ENGS}
        self.waited = {e: {} for e in self.ENGS}
        self.dbufs = []
        self.extra_evs = []
        self.stopped = False

    def stop_if(self, tag):
        if os.environ.get("KSTOP", "") == tag:
            self.stopped = True

    def _collect(self, eng, reads, writes, extra):
        need = {}

        def add(ev):
            if ev is None:
                return
            k = id(ev.sem)
            if k not in need or need[k].val < ev.val:
                need[k] = ev

        for b in reads:
            add(b.w)
        for b in writes:
            add(b.w)
            for ev in b.r.values():
                add(ev)
        for ev in extra:
            add(ev)
        out = []
        wd = self.waited[eng]
        for k, ev in need.items():
            if wd.get(k, 0) >= ev.val:
                continue
            wd[k] = ev.val
            out.append(ev)
        return out

    def _commit(self, ev, reads, writes):
        for b in reads:
            k = id(ev.sem)
            if k not in b.r or b.r[k].val < ev.val:
                b.r[k] = ev
        for b in writes:
            b.w = ev
            b.r = {}

    def op(self, eng, fn, reads=(), writes=(), extra=()):
        if self.stopped:
            return Ev(None, 0)
        waits = self._collect(eng, reads, writes, extra)
        self.cnt[eng] += 1
        ev = Ev(self.sem[eng], self.cnt[eng])
        self.q[eng].append((waits, fn, (self.sem[eng], 1)))
        self._commit(ev, reads, writes)
        return ev

    def dma(self, eng, fn, owner, reads=(), writes=(), ndma=1, extra=()):
        if self.stopped:
            return Ev(None, 0)
        waits = self._collect(eng, reads, writes, extra)
        if owner.dsem is None:
            owner.dsem = self.dma_sems.pop()
            self.dbufs.append(owner)
        owner.dcount += 16 * ndma
        ev = Ev(owner.dsem, owner.dcount)
        self.q[eng].append((waits, fn, ("dma", owner.dsem)))
        self._commit(ev, reads, writes)
        return ev

    def barrier(self):
        if self.stopped:
            return
        evs = [Ev(self.sem[f], self.cnt[f]) for f in self.ENGS if self.cnt[f] > 0]
        evs += [Ev(b.dsem, b.dcount) for b in self.dbufs]
        evs += list(self.extra_evs)
        for e in self.ENGS:
            self.wait_only(e, evs)

    def wait_only(self, eng, evs):
        if self.stopped:
            return
        waits = self._collect(eng, (), (), evs)
        self.q[eng].append((waits, None, None))

    def emit(self, eng, handle):
        for waits, fn, inc in self.q[eng]:
            for ev in waits:
                handle.wait_ge(ev.sem, ev.val)
            if fn is None:
                continue
            if inc[0] == "dma":
                dsem = inc[1]
                fn(handle, lambda ins, dsem=dsem: ins.then_inc(dsem, 16))
            else:
                ins = fn(handle)
                if inc[1] is None:
                    ins.then_inc(inc[0])
                else:
                    ins.then_inc(inc[0], inc[1])


def build_nc():
    nc = bass.Bass("TRN2", target_bir_lowering=False)
    x_seg = nc.dram_tensor("x_seg", [NT, D], F32, kind="ExternalInput").ap()
    x_prev = nc.dram_tensor("x_prev", [L, D], F32, kind="ExternalInput").ap()
    p_seg = nc.dram_tensor("p_seg", [NT, 256], F32, kind="ExternalInput").ap()
    wq32 = nc.dram_tensor("wq32", [NWCH, 128, 4096], F32, kind="ExternalInput").ap()
    wkv32 = nc.dram_tensor("wkv32", [128, 8 * 3072], F32, kind="ExternalInput").ap()
    wif32 = nc.dram_tensor("wif32", [128, 64], F32, kind="ExternalInput").ap()
    cst_d = nc.dram_tensor("cst", [128, NCST], F32, kind="ExternalInput").ap()
    out_d = nc.dram_tensor("out", [NT, D], F32, kind="ExternalOutput").ap()
    wq16 = nc.dram_tensor("wq16", [NWCH, 128, 4096], BF16).ap()
    kws = nc.dram_tensor("kws", [NT, 1024], BF16).ap()
    hns = nc.dram_tensor("hns", [NMT, 128, 8 * T], BF16).ap()
    vs = nc.dram_tensor("vs", [NT, 2048], BF16).ap()
    GWS = (2048, 2048, 16)
    GOFF = (0, 2048, 4096)
    gin_ts = [nc.dram_tensor("gin%d" % i, [128, w], F32) for i, w in enumerate(GWS)]
    gout_ts = [nc.dram_tensor("gout%d" % i, [4 * 128, w], F32) for i, w in enumerate(GWS)]

    import contextlib
    es = contextlib.ExitStack()
    with es:
        def sb(name, shape, dt):
            return es.enter_context(nc.sbuf_tensor(name, shape, dt))

        def sem(name):
            return es.enter_context(nc.semaphore(name))

        DBG = os.environ.get("KDBG", "") != ""
        dbg_t = {}
        dbgB = Buf("dbg")

        def dbg(name, ap, shape, dt, reads):
            if not DBG or name in dbg_t:
                return
            t_ = nc.dram_tensor("dbg_" + name, list(shape), dt, kind="ExternalOutput")
            dbg_t[name] = t_
            S.dma("pool", lambda e, inc: inc(e.dma_start(out=t_.ap(), in_=ap)), dbgB, reads=reads)

        sems = {e: sem("s_" + e) for e in Sched.ENGS}
        dma_sems = [sem("d%d" % i) for i in range(60)]
        cc_sem = sem("cc")
        S = Sched(nc, sems, dma_sems)

        cst = sb("cst_sb", [128, NCST], F32)
        cstB = Buf("cst")
        ident = sb("ident", [128, 128], BF16)
        mask4 = sb("mask4", [128, 512], BF16)
        ones_bf = sb("ones_bf", [128, 1], BF16)
        constB = Buf("constbf")
        Cst = sb("Cst", [128, 4096], F32)
        CB = [[Buf("C%d_%d" % (h, cc)) for cc in range(2)] for h in range(H)]
        CBall = [b for hb in CB for b in hb]
        nst = sb("nst", [128, 8], F32)
        nB = Buf("n")
        EQ = sb("EQ", [128, NCHK, 4], F32)
        DEC = sb("DEC", [128, NCHK, 4], F32)
        EQ2 = sb("EQ2", [128, NCHK, 4], F32)
        RSTD1 = sb("RSTD1", [128, NCHK], F32)
        gateB = [Buf("gates%d" % m) for m in range(NMT)]
        rstdB = [Buf("rstd%d" % m) for m in range(NMT)]
        LD = sb("LD", [128, 4], F32)
        LDB = Buf("LD")
        junk = sb("junk", [128, 1024], BF16)
        junkB = {"act": Buf("junk_act")}
        xbuf = [sb("xbuf%d" % i, [128, NJ, 1024], F32) for i in range(2)]
        xB = [[Buf("x%d_%d" % (i, j)) for j in range(NJ)] for i in range(2)]
        hn_bf = sb("hn_bf", [128, 1024], BF16)
        hnB = Buf("hn")
        hnT = sb("hnT", [128, 8, T], BF16)
        hnTB = Buf("hnT")
        hnsB = [Buf("hns%d" % i) for i in range(NMT)]
        hT_h = sb("hT_h", [128, 8, 2], BF16)
        hThB = Buf("hT_h")

        banks = [es.enter_context(nc.psum_tensor("bank%d" % i, [128, 512], F32)) for i in range(8)]
        bankB = [Buf("bank%d" % i) for i in range(8)]
        bank_rr = [0]

        def next_bank():
            i = bank_rr[0] % 8
            bank_rr[0] += 1
            return banks[i], bankB[i]

        def v3(ap, a):
            return ap.rearrange("p (a b) -> p a b", a=a)

        S.dma("sp", lambda e, inc: inc(e.dma_start(out=cst[:], in_=cst_d[:, :])), cstB, writes=[cstB])
        S.op("dve", lambda e: e.tensor_copy(out=ident[:], in_=cst[:, C_ID:C_ID + 128]), reads=[cstB], writes=[constB])
        S.op("dve", lambda e: e.tensor_copy(out=mask4[:], in_=cst[:, C_MASK:C_MASK + 512]), reads=[cstB], writes=[constB])
        S.op("dve", lambda e: e.tensor_copy(out=ones_bf[:], in_=cst[:, C_ONES:C_ONES + 1]), reads=[cstB], writes=[constB])
        S.op("dve", lambda e: e.memset(Cst[:], 0.0), writes=CBall)
        S.op("dve", lambda e: e.memset(nst[:], 0.0), writes=[nB])
        S.op("dve", lambda e: e.memset(LD[:], 0.0), writes=[LDB])

        wq16B = [Buf("wq16_%d" % i) for i in range(NWCH)]

        def ld_x(m):
            xb_, xbB_ = xbuf[m % 2], xB[m % 2]
            S.dma("sp", lambda e, inc: inc(e.dma_start(
                out=xb_[:], in_=x_seg[m * T:(m + 1) * T, :].rearrange("(j p) d -> p j d", p=128))), xbB_[0], writes=xbB_)

        p1 = contextlib.ExitStack()
        with p1:
            def sb1(name, shape, dt):
                return p1.enter_context(nc.sbuf_tensor(name, shape, dt))

            wkv = sb1("wkv", [128, 8, 3072], BF16)
            wkvB = Buf("wkv")
            wif = sb1("wif", [128, 8, 8], BF16)
            wifB = Buf("wif")
            S.dma("pool", lambda e, inc: inc(e.dma_start(out=wif[:], in_=wif32.rearrange("p (k c) -> p k c", k=8))), wifB, writes=[wifB])
            ld_x(0)
            x1flat = xbuf[1][:].rearrange("p j d -> p (j d)")
            wkvflat = wkv[:].rearrange("p k c -> p (k c)")
            stgB = [Buf("stg_a"), Buf("stg_b")]
            for q in range(12):
                hb_ = q % 2
                S.dma("sp", lambda e, inc, q=q, hb_=hb_: inc(e.dma_start(out=x1flat[:, hb_ * 2048:(hb_ + 1) * 2048],
                                                                       in_=wkv32[:, q * 2048:(q + 1) * 2048])),
                      stgB[hb_], writes=[stgB[hb_]])
                if q % 2 == 0:
                    S.op("act", lambda e, q=q, hb_=hb_: e.activation(out=wkvflat[:, q * 2048:(q + 1) * 2048],
                                                                    in_=x1flat[:, hb_ * 2048:(hb_ + 1) * 2048], func=AF.Copy),
                         reads=[stgB[hb_]], writes=[wkvB])
                else:
                    S.op("dve", lambda e, q=q, hb_=hb_: e.tensor_copy(out=wkvflat[:, q * 2048:(q + 1) * 2048],
                                                                     in_=x1flat[:, hb_ * 2048:(hb_ + 1) * 2048]),
                         reads=[stgB[hb_]], writes=[wkvB])
            for b_ in xB[1]:
                b_.r = dict(stgB[0].r)
                b_.r.update(stgB[1].r)
            st32 = [sb1("st32_%d" % i, [128, 1024], F32) for i in range(2)]
            st32B = [Buf("st32_%d" % i) for i in range(2)]
            st16 = [sb1("st16_%d" % i, [128, 1024], BF16) for i in range(2)]
            st16B = [Buf("st16_%d" % i) for i in range(2)]
            NU = NWCH * 4
            cv_state = {"u": 0}

            def cv_load(u):
                S.dma("pool", lambda e, inc, u=u: inc(e.dma_start(out=st32[u % 2][:], in_=wq32[u // 4][:, (u % 4) * 1024:(u % 4 + 1) * 1024])),
                      st32B[u % 2], writes=[st32B[u % 2]])
            cv_load(0)
            cv_load(1)

            def cv_emit_until(n):
                while cv_state["u"] < min(n, NU):
                    u = cv_state["u"]
                    S.op("act", lambda e, u=u: e.activation(out=st16[u % 2][:], in_=st32[u % 2][:], func=AF.Copy),
                         reads=[st32B[u % 2]], writes=[st16B[u % 2]])
                    S.dma("pool", lambda e, inc, u=u: inc(e.dma_start(out=wq16[u // 4][:, (u % 4) * 1024:(u % 4 + 1) * 1024], in_=st16[u % 2][:])),
                          st16B[u % 2], reads=[st16B[u % 2]], writes=[wq16B[u // 4]])
                    if u + 2 < NU:
                        cv_load(u + 2)
                    cv_state["u"] += 1

            S.stop_if("setup")
            Gs = sb1("Gs", [128, NJ, 8], F32)
            GsB = Buf("Gs")
            sp_t = sb1("sp_t", [128, NJ, 4], F32)
            spB = Buf("sp")
            tot_s = sb1("tot_s", [128, NJ, 4], F32)
            totB = Buf("tot")
            arg1 = sb1("arg1", [128, NJ, 4], F32)
            arg1B = Buf("arg1")
            wsc = sb1("wsc", [128, NJ, 4], F32)
            wB = Buf("w")
            ss1 = sb1("ss1", [128, NJ], F32)
            ss1B = Buf("ss1")
            lnt = sb1("lnt", [128, NJ], F32)
            lntB = Buf("lnt")
            kw_m = [sb1("kw_m%d" % i, [128, NJ, 1024], BF16) for i in range(2)]
            kwmB = [[Buf("kwm%d_%d" % (i, j)) for j in range(NJ)] for i in range(2)]
            v_m = [sb1("v_m%d" % i, [128, NJ, 2048], BF16) for i in range(2)]
            vmB = [[Buf("vm%d_%d" % (i, j)) for j in range(NJ)] for i in range(2)]
            ntmp = sb1("ntmp", [128, 8], F32)
            ntmpB = Buf("ntmp")
            dec8 = sb1("dec8", [128, 8], F32)
            dec8B = Buf("dec8")

            hnT2 = sb1("hnT2", [128, 8, T], BF16)
            hnTs = [hnT, hnT2]
            hnTBs = [hnTB, Buf("hnT2")]

            def p1_stats(m):
                xb, xbB = xbuf[m % 2], xB[m % 2]
                gB = gateB[m]
                for j in range(NJ):
                    S.op("act", lambda e, xb=xb, j=j: e.activation(out=junk[:], in_=xb[:, j, :], func=AF.Square,
                                                                  accum_out=ss1[:, j:j + 1]),
                         reads=[xbB[j]], writes=[ss1B])
                S.op("act", lambda e: e.activation(out=lnt[:], in_=ss1[:], func=AF.Ln, bias=EPS, scale=1.0 / D),
                     reads=[ss1B], writes=[lntB])
                S.op("act", lambda e, m=m: e.activation(out=RSTD1[:, m * NJ:(m + 1) * NJ], in_=lnt[:], func=AF.Exp, scale=-0.5),
                     reads=[lntB], writes=[rstdB[m]])

            def p1_hn_tile(m, j, part=None):
                xb, xbB = xbuf[m % 2], xB[m % 2]
                hT, hTB = hnTs[m % 2], hnTBs[m % 2]
                if part in (None, 0):
                  S.op("dve", lambda e, xb=xb, j=j, m=m: e.scalar_tensor_tensor(
                    out=hn_bf[:], in0=xb[:, j, :], scalar=RSTD1[:, m * NJ + j:m * NJ + j + 1],
                    in1=cst[:, C_GMIX:C_GMIX + D], op0=ALU.mult, op1=ALU.mult),
                    reads=[xbB[j], rstdB[m], cstB], writes=[hnB])
                if part == 0:
                    return
                for half in range(2):
                    bk, bB = next_bank()

                    def tr(e, bk=bk, half=half):
                        ins = None
                        for i in range(4):
                            kc = half * 4 + i
                            ins = e.matmul(bk[:, i * 128:(i + 1) * 128], lhsT=hn_bf[:, kc * 128:(kc + 1) * 128],
                                           rhs=ident[:], start=True, stop=True)
                        return ins
                    S.op("pe", tr, reads=[hnB, constB], writes=[bB])
                    S.op("act", lambda e, bk=bk, half=half, j=j, hT=hT: e.activation(
                        out=hT[:, half * 4:half * 4 + 4, j * 128:(j + 1) * 128], in_=v3(bk[:, :], 4), func=AF.Copy),
                        reads=[bB], writes=[hTB])

            def p1_store_hns(m):
                hT, hTB = hnTs[m % 2], hnTBs[m % 2]
                S.dma("sp", lambda e, inc, m=m, hT=hT: inc(e.dma_start(out=hns[m], in_=hT[:].rearrange("p k t -> p (k t)"))),
                      hTB, reads=[hTB], writes=[hnsB[m]])

            def p1_gates(m):
                hT, hTB = hnTs[m % 2], hnTBs[m % 2]
                gB = gateB[m]
                for j in range(NJ):
                    bk, bB = next_bank()

                    def gj(e, bk=bk, j=j, hT=hT):
                        ins = None
                        for kc in range(8):
                            ins = e.matmul(bk[:, 0:8], lhsT=hT[:, kc, j * 128:(j + 1) * 128], rhs=wif[:, kc, :],
                                           start=(kc == 0), stop=(kc == 7))
                        return ins
                    S.op("pe", gj, reads=[hTB, wifB], writes=[bB])
                    S.op("dve", lambda e, bk=bk, j=j: e.tensor_tensor(out=Gs[:, j, :], in0=bk[:, 0:8],
                                                                      in1=cst[:, C_BG:C_BG + 8], op=ALU.add),
                         reads=[bB, cstB], writes=[GsB])
                S.op("act", lambda e: e.activation(out=sp_t[:], in_=Gs[:, :, 4:8], func=AF.Exp, scale=-1.0),
                     reads=[GsB], writes=[spB])
                S.op("act", lambda e: e.activation(out=sp_t[:], in_=sp_t[:], func=AF.Ln, bias=1.0, scale=1.0),
                     reads=[spB], writes=[spB])

            def p1_gates_b(m):
                gB = gateB[m]
                bk_cs, bB_cs = next_bank()

                def csj(e, bk=bk_cs):
                    ins = None
                    for j in range(NJ):
                        ins = e.matmul(bk[:, j * 4:(j + 1) * 4], lhsT=cst[:, C_U:C_U + 128], rhs=sp_t[:, j, :],
                                       start=True, stop=True)
                    for j in range(NJ):
                        ins = e.matmul(bk[:, 16 + j * 4:16 + (j + 1) * 4], lhsT=cst[:, C_ONES:C_ONES + 128],
                                       rhs=sp_t[:, j, :], start=True, stop=True)
                    return ins
                S.op("pe", csj, reads=[spB, cstB], writes=[bB_cs])
                S.op("act", lambda e, bk=bk_cs: e.activation(out=tot_s[:], in_=v3(bk[:, 16:32], NJ), func=AF.Copy),
                     reads=[bB_cs], writes=[totB])
                S.op("dve", lambda e, bk=bk_cs: e.tensor_tensor(out=arg1[:], in0=v3(bk[:, 0:16], NJ), in1=tot_s[:],
                                                                op=ALU.subtract),
                     reads=[bB_cs, totB], writes=[arg1B])
                S.op("act", lambda e, m=m: e.activation(out=EQ[:, m * NJ:(m + 1) * NJ, :], in_=arg1[:], func=AF.Exp),
                     reads=[arg1B], writes=[gB])
                S.op("act", lambda e, m=m: e.activation(out=EQ2[:, m * NJ:(m + 1) * NJ, :], in_=arg1[:], func=AF.Exp, scale=2.0),
                     reads=[arg1B], writes=[gB])
                S.op("dve", lambda e: e.tensor_tensor(out=arg1[:], in0=arg1[:], in1=Gs[:, :, 0:4], op=ALU.add),
                     reads=[arg1B, GsB], writes=[arg1B])
                S.op("act", lambda e: e.activation(out=wsc[:], in_=arg1[:], func=AF.Exp), reads=[arg1B], writes=[wB])
                S.op("act", lambda e, m=m: e.activation(out=DEC[:, m * NJ:(m + 1) * NJ, :], in_=tot_s[:], func=AF.Exp,
                                                       scale=-1.0), reads=[totB], writes=[gB])
                for j in range(NJ):
                    S.op("dve", lambda e, j=j: e.tensor_tensor(out=LD[:], in0=LD[:], in1=tot_s[:, j, :], op=ALU.add),
                         reads=[totB, LDB], writes=[LDB])

            def p1_proj_job(m, j, cb):
                hT, hTB = hnTs[m % 2], hnTBs[m % 2]
                kwt, kwtB = kw_m[m % 2], kwmB[m % 2]
                vt, vtB = v_m[m % 2], vmB[m % 2]
                bk, bB = next_bank()

                def pj(e, bk=bk, j=j, cb=cb, hT=hT):
                    ins = None
                    for kc in range(8):
                        ins = e.matmul(bk[:, :], lhsT=hT[:, kc, j * 128:(j + 1) * 128],
                                       rhs=wkv[:, kc, cb * 512:(cb + 1) * 512], start=(kc == 0), stop=(kc == 7))
                    return ins
                S.op("pe", pj, reads=[hTB, wkvB], writes=[bB])
                if cb < 2:
                    for hh in range(2):
                        h = cb * 2 + hh
                        S.op("act", lambda e, bk=bk, j=j, h=h, hh=hh, kwt=kwt: e.activation(
                            out=kwt[:, j, h * 256:(h + 1) * 256], in_=bk[:, hh * 256:(hh + 1) * 256], func=AF.Copy,
                            scale=wsc[:, j, h:h + 1]), reads=[bB, wB], writes=[kwtB[j]])
                else:
                    c0 = (cb - 2) * 512
                    if cb == 5:
                        S.op("act", lambda e, bk=bk, j=j, c0=c0, vt=vt: e.activation(out=vt[:, j, c0:c0 + 512], in_=bk[:, :], func=AF.Copy),
                             reads=[bB], writes=[vtB[j]])
                    else:
                        S.op("dve", lambda e, bk=bk, j=j, c0=c0, vt=vt: e.tensor_copy(out=vt[:, j, c0:c0 + 512], in_=bk[:, :]),
                             reads=[bB], writes=[vtB[j]])

            def p1_state_items(m, j):
                kwt, kwtB = kw_m[m % 2], kwmB[m % 2]
                vt, vtB = v_m[m % 2], vmB[m % 2]
                gB = gateB[m]
                ch = m * NJ + j
                items = []

                def it_n():
                    bkn, bBn = next_bank()

                    def nj(e, bk=bkn):
                        ins = None
                        for hc in range(8):
                            ins = e.matmul(bk[:, hc:hc + 1], lhsT=kwt[:, j, hc * 128:(hc + 1) * 128], rhs=ones_bf[:],
                                           start=True, stop=True)
                        return ins
                    S.op("pe", nj, reads=[kwtB[j], constB], writes=[bBn])
                    for h in range(H):
                        S.op("dve", lambda e, h=h: e.tensor_scalar(
                            out=nst[:, h * 2:h * 2 + 2], in0=nst[:, h * 2:h * 2 + 2], scalar1=DEC[:, ch, h:h + 1],
                            scalar2=None, op0=ALU.mult), reads=[nB, gB], writes=[nB])
                    S.op("dve", lambda e, bk=bkn: e.tensor_tensor(out=nst[:], in0=nst[:], in1=bk[:, 0:8], op=ALU.add),
                         reads=[nB, bBn], writes=[nB])
                items.append(it_n)
                for h in range(H):
                    for cc in range(2):
                        def it_s(h=h, cc=cc):
                            bk, bB = next_bank()
                            S.op("pe", lambda e, bk=bk: e.matmul(
                                bk[:, :], lhsT=kwt[:, j, h * 256 + cc * 128:h * 256 + (cc + 1) * 128],
                                rhs=vt[:, j, h * 512:(h + 1) * 512], start=True, stop=True),
                                reads=[kwtB[j], vtB[j]], writes=[bB])
                            S.op("dve", lambda e, bk=bk: e.scalar_tensor_tensor(
                                out=Cst[:, h * 1024 + cc * 512:h * 1024 + (cc + 1) * 512], in0=Cst[:, h * 1024 + cc * 512:h * 1024 + (cc + 1) * 512],
                                scalar=DEC[:, ch, h:h + 1], in1=bk[:, :], op0=ALU.mult, op1=ALU.add),
                                reads=[bB, gB, CB[h][cc]], writes=[CB[h][cc]])
                        items.append(it_s)
                return items

            p1_stats(0)
            for j in range(NJ):
                p1_hn_tile(0, j)
            p1_store_hns(0)
            ld_x(1)
            p1_stats(1)
            slot = 0
            for m in range(NMT):
                if m == 1:
                    S.stop_if("p1m0")
                if m + 2 < NMT:
                    ld_x(m + 2)
                p1_gates(m)
                items = []
                for j in range(NJ):
                    its = []
                    if m >= 1:
                        its += p1_state_items(m - 1, j)
                    if m + 1 < NMT:
                        its.insert(0, lambda m=m, j=j: p1_hn_tile(m + 1, j, part=0))
                        its.append(lambda m=m, j=j: p1_hn_tile(m + 1, j, part=1))
                    items += its
                porder = ([(0, cb) for cb in (2, 3, 4, 5)] + [(1, cb) for cb in (2, 3, 4, 5)] + ["GB"]
                          + [(0, 0), (0, 1), (1, 0), (1, 1)]
                          + [(2, cb) for cb in (2, 3, 4, 5, 0, 1)] + [(3, cb) for cb in (2, 3, 4, 5, 0, 1)])
                k = 0
                idx = 0
                for po in porder:
                    if po == "GB":
                        p1_gates_b(m)
                        continue
                    p1_proj_job(m, po[0], po[1])
                    idx += 1
                    tgt = (len(items) * idx + 23) // 24
                    while k < tgt:
                        items[k]()
                        k += 1
                    slot += 1
                    cv_emit_until((NU * slot) // 192)
                if m + 2 < NMT:
                    p1_stats(m + 2)
                if m == NMT - 1:
                    xhb, xhB = xbuf[1], xB[1][0]
                    S.dma("sp", lambda e, inc: inc(e.dma_start(out=xhb[:, 0, :], in_=x_prev[:, :])), xhB, writes=[xhB])
                    S.op("act", lambda e: e.activation(out=junk[:], in_=xhb[:, 0, :], func=AF.Square, accum_out=ss1[:, 0:1]),
                         reads=[xhB], writes=[ss1B])
                    S.op("act", lambda e: e.activation(out=lnt[:, 0:1], in_=ss1[:, 0:1], func=AF.Ln, bias=EPS, scale=1.0 / D),
                         reads=[ss1B], writes=[lntB])
                    S.op("act", lambda e: e.activation(out=lnt[:, 0:1], in_=lnt[:, 0:1], func=AF.Exp, scale=-0.5),
                         reads=[lntB], writes=[lntB])
                    S.op("dve", lambda e: e.scalar_tensor_tensor(out=hn_bf[:], in0=xhb[:, 0, :], scalar=lnt[:, 0:1],
                                                                 in1=cst[:, C_GMIX:C_GMIX + D], op0=ALU.mult, op1=ALU.mult),
                         reads=[xhB, lntB, cstB], writes=[hnB])
                    for half in range(2):
                        bk, bB = next_bank()

                        def trh(e, bk=bk, half=half):
                            ins = None
                            for i in range(4):
                                kc = half * 4 + i
                                ins = e.matmul(bk[:, i * 128:(i + 1) * 128], lhsT=hn_bf[:, kc * 128:(kc + 1) * 128],
                                               rhs=ident[:], start=True, stop=True)
                            return ins
                        S.op("pe", trh, reads=[hnB, constB], writes=[bB])
                        S.op("act", lambda e, bk=bk, half=half: e.activation(
                            out=hnT[:, half * 4:half * 4 + 4, 0:128], in_=v3(bk[:, :], 4), func=AF.Copy),
                            reads=[bB], writes=[hnTB])
                    S.op("dve", lambda e: e.tensor_copy(out=hT_h[:], in_=hnT[:, :, 126:128]), reads=[hnTB], writes=[hThB])
                    ld_x(0)
                    S.dma("sp", lambda e, inc: inc(e.dma_start(out=hnT[:].rearrange("p k t -> p (k t)"), in_=hns[0])),
                          hnTB, reads=[hnsB[0]], writes=[hnTB])
                kwt, kwtB = kw_m[m % 2], kwmB[m % 2]
                vt, vtB = v_m[m % 2], vmB[m % 2]
                if m == 0:
                    dbg("kw0", kwt[:], [128, NJ, 1024], BF16, kwtB)
                    dbg("v0", vt[:], [128, NJ, 2048], BF16, vtB)
                    dbg("EQ", EQ[:], [128, NCHK, 4], F32, [gateB[0]])
                    dbg("DEC", DEC[:], [128, NCHK, 4], F32, [gateB[0]])
                    dbg("Gs", Gs[:], [128, NJ, 8], F32, [GsB])
                S.dma("sp", lambda e, inc, kwt=kwt, m=m: inc(e.dma_start(
                    out=kws[m * T:(m + 1) * T, :].rearrange("(j p) d -> p j d", p=128), in_=kwt[:])), kwtB[0], reads=kwtB)
                S.dma("sp", lambda e, inc, vt=vt, m=m: inc(e.dma_start(
                    out=vs[m * T:(m + 1) * T, :].rearrange("(j p) d -> p j d", p=128), in_=vt[:])), vtB[0], reads=vtB)
                if m + 1 < NMT:
                    p1_store_hns(m + 1)
            for j in range(NJ):
                for it in p1_state_items(NMT - 1, j):
                    it()
            cv_emit_until(NU)

            S.stop_if("p1")
            ginB = Buf("gin")

            def st_gin(e, inc):
                inc(e.dma_start(out=gin_ts[0].ap()[:, :], in_=Cst[:, 0:2048]))
                inc(e.dma_start(out=gin_ts[1].ap()[:, :], in_=Cst[:, 2048:4096]))
                inc(e.dma_start(out=gin_ts[2].ap()[:, 0:8], in_=nst[:]))
                inc(e.dma_start(out=gin_ts[2].ap()[:, 8:12], in_=LD[:]))
                inc(e.dma_start(out=gin_ts[2].ap()[:, 12:16], in_=LD[:]))
            S.dma("pool", st_gin, ginB, reads=CBall + [nB, LDB], writes=[ginB], ndma=5)
            goutB = Buf("gout")
            waits = S._collect("pool", [ginB], [goutB], ())
            if not S.stopped:
                for i in range(3):
                    S.q["pool"].append((waits if i == 0 else [], lambda e, i=i: e.collective_compute(
                        "AllGather", ALU.bypass, replica_groups=[[0, 1, 2, 3], [4, 5, 6, 7]],
                        ins=[gin_ts[i].ap().opt()], outs=[gout_ts[i].ap().opt()]), (cc_sem, None)))
            cc_ev = Ev(cc_sem, 3)
            goutB.w = cc_ev
            S.barrier()

        S.stop_if("comb")
        p2 = contextlib.ExitStack()
        with p2:
            def sb2(name, shape, dt):
                return p2.enter_context(nc.sbuf_tensor(name, shape, dt))
            NWB = 3
            wring = [sb2("wring%d" % i, [128, 4096], BF16) for i in range(NWB)]
            wringB = [Buf("wring%d" % i) for i in range(NWB)]
            wr_state = {"n": 0}
            order = ([CH_Q, CH_Q + 1] + list(range(CH_GATE, CH_GATE + 8)) + list(range(CH_CONV, CH_CONV + 8))
                     + [CH_GA, CH_WA, CH_GA + 1, CH_WA + 1]
                     + [CH_GB, CH_WB, CH_WB + 1, CH_GB + 1, CH_WB + 2, CH_WB + 3]
                     + [CH_WO, CH_WO + 1, CH_WPG, CH_WPG + 1, CH_WP])
            stream = [c for _ in range(NMT) for c in order]
            pre = {"issued": 0}

            def issue_w():
                k = pre["issued"]
                if k >= len(stream):
                    return
                ci = stream[k]
                slot = k % NWB
                S.dma("sp", lambda e, inc, ci=ci, slot=slot: inc(e.dma_start(out=wring[slot][:], in_=wq16[ci])),
                      wringB[slot], reads=[wq16B[ci]], writes=[wringB[slot]])
                pre["issued"] += 1

            def get_w(expect):
                k = wr_state["n"]
                assert stream[k] == expect, (stream[k], expect)
                while pre["issued"] < k + 1:
                    issue_w()
                wr_state["n"] += 1
                return wring[k % NWB], wringB[k % NWB]

            def after_w(hold=0):
                k = wr_state["n"]
                while pre["issued"] < min(len(stream), k + NWB - 1 - hold):
                    issue_w()

            Cgp = [sb2("Cgp%d" % i, [128, 2048], F32) for i in range(1)]
            CgpB = [Buf("Cgp%d" % i) for i in range(1)]
            Cgs = sb2("Cgs", [128, 4, 16], F32)
            CgsB = Buf("Cgs")
            Eacc = sb2("Eacc", [128, 4, 4], F32)
            EB = Buf("E")
            coef = sb2("coef", [128, 4, 4], F32)
            coefB = Buf("coef")

            def emit_combine():
                def ld_s(e, inc):
                    for r in range(4):
                        inc(e.dma_start(out=Cgs[:, r, :], in_=gout_ts[2].ap()[r * 128:(r + 1) * 128, :]))
                S.dma("sp", ld_s, CgsB, reads=[goutB], writes=[CgsB], ndma=4)
                S.op("dve", lambda e: e.memset(Eacc[:], 0.0), writes=[EB])
                for r in range(4):
                    for i in range(4):
                        S.op("dve", lambda e, r=r, i=i: e.scalar_tensor_tensor(
                            out=Eacc[:, r, :], in0=Cgs[:, i, 8:12], scalar=cst[:, C_BTW + i * 4 + r:C_BTW + i * 4 + r + 1],
                            in1=Eacc[:, r, :], op0=ALU.mult, op1=ALU.add), reads=[CgsB, cstB, EB], writes=[EB])
                S.op("act", lambda e: e.activation(out=coef[:], in_=Eacc[:], func=AF.Exp, scale=-1.0), reads=[EB], writes=[coefB])
                for r in range(4):
                    S.op("dve", lambda e, r=r: e.tensor_scalar(out=coef[:, r, :], in0=coef[:, r, :],
                                                               scalar1=cst[:, C_SEL + r:C_SEL + r + 1], scalar2=None, op0=ALU.mult),
                         reads=[coefB, cstB], writes=[coefB])
                for h in range(H):
                    S.op("dve", lambda e, h=h: e.tensor_scalar(out=nst[:, h * 2:h * 2 + 2], in0=Cgs[:, 0, h * 2:h * 2 + 2],
                                                               scalar1=coef[:, 0, h:h + 1], scalar2=None, op0=ALU.mult),
                         reads=[CgsB, coefB], writes=[nB])
                    for r in range(1, 4):
                        S.op("dve", lambda e, h=h, r=r: e.scalar_tensor_tensor(
                            out=nst[:, h * 2:h * 2 + 2], in0=Cgs[:, r, h * 2:h * 2 + 2], scalar=coef[:, r, h:h + 1],
                            in1=nst[:, h * 2:h * 2 + 2], op0=ALU.mult, op1=ALU.add), reads=[CgsB, coefB, nB], writes=[nB])
                k = 0
                for half in range(2):
                    for r in range(4):
                        pb, pbB = Cgp[0], CgpB[0]
                        k += 1
                        S.dma("sp", lambda e, inc, pb=pb, half=half, r=r: inc(e.dma_start(
                            out=pb[:], in_=gout_ts[half].ap()[r * 128:(r + 1) * 128, :])), pbB, reads=[goutB], writes=[pbB])
                        for hh in range(2):
                            h = half * 2 + hh
                            if r == 0:
                                S.op("dve", lambda e, h=h, hh=hh, pb=pb: e.tensor_scalar(
                                    out=Cst[:, h * 1024:(h + 1) * 1024], in0=pb[:, hh * 1024:(hh + 1) * 1024],
                                    scalar1=coef[:, 0, h:h + 1], scalar2=None, op0=ALU.mult),
                                    reads=[pbB, coefB], writes=CB[h])
                            else:
                                S.op("dve", lambda e, h=h, hh=hh, pb=pb, r=r: e.scalar_tensor_tensor(
                                    out=Cst[:, h * 1024:(h + 1) * 1024], in0=pb[:, hh * 1024:(hh + 1) * 1024],
                                    scalar=coef[:, r, h:h + 1], in1=Cst[:, h * 1024:(h + 1) * 1024], op0=ALU.mult, op1=ALU.add),
                                    reads=[pbB, coefB] + CB[h], writes=CB[h])

            ya_inT = sb2("ya_inT", [128, 8, T], BF16)
            yainB = Buf("ya_in")
            mrg = sb2("mrg", [128, 8, T], BF16)
            mrgB = [Buf("mrg%d" % e) for e in range(8)]
            qT = sb2("qT", [128, 8, T], BF16)
            qTB = Buf("qT")
            gateT = sb2("gateT", [128, 16, T], BF16)
            gtB = [Buf("gateT%d" % d) for d in range(16)]
            hn2b = [hn_bf, sb2("hn_bf2", [128, 1024], BF16)]
            hn2bB = [hnB, Buf("hn_bf2")]
            hn2T = sb2("hn2T", [128, 8, T], BF16)
            hn2TB = Buf("hn2T")
            u_t = sb2("u_t", [128, T + 2], F32)
            uB = Buf("u")
            uh = sb2("uh", [128, 8, 2], F32)
            uhB = Buf("uh")
            t1 = [sb2("t1_%d" % i, [128, T], F32) for i in range(4)]
            t1B = [Buf("t1_%d" % i) for i in range(4)]
            kw_c = [sb2("kw_c%d" % i, [128, 1024], BF16) for i in range(2)]
            kwcB = [Buf("kwc%d" % i) for i in range(2)]
            v_c = [sb2("v_c%d" % i, [128, 2048], BF16) for i in range(2)]
            vcB = [Buf("vc%d" % i) for i in range(2)]
            kwT = sb2("kwT", [128, 8, 128], BF16)
            kwTB = Buf("kwT")
            scT = sb2("scT", [128, 512], BF16)
            scTB = Buf("scT")
            cdec = sb2("cdec", [128, 4096], BF16)
            cdecB = [Buf("cdec%d" % h) for h in range(H)]
            ndec = sb2("ndec", [128, 8], BF16)
            ndecB = Buf("ndec")
            hbh = sb2("hbh", [128, 2048], BF16)
            hbhB = Buf("hbh")
            sm = {k: sb2("sm_" + k, [128, 4], F32) for k in ("c", "rc", "ss", "t", "ln", "sc")}
            smB = {k: Buf("sm_" + k) for k in sm}
            ss2 = sb2("ss2", [128, NJ], F32)
            ss2B = Buf("ss2")
            rs2 = sb2("rs2", [128, NJ], F32)
            rs2B = Buf("rs2")
            p_t = sb2("p_t", [128, NJ, 256], F32)
            ptB = Buf("p_t")
            p_bf = sb2("p_bf", [128, NJ, 256], BF16)
            pbfB = Buf("p_bf")
            pT = sb2("pT", [128, 2, T], BF16)
            pTB = Buf("pT")
            out_evs = []


            def ld_p(m):
                S.dma("sp", lambda e, inc: inc(e.dma_start(
                    out=p_t[:], in_=p_seg[m * T:(m + 1) * T, :].rearrange("(j p) d -> p j d", p=128))), ptB, writes=[ptB])

            def ld_kv(g):
                kwc_, vc_ = kw_c[g % 2], v_c[g % 2]
                S.dma("sp", lambda e, inc: inc(e.dma_start(out=kwc_[:], in_=kws[g * L:(g + 1) * L, :])), kwcB[g % 2], writes=[kwcB[g % 2]])
                S.dma("sp", lambda e, inc: inc(e.dma_start(out=vc_[:], in_=vs[g * L:(g + 1) * L, :])), vcB[g % 2], writes=[vcB[g % 2]])

            def ld_hnT(m):
                S.dma("sp", lambda e, inc: inc(e.dma_start(out=hnT[:].rearrange("p k t -> p (k t)"), in_=hns[m])),
                      hnTB, reads=[hnsB[m]], writes=[hnTB])

            for m in range(NMT):
                xb, xbB = xbuf[m % 2], xB[m % 2]
                gB = gateB[m]
                if m == 0:
                    ld_p(0)
                if m == 0:
                    pass
                hn_jobs = []
                for j in range(NJ):
                    hn_jobs.append(j)
                if m == 0:
                    pass
                dbg("hnT", hnT[:], [128, 8, T], BF16, [hnTB])
                S.stop_if("p2B")
                def conv_chunk(c, m=m):
                    wt, wtB = get_w(CH_CONV + c)
                    w3 = v3(wt[:, :], 8)
                    if m == 0:
                        bk, bB = next_bank()

                        def hal(e, bk=bk, w3=w3):
                            ins = None
                            for g in range(2):
                                for kc in range(8):
                                    ins = e.matmul(bk[:, g * 2:g * 2 + 2], lhsT=w3[:, kc, g * 128:(g + 1) * 128],
                                                   rhs=hT_h[:, kc, :], start=(kc == 0), stop=(kc == 7))
                            return ins
                        S.op("pe", hal, reads=[wtB, hThB], writes=[bB])
                        S.op("act", lambda e, bk=bk: e.activation(out=t1[0][:, 0:2], in_=bk[:, 0:2], func=AF.Copy),
                             reads=[bB], writes=[t1B[0]])
                        S.op("dve", lambda e, bk=bk: e.tensor_tensor(out=u_t[:, 0:2], in0=t1[0][:, 0:2], in1=bk[:, 2:4],
                                                                     op=ALU.mult), reads=[bB, t1B[0]], writes=[uB])
                    else:
                        S.op("pool", lambda e, c=c: e.tensor_copy(out=u_t[:, 0:2], in_=uh[:, c, :]), reads=[uhB], writes=[uB])
                    bks = []
                    for g in range(4):
                        bk, bB = next_bank()

                        def cj(e, bk=bk, g=g, w3=w3):
                            ins = None
                            for kc in range(8):
                                ins = e.matmul(bk[:, :], lhsT=w3[:, kc, g * 128:(g + 1) * 128], rhs=hnT[:, kc, :],
                                               start=(kc == 0), stop=(kc == 7))
                            return ins
                        S.op("pe", cj, reads=[wtB, hnTB], writes=[bB])
                        bks.append((bk, bB))
                    after_w()
                    (bxa, Bxa), (bca, Bca), (bba, Bba), (bza, Bza) = bks
                    S.op("act", lambda e, bk=bxa: e.activation(out=t1[0][:], in_=bk[:, :], func=AF.Copy),
                         reads=[Bxa], writes=[t1B[0]])
                    S.op("dve", lambda e, bk=bca: e.tensor_tensor(out=u_t[:, 2:T + 2], in0=t1[0][:], in1=bk[:, :], op=ALU.mult),
                         reads=[Bca, t1B[0]], writes=[uB])
                    S.op("pool", lambda e, c=c: e.tensor_copy(out=uh[:, c, :], in_=u_t[:, T:T + 2]), reads=[uB], writes=[uhB])
                    S.op("act", lambda e, c=c: e.activation(out=t1[1][:], in_=u_t[:, 0:T], func=AF.Identity,
                                                            bias=cst[:, C_CB + c:C_CB + c + 1],
                                                            scale=cst[:, C_CW + c * 3:C_CW + c * 3 + 1]),
                         reads=[uB, cstB], writes=[t1B[1]])
                    S.op("dve", lambda e, c=c: e.scalar_tensor_tensor(out=t1[1][:], in0=u_t[:, 1:T + 1],
                                                                      scalar=cst[:, C_CW + c * 3 + 1:C_CW + c * 3 + 2],
                                                                      in1=t1[1][:], op0=ALU.mult, op1=ALU.add),
                         reads=[uB, cstB, t1B[1]], writes=[t1B[1]])
                    S.op("dve", lambda e, c=c: e.scalar_tensor_tensor(out=t1[1][:], in0=u_t[:, 2:T + 2],
                                                                      scalar=cst[:, C_CW + c * 3 + 2:C_CW + c * 3 + 3],
                                                                      in1=t1[1][:], op0=ALU.mult, op1=ALU.add),
                         reads=[uB, cstB, t1B[1]], writes=[t1B[1]])
                    S.op("dve", lambda e, bk=bba: e.tensor_tensor(out=t1[2][:], in0=t1[1][:], in1=bk[:, :], op=ALU.mult),
                         reads=[Bba, t1B[1]], writes=[t1B[2]])
                    S.op("act", lambda e, bk=bza: e.activation(out=t1[3][:], in_=bk[:, :], func=AF.Silu),
                         reads=[Bza], writes=[t1B[3]])
                    S.op("pool", lambda e, c=c: e.tensor_tensor(out=ya_inT[:, c, :], in0=t1[2][:], in1=t1[3][:], op=ALU.mult),
                         reads=[t1B[2], t1B[3]], writes=[yainB])

                for i in range(2):
                    wt, wtB = get_w(CH_Q + i)
                    w3 = v3(wt[:, :], 8)
                    for s4 in range(4):
                        hc = i * 4 + s4
                        bk, bB = next_bank()

                        def qj(e, bk=bk, w3=w3, s4=s4):
                            ins = None
                            for kc in range(8):
                                ins = e.matmul(bk[:, :], lhsT=w3[:, kc, s4 * 128:(s4 + 1) * 128], rhs=hnT[:, kc, :],
                                               start=(kc == 0), stop=(kc == 7))
                            return ins
                        S.op("pe", qj, reads=[wtB, hnTB], writes=[bB])
                        S.op("act", lambda e, bk=bk, hc=hc: e.activation(out=qT[:, hc, :], in_=bk[:, :], func=AF.Copy,
                                                                         scale=DK ** -0.5), reads=[bB], writes=[qTB])
                    after_w()

                S.op("pool", lambda e: e.tensor_copy(out=p_bf[:], in_=p_t[:]), reads=[ptB], writes=[pbfB])
                if m + 1 < NMT:
                    ld_p(m + 1)
                for kc2 in range(2):
                    bk, bB = next_bank()

                    def trp(e, bk=bk, kc2=kc2):
                        ins = None
                        for j in range(NJ):
                            ins = e.matmul(bk[:, j * 128:(j + 1) * 128], lhsT=p_bf[:, j, kc2 * 128:(kc2 + 1) * 128],
                                           rhs=ident[:], start=True, stop=True)
                        return ins
                    S.op("pe", trp, reads=[pbfB, constB], writes=[bB])
                    S.op("act", lambda e, bk=bk, kc2=kc2: e.activation(out=pT[:, kc2, :], in_=bk[:, :], func=AF.Copy),
                         reads=[bB], writes=[pTB])

                for i in range(8):
                    wt, wtB = get_w(CH_GATE + i)
                    w3 = v3(wt[:, :], 8)
                    for dd in range(2):
                        d = i * 2 + dd
                        bko, bBo = next_bank()

                        def oj(e, bk=bko, w3=w3, dd=dd):
                            ins = None
                            for kc in range(8):
                                ins = e.matmul(bk[:, :], lhsT=w3[:, kc, dd * 256:dd * 256 + 128], rhs=hnT[:, kc, :],
                                               start=(kc == 0), stop=(kc == 7))
                            return ins
                        S.op("pe", oj, reads=[wtB, hnTB], writes=[bBo])
                        bkz, bBz = next_bank()

                        def zj(e, bk=bkz, w3=w3, dd=dd):
                            ins = None
                            for kc in range(8):
                                ins = e.matmul(bk[:, :], lhsT=w3[:, kc, dd * 256 + 128:dd * 256 + 256], rhs=hnT[:, kc, :],
                                               start=(kc == 0), stop=(kc == 7))
                            return ins
                        S.op("pe", zj, reads=[wtB, hnTB], writes=[bBz])
                        ta, tb = (2, 3) if d % 2 == 0 else (0, 1)
                        S.op("act", lambda e, bk=bko, ta=ta: e.activation(out=t1[ta][:], in_=bk[:, :], func=AF.Tanh, scale=0.5),
                             reads=[bBo], writes=[t1B[ta]])
                        S.op("act", lambda e, bk=bkz, tb=tb: e.activation(out=t1[tb][:], in_=bk[:, :], func=AF.Silu),
                             reads=[bBz], writes=[t1B[tb]])
                        S.op("dve", lambda e, ta=ta, tb=tb: e.scalar_tensor_tensor(out=t1[ta][:], in0=t1[ta][:], scalar=1.0, in1=t1[tb][:],
                                                                                   op0=ALU.add, op1=ALU.mult),
                             reads=[t1B[ta], t1B[tb]], writes=[t1B[ta]])
                        S.op("pool", lambda e, d=d, ta=ta: e.tensor_scalar(out=gateT[:, d, :], in0=t1[ta][:],
                                                                           scalar1=cst[:, C_GHEAD + d:C_GHEAD + d + 1], scalar2=0.5,
                                                                           op0=ALU.mult, op1=ALU.mult),
                             reads=[t1B[ta], cstB], writes=[gtB[d]])
                    after_w()

                dbg("qT", qT[:], [128, 8, T], BF16, [qTB])
                dbg("gateF", gateT[:], [128, 16, T], BF16, gtB)
                S.stop_if("p2F")
                if m + 1 < NMT:
                    ld_x(m + 1)
                if m == 0:
                    ld_kv(0)
                    emit_combine()
                for j in range(NJ):
                    ch = m * NJ + j
                    tok0 = m * T + j * L
                    kwc, kwB_ = kw_c[ch % 2], kwcB[ch % 2]
                    vc, vB_ = v_c[ch % 2], vcB[ch % 2]
                    if ch + 1 < NCHK:
                        ld_kv(ch + 1)
                    for half in range(2):
                        bk, bB = next_bank()

                        def trk(e, bk=bk, half=half, kwc=kwc):
                            ins = None
                            for i in range(4):
                                hc = half * 4 + i
                                ins = e.matmul(bk[:, i * 128:(i + 1) * 128], lhsT=kwc[:, hc * 128:(hc + 1) * 128],
                                               rhs=ident[:], start=True, stop=True)
                            return ins
                        S.op("pe", trk, reads=[kwB_, constB], writes=[bB])
                        S.op("act", lambda e, bk=bk, half=half: e.activation(
                            out=kwT[:, half * 4:half * 4 + 4, :], in_=v3(bk[:, :], 4), func=AF.Copy),
                            reads=[bB], writes=[kwTB])
                    for h in range(H):
                        S.op("act", lambda e, h=h, ch=ch: e.activation(out=cdec[:, h * 1024:(h + 1) * 1024], in_=Cst[:, h * 1024:(h + 1) * 1024],
                                                                       func=AF.Copy, scale=DEC[:, ch, h:h + 1]),
                             reads=[CB[h][0], CB[h][1], gB], writes=[cdecB[h]])
                        S.op("dve", lambda e, h=h, ch=ch: e.tensor_scalar(out=ndec[:, h * 2:h * 2 + 2], in0=nst[:, h * 2:h * 2 + 2],
                                                                          scalar1=DEC[:, ch, h:h + 1], scalar2=None,
                                                                          op0=ALU.mult), reads=[nB, gB], writes=[ndecB])
                    bks_, bBs = next_bank()

                    def sj(e, bk=bks_, j=j):
                        ins = None
                        for h in range(H):
                            for cc in range(2):
                                ins = e.matmul(bk[:, h * 128:(h + 1) * 128], lhsT=kwT[:, h * 2 + cc, :],
                                               rhs=qT[:, h * 2 + cc, j * 128:(j + 1) * 128], start=(cc == 0), stop=(cc == 1))
                        return ins
                    S.op("pe", sj, reads=[kwTB, qTB], writes=[bBs])
                    conv_chunk(2 * j)
                    S.op("dve", lambda e, bk=bks_: e.tensor_tensor(out=scT[:], in0=bk[:, :], in1=mask4[:], op=ALU.mult),
                         reads=[bBs, constB], writes=[scTB])
                    bkd, bBd = next_bank()

                    def dj(e, bk=bkd, j=j):
                        ins = None
                        for h in range(H):
                            ins = e.matmul(bk[:, h:h + 1], lhsT=scT[:, h * 128:(h + 1) * 128], rhs=ones_bf[:], start=True, stop=False)
                            for cc in range(2):
                                ins = e.matmul(bk[:, h:h + 1], lhsT=qT[:, h * 2 + cc, j * 128:(j + 1) * 128],
                                               rhs=ndec[:, h * 2 + cc:h * 2 + cc + 1], start=False, stop=(cc == 1))
                        return ins
                    S.op("pe", dj, reads=[scTB, qTB, ndecB, constB], writes=[bBd])
                    S.op("act", lambda e, bk=bkd: e.activation(out=sm["t"][:], in_=bk[:, 0:4], func=AF.Square),
                         reads=[bBd], writes=[smB["t"]])
                    S.op("dve", lambda e, ch=ch: e.tensor_tensor(out=sm["c"][:], in0=sm["t"][:], in1=EQ2[:, ch, :], op=ALU.max),
                         reads=[smB["t"], gB], writes=[smB["c"]])
                    nbk = []
                    for h in range(H):
                        bk, bB = next_bank()

                        def numj(e, bk=bk, h=h, j=j, vc=vc):
                            e.matmul(bk[:, :], lhsT=scT[:, h * 128:(h + 1) * 128], rhs=vc[:, h * 512:(h + 1) * 512],
                                     start=True, stop=False)
                            ins = None
                            for cc in range(2):
                                ins = e.matmul(bk[:, :], lhsT=qT[:, h * 2 + cc, j * 128:(j + 1) * 128],
                                               rhs=cdec[:, h * 1024 + cc * 512:h * 1024 + (cc + 1) * 512], start=False, stop=(cc == 1))
                            return ins
                        S.op("pe", numj, reads=[scTB, vB_, qTB, cdecB[h]], writes=[bB])
                        S.op("act", lambda e, bk=bk, h=h: e.activation(out=junk[:, 0:512], in_=bk[:, :], func=AF.Square,
                                                                       accum_out=sm["ss"][:, h:h + 1]),
                             reads=[bB], writes=[smB["ss"]])
                        nbk.append((bk, bB))
                    S.op("dve", lambda e: e.scalar_tensor_tensor(out=sm["rc"][:], in0=sm["c"][:], scalar=float(EPS * DV), in1=sm["ss"][:],
                                                                 op0=ALU.mult, op1=ALU.add),
                         reads=[smB["c"], smB["ss"]], writes=[smB["rc"]])
                    S.op("act", lambda e: e.activation(out=sm["ln"][:], in_=sm["rc"][:], func=AF.Ln, scale=1.0 / DV),
                         reads=[smB["rc"]], writes=[smB["ln"]])
                    S.op("act", lambda e: e.activation(out=sm["sc"][:], in_=sm["ln"][:], func=AF.Exp, scale=-0.5),
                         reads=[smB["ln"]], writes=[smB["sc"]])
                    for h in range(H):
                        bk, bB = nbk[h]
                        S.op("act", lambda e, bk=bk, h=h: e.activation(out=hbh[:, h * 512:(h + 1) * 512], in_=bk[:, :], func=AF.Copy,
                                                                       scale=sm["sc"][:, h:h + 1]),
                             reads=[bB, smB["sc"]], writes=[hbhB])
                    conv_chunk(2 * j + 1)
                    for q4 in range(4):
                        bk, bB = next_bank()

                        def trh2(e, bk=bk, q4=q4):
                            ins = None
                            for i in range(4):
                                d = q4 * 4 + i
                                ins = e.matmul(bk[:, i * 128:(i + 1) * 128], lhsT=hbh[:, d * 128:(d + 1) * 128],
                                               rhs=ident[:], start=True, stop=True)
                            return ins
                        S.op("pe", trh2, reads=[hbhB, constB], writes=[bB])
                        S.op("dve", lambda e, bk=bk, q4=q4, j=j: e.tensor_tensor(
                            out=gateT[:, q4 * 4:q4 * 4 + 4, j * 128:(j + 1) * 128], in0=v3(bk[:, :], 4),
                            in1=gateT[:, q4 * 4:q4 * 4 + 4, j * 128:(j + 1) * 128], op=ALU.mult),
                            reads=[bB] + gtB[q4 * 4:q4 * 4 + 4], writes=gtB[q4 * 4:q4 * 4 + 4])
                    bkn, bBn = next_bank()

                    def nj2(e, bk=bkn, kwc=kwc):
                        ins = None
                        for hc in range(8):
                            ins = e.matmul(bk[:, hc:hc + 1], lhsT=kwc[:, hc * 128:(hc + 1) * 128], rhs=ones_bf[:],
                                           start=True, stop=True)
                        return ins
                    S.op("pe", nj2, reads=[kwB_, constB], writes=[bBn])
                    for h in range(H):
                        S.op("dve", lambda e, h=h, ch=ch: e.tensor_scalar(
                            out=nst[:, h * 2:h * 2 + 2], in0=nst[:, h * 2:h * 2 + 2], scalar1=DEC[:, ch, h:h + 1],
                            scalar2=None, op0=ALU.mult), reads=[nB, gB], writes=[nB])
                    S.op("dve", lambda e, bk=bkn: e.tensor_tensor(out=nst[:], in0=nst[:], in1=bk[:, 0:8], op=ALU.add),
                         reads=[nB, bBn], writes=[nB])
                    for h in range(H):
                        for cc in range(2):
                            bk, bB = next_bank()
                            S.op("pe", lambda e, bk=bk, h=h, cc=cc, kwc=kwc, vc=vc: e.matmul(
                                bk[:, :], lhsT=kwc[:, h * 256 + cc * 128:h * 256 + (cc + 1) * 128],
                                rhs=vc[:, h * 512:(h + 1) * 512], start=True, stop=True),
                                reads=[kwB_, vB_], writes=[bB])
                            S.op("dve", lambda e, bk=bk, h=h, cc=cc, ch=ch: e.scalar_tensor_tensor(
                                out=Cst[:, h * 1024 + cc * 512:h * 1024 + (cc + 1) * 512], in0=Cst[:, h * 1024 + cc * 512:h * 1024 + (cc + 1) * 512],
                                scalar=DEC[:, ch, h:h + 1], in1=bk[:, :], op0=ALU.mult, op1=ALU.add),
                                reads=[bB, gB, CB[h][cc]], writes=[CB[h][cc]])

                dbg("ya_inT", ya_inT[:], [128, 8, T], BF16, [yainB])
                S.stop_if("p2C")
                for e8 in range(8):
                    if e8 % 4 == 0:
                        if e8:
                            after_w()
                        wg, wgB = get_w(CH_GA + e8 // 4)
                        wa, waB = get_w(CH_WA + e8 // 4)
                    wg3, wa3 = v3(wg[:, :], 8), v3(wa[:, :], 8)
                    co = (e8 % 4) * 128
                    bkg, bBg = next_bank()

                    def gaj(e, bk=bkg, wg3=wg3, co=co):
                        ins = None
                        for kc in range(8):
                            ins = e.matmul(bk[:, :], lhsT=wg3[:, kc, co:co + 128], rhs=hnT[:, kc, :], start=(kc == 0), stop=(kc == 7))
                        return ins
                    S.op("pe", gaj, reads=[wgB, hnTB], writes=[bBg])
                    bka, bBa = next_bank()

                    def yaj(e, bk=bka, wa3=wa3, co=co):
                        ins = None
                        for kc in range(8):
                            ins = e.matmul(bk[:, :], lhsT=wa3[:, kc, co:co + 128], rhs=ya_inT[:, kc, :], start=(kc == 0), stop=(kc == 7))
                        return ins
                    S.op("pe", yaj, reads=[waB, yainB], writes=[bBa])
                    ta = (e8 % 4)
                    S.op("act", lambda e, bk=bkg, ta=ta: e.activation(out=t1[ta][:], in_=bk[:, :], func=AF.Tanh, scale=0.5),
                         reads=[bBg], writes=[t1B[ta]])
                    S.op("dve", lambda e, bk=bka, e8=e8, ta=ta: e.scalar_tensor_tensor(out=mrg[:, e8, :], in0=t1[ta][:], scalar=1.0,
                                                                                       in1=bk[:, :], op0=ALU.add, op1=ALU.mult),
                         reads=[bBa, t1B[ta]], writes=[mrgB[e8]])
                after_w()

                dbg("mrgD", mrg[:], [128, 8, T], BF16, mrgB)
                dbg("gateG", gateT[:], [128, 16, T], BF16, gtB)
                dbg("Cst", Cst[:], [128, 4096], F32, CBall)
                S.stop_if("p2G")
                for i in range(4):
                    if i % 2 == 0:
                        wg, wgB = get_w(CH_GB + i // 2)
                    wt, wtB = get_w(CH_WB + i)
                    wb3 = v3(wt[:, :], 16)
                    for ee in range(2):
                        e8 = i * 2 + ee
                        wg3 = v3(wg[:, :], 8)
                        co = (e8 % 4) * 128
                        bkg, bBg = next_bank()

                        def gbj(e, bk=bkg, wg3=wg3, co=co):
                            ins = None
                            for kc in range(8):
                                ins = e.matmul(bk[:, :], lhsT=wg3[:, kc, co:co + 128], rhs=hnT[:, kc, :], start=(kc == 0), stop=(kc == 7))
                            return ins
                        S.op("pe", gbj, reads=[wgB, hnTB], writes=[bBg])
                        bky, bBy = next_bank()

                        def ybj(e, bk=bky, wb3=wb3, ee=ee):
                            ins = None
                            for kc in range(16):
                                ins = e.matmul(bk[:, :], lhsT=wb3[:, kc, ee * 128:(ee + 1) * 128], rhs=gateT[:, kc, :],
                                               start=(kc == 0), stop=(kc == 15))
                            return ins
                        S.op("pe", ybj, reads=[wtB] + gtB, writes=[bBy])
                        ta, tb = (0, 1) if e8 % 2 == 0 else (2, 3)
                        S.op("act", lambda e, bk=bkg, ta=ta: e.activation(out=t1[ta][:], in_=bk[:, :], func=AF.Tanh, scale=0.5),
                             reads=[bBg], writes=[t1B[ta]])
                        S.op("dve", lambda e, bk=bky, ta=ta, tb=tb: e.scalar_tensor_tensor(out=t1[tb][:], in0=t1[ta][:], scalar=1.0, in1=bk[:, :],
                                                                                           op0=ALU.add, op1=ALU.mult),
                             reads=[bBy, t1B[ta]], writes=[t1B[tb]])
                        S.op("pool", lambda e, e8=e8, tb=tb: e.tensor_tensor(out=mrg[:, e8, :], in0=mrg[:, e8, :], in1=t1[tb][:], op=ALU.add),
                             reads=[t1B[tb], mrgB[e8]], writes=[mrgB[e8]])
                    after_w(hold=1 if i % 2 == 0 else 0)

                dbg("mrgH", mrg[:], [128, 8, T], BF16, mrgB)
                S.stop_if("p2H")
                if m + 1 < NMT:
                    ld_hnT(m + 1)
                wow = [get_w(CH_WO), get_w(CH_WO + 1)]
                ss2j = [Buf("ss2_%d" % j) for j in range(NJ)]
                rs2j = [Buf("rs2_%d" % j) for j in range(NJ)]
                for jj in range(NJ + 1):
                    if jj < NJ:
                        j = jj
                        for cb in range(2):
                            wo, woB = wow[cb]
                            wo3 = v3(wo[:, :], 8)
                            bk, bB = next_bank()

                            def x2j(e, bk=bk, wo3=wo3, j=j):
                                ins = None
                                for kc in range(8):
                                    ins = e.matmul(bk[:, :], lhsT=mrg[:, kc, j * 128:(j + 1) * 128], rhs=wo3[:, kc, :],
                                                   start=(kc == 0), stop=(kc == 7))
                                return ins
                            S.op("pe", x2j, reads=[woB] + mrgB, writes=[bB])
                            S.op("dve", lambda e, bk=bk, xb=xb, j=j, cb=cb: e.scalar_tensor_tensor(
                                out=xb[:, j, cb * 512:(cb + 1) * 512], in0=bk[:, :], scalar=0.5, in1=xb[:, j, cb * 512:(cb + 1) * 512],
                                op0=ALU.mult, op1=ALU.add), reads=[bB, xbB[j]], writes=[xbB[j]])
                        if jj == NJ - 1:
                            after_w()
                        S.op("act", lambda e, xb=xb, j=j: e.activation(out=junk[:], in_=xb[:, j, :], func=AF.Square,
                                                                      accum_out=ss2[:, j:j + 1]),
                             reads=[xbB[j]], writes=[ss2j[j]])
                        S.op("act", lambda e, j=j: e.activation(out=rs2[:, j:j + 1], in_=ss2[:, j:j + 1], func=AF.Ln, bias=EPS, scale=1.0 / D),
                             reads=[ss2j[j]], writes=[rs2j[j]])
                        S.op("act", lambda e, j=j: e.activation(out=rs2[:, j:j + 1], in_=rs2[:, j:j + 1], func=AF.Exp, scale=-0.5),
                             reads=[rs2j[j]], writes=[rs2j[j]])
                        S.op("dve", lambda e, xb=xb, j=j: e.scalar_tensor_tensor(
                            out=hn2b[j % 2][:], in0=xb[:, j, :], scalar=rs2[:, j:j + 1], in1=cst[:, C_GPLE:C_GPLE + D],
                            op0=ALU.mult, op1=ALU.mult), reads=[xbB[j], rs2j[j], cstB], writes=[hn2bB[j % 2]])
                    if jj >= 1:
                        j = jj - 1
                        for half in range(2):
                            bk, bB = next_bank()

                            def tr2(e, bk=bk, half=half, j=j):
                                ins = None
                                for i in range(4):
                                    kc = half * 4 + i
                                    ins = e.matmul(bk[:, i * 128:(i + 1) * 128], lhsT=hn2b[j % 2][:, kc * 128:(kc + 1) * 128],
                                                   rhs=ident[:], start=True, stop=True)
                                return ins
                            S.op("pe", tr2, reads=[hn2bB[j % 2], constB], writes=[bB])
                            S.op("act", lambda e, bk=bk, half=half, j=j: e.activation(
                                out=hn2T[:, half * 4:half * 4 + 4, j * 128:(j + 1) * 128], in_=v3(bk[:, :], 4), func=AF.Copy),
                                reads=[bB], writes=[hn2TB])

                dbg("x2", xb[:], [128, NJ, 1024], F32, xbB)
                wpgw = [get_w(CH_WPG), get_w(CH_WPG + 1)]
                wp, wpB = get_w(CH_WP)
                wp3 = wp[:, 0:2048].rearrange("p (a b) -> p a b", a=2)
                for j in range(NJ):
                    for cb in range(2):
                        wg, wgB = wpgw[cb]
                        wg3 = v3(wg[:, :], 8)
                        bkg, bBg = next_bank()

                        def pgj(e, bk=bkg, wg3=wg3, j=j):
                            ins = None
                            for kc in range(8):
                                ins = e.matmul(bk[:, :], lhsT=hn2T[:, kc, j * 128:(j + 1) * 128], rhs=wg3[:, kc, :],
                                               start=(kc == 0), stop=(kc == 7))
                            return ins
                        S.op("pe", pgj, reads=[wgB, hn2TB], writes=[bBg])
                        bkp, bBp = next_bank()

                        def pwj(e, bk=bkp, j=j, cb=cb):
                            ins = None
                            for kc in range(2):
                                ins = e.matmul(bk[:, :], lhsT=pT[:, kc, j * 128:(j + 1) * 128], rhs=wp3[:, kc, cb * 512:(cb + 1) * 512],
                                               start=(kc == 0), stop=(kc == 1))
                            return ins
                        S.op("pe", pwj, reads=[wpB, pTB], writes=[bBp])
                        ta, tb = (0, 1) if cb == 0 else (2, 3)
                        S.op("act", lambda e, bk=bkg, ta=ta: e.activation(out=t1[ta][:], in_=bk[:, :], func=AF.Tanh, scale=0.5),
                             reads=[bBg], writes=[t1B[ta]])
                        S.op("dve", lambda e, bk=bkp, ta=ta, tb=tb: e.scalar_tensor_tensor(out=t1[tb][:], in0=t1[ta][:], scalar=1.0, in1=bk[:, :],
                                                                                           op0=ALU.add, op1=ALU.mult),
                             reads=[bBp, t1B[ta]], writes=[t1B[tb]])
                        S.op("dve", lambda e, xb=xb, j=j, cb=cb, tb=tb: e.scalar_tensor_tensor(
                            out=xb[:, j, cb * 512:(cb + 1) * 512], in0=t1[tb][:], scalar=0.5, in1=xb[:, j, cb * 512:(cb + 1) * 512],
                            op0=ALU.mult, op1=ALU.add), reads=[t1B[tb], xbB[j]], writes=[xbB[j]])
                after_w()

                dbg("x3", xb[:], [128, NJ, 1024], F32, xbB)
                S.stop_if("p2J")
                for j in range(NJ):
                    S.op("act", lambda e, xb=xb, j=j: e.activation(out=junk[:], in_=xb[:, j, :], func=AF.Square,
                                                                  accum_out=ss2[:, j:j + 1]),
                         reads=[xbB[j]], writes=[ss2B])
                S.op("act", lambda e: e.activation(out=rs2[:], in_=ss2[:], func=AF.Ln, bias=EPS, scale=1.0 / D),
                     reads=[ss2B], writes=[rs2B])
                S.op("act", lambda e: e.activation(out=rs2[:], in_=rs2[:], func=AF.Exp, scale=-0.5), reads=[rs2B], writes=[rs2B])
                for j in range(NJ):
                    S.op("dve", lambda e, xb=xb, j=j: e.scalar_tensor_tensor(
                        out=xb[:, j, :], in0=xb[:, j, :], scalar=rs2[:, j:j + 1], in1=cst[:, C_GFIN:C_GFIN + D],
                        op0=ALU.mult, op1=ALU.mult), reads=[xbB[j], rs2B, cstB], writes=[xbB[j]])
                ev = S.dma("sp", lambda e, inc, xb=xb, m=m: inc(e.dma_start(
                    out=out_d[m * T:(m + 1) * T, :].rearrange("(j p) d -> p j d", p=128), in_=xb[:])), xbB[0], reads=xbB)
                out_evs.append(ev)

            S.stop_if("m0") if False else None
            S.wait_only("sp", out_evs)

            with nc.Block() as block:
                @block.tensor
                def _(pe):
                    S.emit("pe", pe)

                @block.scalar
                def _(act):
                    S.emit("act", act)

                @block.vector
                def _(dve):
                    S.emit("dve", dve)

                @block.gpsimd
                def _(pool):
                    S.emit("pool", pool)

                @block.sync
                def _(sp):
                    S.emit("sp", sp)
    return nc


def _chunk(W, cols, kc):
    a = W[:, cols]
    n = a.shape[1]
    a = a.reshape(kc, 128, n).transpose(1, 0, 2).reshape(128, kc * n)
    out = np.zeros((128, 4096), np.float32)
    out[:, :kc * n] = a
    return out


_NC_CACHE = {}


def kernel(x, p, g_mix, w_in, conv_w, conv_b, w_a_out, b_gates, g_head, w_b_out, w_o, g_ple,
           w_ple_gate, w_ple, g_final):
    x = np.asarray(x, np.float32)
    p = np.asarray(p, np.float32)
    w_in0 = np.asarray(w_in, np.float32)[0]
    r = np.arange
    OXA, OBA, OCA, OZA, OQ, OK_, OV, OO, OZB, OIG, OGA, OGB = 0, 1024, 2048, 3072, 4096, 5120, 6144, 8192, 10240, 12288, 12296, 13320
    chunks = []
    for c in range(8):
        cols = np.concatenate([OXA + c * 128 + r(128), OCA + c * 128 + r(128), OBA + c * 128 + r(128), OZA + c * 128 + r(128)])
        chunks.append(_chunk(w_in0, cols, 8))
    for i in range(2):
        chunks.append(_chunk(w_in0, OQ + i * 512 + r(512), 8))
    for i in range(8):
        d0, d1 = 2 * i, 2 * i + 1
        cols = np.concatenate([OO + d0 * 128 + r(128), OZB + d0 * 128 + r(128), OO + d1 * 128 + r(128), OZB + d1 * 128 + r(128)])
        chunks.append(_chunk(w_in0, cols, 8))
    for i in range(2):
        chunks.append(_chunk(w_in0, OGA + i * 512 + r(512), 8))
    wa = np.asarray(w_a_out, np.float32)[0]
    for i in range(2):
        chunks.append(_chunk(wa, i * 512 + r(512), 8))
    for i in range(2):
        chunks.append(_chunk(w_in0, OGB + i * 512 + r(512), 8))
    wb = np.asarray(w_b_out, np.float32)[0]
    for i in range(4):
        chunks.append(_chunk(wb, i * 256 + r(256), 16))
    wo = np.asarray(w_o, np.float32)[0]
    for i in range(2):
        chunks.append(_chunk(wo, i * 512 + r(512), 8))
    wpg = np.asarray(w_ple_gate, np.float32)[0]
    for i in range(2):
        chunks.append(_chunk(wpg, i * 512 + r(512), 8))
    wp = np.asarray(w_ple, np.float32)[0]
    chunks.append(_chunk(wp, r(1024), 2))
    wq32 = np.ascontiguousarray(np.stack(chunks, 0))
    assert wq32.shape == (NWCH, 128, 4096)
    wkv32 = np.ascontiguousarray(w_in0[:, OK_:OK_ + 3072].reshape(8, 128, 3072).transpose(1, 0, 2).reshape(128, 8 * 3072))
    wif32 = np.ascontiguousarray(w_in0[:, OIG:OIG + 8].reshape(8, 128, 8).transpose(1, 0, 2).reshape(128, 64))

    cst = np.zeros((NCORES, 128, NCST), np.float32)
    cst[:, :, C_ID:C_ID + 128] = np.eye(128, dtype=np.float32)
    tri = (np.arange(128)[:, None] <= np.arange(128)[None, :]).astype(np.float32)
    cst[:, :, C_U:C_U + 128] = tri
    cst[:, :, C_ONES:C_ONES + 128] = 1.0
    cst[:, :, C_MASK:C_MASK + 512] = np.tile(tri, (1, 4))
    cst[:, :, C_GMIX:C_GMIX + D] = np.asarray(g_mix, np.float32)[0][None, :]
    cst[:, :, C_GPLE:C_GPLE + D] = np.asarray(g_ple, np.float32)[0][None, :]
    cst[:, :, C_GFIN:C_GFIN + D] = np.asarray(g_final, np.float32)[None, :]
    cst[:, :, C_GHEAD:C_GHEAD + 16] = np.asarray(g_head, np.float32)[0].reshape(16, 128).T
    cw = np.asarray(conv_w, np.float32)[0]
    cst[:, :, C_CW:C_CW + 24] = cw.reshape(3, 8, 128).transpose(2, 1, 0).reshape(128, 24)
    cst[:, :, C_CB:C_CB + 8] = np.asarray(conv_b, np.float32)[0].reshape(8, 128).T
    cst[:, :, C_BG:C_BG + 8] = np.asarray(b_gates, np.float32)[0][None, :]
    for c in range(NCORES):
        s = c % 4
        btw = np.zeros((4, 4), np.float32)
        sel = np.zeros((4,), np.float32)
        for rr in range(4):
            if rr < s:
                sel[rr] = 1.0
            for i in range(4):
                if rr < i < s:
                    btw[i, rr] = 1.0
        cst[c, :, C_BTW:C_BTW + 16] = btw.reshape(16)[None, :]
        cst[c, :, C_SEL:C_SEL + 4] = sel[None, :]

    in_maps = []
    for c in range(NCORES):
        b, s = c // 4, c % 4
        xs = np.ascontiguousarray(x[b, s * NT:(s + 1) * NT])
        if s == 0:
            xp = np.zeros((L, D), np.float32)
        else:
            xp = np.ascontiguousarray(x[b, s * NT - L:s * NT])
        ps = np.ascontiguousarray(p[0, b, s * NT:(s + 1) * NT])
        in_maps.append({"x_seg": xs, "x_prev": xp, "p_seg": ps, "wq32": wq32, "wkv32": wkv32, "wif32": wif32,
                        "cst": np.ascontiguousarray(cst[c])})
    if "nc" not in _NC_CACHE:
        _NC_CACHE["nc"] = build_nc()
    nc = _NC_CACHE["nc"]
    res = run_bass_kernel_spmd(nc, in_maps, core_ids=list(range(NCORES)))
    _NC_CACHE["res"] = res
    out = np.empty((2, SEQ, D), np.float32)
    for c in range(NCORES):
        b, s = c // 4, c % 4
        out[b, s * NT:(s + 1) * NT] = res.results[c]["out"]
    return out
```
